# Optimizing a Trainium2 kernel written in Bass

```python
import jax, jax.numpy as jnp
from jax import lax
import numpy as np

D_MODEL = 1024
BATCH = 8
SEQ = 4096
DEPTH = 2

GMLP_WIDTH = D_MODEL
GMLP_GROUPS = 8
GMLP_GROUP_DIM = GMLP_WIDTH // GMLP_GROUPS
CHUNK = 128
LRU_WIDTH = D_MODEL
LRU_HEADS = 8
LRU_HEAD_DIM = LRU_WIDTH // LRU_HEADS
CONV_WIDTH = 4
CONV_LEFT = 1
LRU_C = 8.0
N_DIRS = 2
D_FF = -(-8 * D_MODEL // (3 * 256)) * 256
N_IN = 2 * GMLP_WIDTH + 2 * LRU_WIDTH + 2 * D_MODEL
EPS = 1e-6

kernel_name = "hybrid_gmlp_rglru_encoder"


def _rmsnorm(x, g):
    x32 = x.astype(jnp.float32)
    y = x32 * lax.rsqrt(jnp.mean(x32 * x32, axis=-1, keepdims=True) + EPS)
    return (y * g.astype(jnp.float32)).astype(x.dtype)


def _layernorm(x, g, b):
    x32 = x.astype(jnp.float32)
    mu = jnp.mean(x32, axis=-1, keepdims=True)
    xc = x32 - mu
    y = xc * lax.rsqrt(jnp.mean(xc * xc, axis=-1, keepdims=True) + EPS)
    return (y * g.astype(jnp.float32) + b.astype(jnp.float32)).astype(x.dtype)


def _blockdiag(x, w, b):
    B, S, _ = x.shape
    xh = x.reshape(B, S, LRU_HEADS, LRU_HEAD_DIM)
    y = jnp.einsum('bshc,hcd->bshd', xh, w.astype(x.dtype))
    return y.reshape(B, S, LRU_WIDTH) + b.astype(x.dtype)


def _lin_combine(left, right):
    a1, b1 = left
    a2, b2 = right
    return a1 * a2, a2 * b1 + b2


def _rglru_scan(x32, w_r, b_r, w_i, b_i, lam, reverse):
    r = jax.nn.sigmoid(_blockdiag(x32, w_r, b_r))
    i = jax.nn.sigmoid(_blockdiag(x32, w_i, b_i))
    log_a = -LRU_C * r * jax.nn.softplus(-lam.astype(jnp.float32))
    a = jnp.exp(log_a)
    mult = jnp.sqrt(jnp.maximum(-jnp.expm1(2.0 * log_a), 0.0))
    bt = mult * (i * x32)
    _, h = lax.associative_scan(_lin_combine, (a, bt), axis=1, reverse=reverse)
    return h


def _gmlp_branch(zu, zv, ln_g, ln_b, w_s, b_s):
    B, S, _ = zu.shape
    u = jax.nn.gelu(zu)
    v = _layernorm(jax.nn.gelu(zv), ln_g, ln_b)
    vc = v.reshape(B, S // CHUNK, CHUNK, GMLP_GROUPS, GMLP_GROUP_DIM)
    mixed = jnp.einsum('gpq,bnqgc->bnpgc', w_s.astype(v.dtype), vc)
    mixed = mixed + b_s.T.astype(v.dtype)[None, None, :, :, None]
    return u * mixed.reshape(B, S, GMLP_WIDTH)


def _rglru_branch(zx, zg, conv_w, conv_b, w_r, b_r, w_i, b_i, lam):
    S = zx.shape[1]
    xp = jnp.pad(zx, ((0, 0), (CONV_LEFT, CONV_WIDTH - 1 - CONV_LEFT), (0, 0)))
    xc = conv_b.astype(zx.dtype) + sum(xp[:, k:k + S] * conv_w[k].astype(zx.dtype) for k in range(CONV_WIDTH))
    x32 = xc.astype(jnp.float32)
    h = (_rglru_scan(x32, w_r[0], b_r[0], w_i[0], b_i[0], lam[0], False)
         + _rglru_scan(x32, w_r[1], b_r[1], w_i[1], b_i[1], lam[1], True))
    return h.astype(zx.dtype) * jax.nn.gelu(zg)


def setup_inputs(seed: int = 0) -> dict:
    key = jax.random.key(seed)
    ks = jax.random.split(key, 24)
    f32 = jnp.float32
    nrm = lambda k, shape, s: jax.random.normal(k, shape, f32) * s
    L = DEPTH
    x = jax.random.normal(ks[0], (BATCH, SEQ, D_MODEL), f32)
    norm1_g = 1.0 + nrm(ks[1], (L, D_MODEL), 0.05)
    w_in = nrm(ks[2], (L, D_MODEL, N_IN), D_MODEL ** -0.5)
    gmlp_ln_g = 1.0 + nrm(ks[3], (L, GMLP_WIDTH), 0.05)
    gmlp_ln_b = nrm(ks[4], (L, GMLP_WIDTH), 0.05)
    gmlp_w_s = nrm(ks[5], (L, GMLP_GROUPS, CHUNK, CHUNK), CHUNK ** -0.5)
    gmlp_b_s = 1.0 + nrm(ks[6], (L, GMLP_GROUPS, CHUNK), 0.1)
    conv_w = nrm(ks[7], (L, CONV_WIDTH, LRU_WIDTH), CONV_WIDTH ** -0.5)
    conv_b = nrm(ks[8], (L, LRU_WIDTH), 0.02)
    lru_w_r = nrm(ks[9], (L, N_DIRS, LRU_HEADS, LRU_HEAD_DIM, LRU_HEAD_DIM), LRU_HEAD_DIM ** -0.5)
    lru_b_r = nrm(ks[10], (L, N_DIRS, LRU_WIDTH), 0.02)
    lru_w_i = nrm(ks[11], (L, N_DIRS, LRU_HEADS, LRU_HEAD_DIM, LRU_HEAD_DIM), LRU_HEAD_DIM ** -0.5)
    lru_b_i = nrm(ks[12], (L, N_DIRS, LRU_WIDTH), 0.02)
    a_c = jax.random.uniform(ks[13], (L, N_DIRS, LRU_WIDTH), f32, 0.9, 0.999)
    a0 = a_c ** (1.0 / LRU_C)
    lru_lambda = jnp.log(a0) - jnp.log1p(-a0)
    w_out = nrm(ks[14], (L, D_MODEL, D_MODEL), D_MODEL ** -0.5)
    norm2_g = 1.0 + nrm(ks[15], (L, D_MODEL), 0.05)
    w_ffn_in = nrm(ks[16], (L, D_MODEL, 2 * D_FF), D_MODEL ** -0.5)
    w_ffn_out = nrm(ks[17], (L, D_FF, D_MODEL), D_FF ** -0.5)
    final_g = 1.0 + nrm(ks[18], (D_MODEL,), 0.05)
    return {"x": x, "norm1_g": norm1_g, "w_in": w_in, "gmlp_ln_g": gmlp_ln_g,
            "gmlp_ln_b": gmlp_ln_b, "gmlp_w_s": gmlp_w_s, "gmlp_b_s": gmlp_b_s,
            "conv_w": conv_w, "conv_b": conv_b, "lru_w_r": lru_w_r, "lru_b_r": lru_b_r,
            "lru_w_i": lru_w_i, "lru_b_i": lru_b_i, "lru_lambda": lru_lambda,
            "w_out": w_out, "norm2_g": norm2_g, "w_ffn_in": w_ffn_in,
            "w_ffn_out": w_ffn_out, "final_g": final_g}


def reference(x, norm1_g, w_in, gmlp_ln_g, gmlp_ln_b, gmlp_w_s, gmlp_b_s, conv_w, conv_b,
              lru_w_r, lru_b_r, lru_w_i, lru_b_i, lru_lambda, w_out, norm2_g, w_ffn_in,
              w_ffn_out, final_g):
    c0 = GMLP_WIDTH
    c1 = 2 * GMLP_WIDTH
    c2 = c1 + LRU_WIDTH
    c3 = c2 + LRU_WIDTH
    c4 = c3 + D_MODEL
    for l in range(DEPTH):
        h = _rmsnorm(x, norm1_g[l])
        z = h @ w_in[l].astype(h.dtype)
        y_a = _gmlp_branch(z[..., :c0], z[..., c0:c1], gmlp_ln_g[l], gmlp_ln_b[l],
                           gmlp_w_s[l], gmlp_b_s[l])
        y_b = _rglru_branch(z[..., c1:c2], z[..., c2:c3], conv_w[l], conv_b[l],
                            lru_w_r[l], lru_b_r[l], lru_w_i[l], lru_b_i[l], lru_lambda[l])
        merged = jax.nn.sigmoid(z[..., c3:c4]) * y_a + jax.nn.sigmoid(z[..., c4:]) * y_b
        x = x + merged @ w_out[l].astype(merged.dtype)
        h = _rmsnorm(x, norm2_g[l])
        gu = h @ w_ffn_in[l].astype(h.dtype)
        ff = jax.nn.silu(gu[..., :D_FF]) * gu[..., D_FF:]
        x = x + ff @ w_ffn_out[l].astype(ff.dtype)
    return _rmsnorm(x, final_g)
```

```python
import contextlib
import numpy as np
import concourse.bass as bass
import concourse.mybir as mybir
from concourse.bass_utils import run_bass_kernel_spmd

F32 = mybir.dt.float32
BF16 = mybir.dt.bfloat16
AF = mybir.ActivationFunctionType
ALU = mybir.AluOpType

D = 1024
S = 4096
DEPTH = 2
KC = 8
DFF = 2816
FC = 22
NIN = 6144
TT = 512
NT = S // TT
NS = TT // 128
EPS = 1e-6
N_CORES = 8


def _snap(fn):
    out = []
    for c in (fn.__closure__ or ()):
        try:
            out.append(c.cell_contents)
        except ValueError:
            out.append(None)
    return out


def _check_snap(fn, snap):
    for name, c, old in zip(fn.__code__.co_freevars, fn.__closure__ or (), snap):
        new = c.cell_contents
        if new is not old and not (isinstance(new, (int, float, str, tuple)) and new == old):
            raise RuntimeError(f"late-binding bug: closure var {name!r} changed between emit and replay "
                               f"(line {fn.__code__.co_firstlineno})")


class Buf:
    __slots__ = ("name", "w", "r")

    def __init__(self, name=""):
        self.name = name
        self.w = None
        self.r = []


class _Eng:
    def __init__(self, name):
        self.name = name
        self.ops = []
        self.seen = {}
        self.semkey = None
        self.count = 0
        self.slots = []
        self.slot_i = 0


class Sched:
    ENGS = ("pe", "act", "dve", "pool", "sp")

    def __init__(self, nc, stack):
        self.nc = nc
        self.stack = stack
        self.sems = {}
        self.eng = {e: _Eng(e) for e in self.ENGS}
        self.nsem = 0
        for e in self.ENGS:
            self._new_eng_sem(self.eng[e])
        for e, n in (("sp", 12), ("pool", 8), ("act", 4)):
            for i in range(n):
                k = self._new_sem(f"d_{e}{i}")
                self.eng[e].slots.append([k, 0])

    def _new_sem(self, name):
        h = self.stack.enter_context(self.nc.semaphore(f"{name}_{self.nsem}"))
        self.nsem += 1
        self.sems[name + str(self.nsem)] = h
        return name + str(self.nsem)

    def _new_eng_sem(self, E):
        E.semkey = self._new_sem("e_" + E.name)
        E.count = 0

    def emit(self, eng, fn, reads=(), writes=(), dma=False, inc=True):
        E = self.eng[eng]
        waits = {}

        def need(ev):
            if ev is None:
                return
            k, v = ev
            if waits.get(k, 0) < v:
                waits[k] = v

        for b in reads:
            need(b.w)
        for b in writes:
            need(b.w)
            for ev in b.r:
                need(ev)
        final = []
        for k, v in waits.items():
            if E.seen.get(k, 0) >= v:
                continue
            E.seen[k] = v
            final.append((k, v))
        if dma:
            slot = E.slots[E.slot_i]
            E.slot_i = (E.slot_i + 1) % len(E.slots)
            k = slot[0]
            if slot[1] > 0 and E.seen.get(k, 0) < slot[1]:
                final.append((k, slot[1]))
                E.seen[k] = slot[1]
            if slot[1] + 16 > 65000:
                slot[0] = self._new_sem("d_" + E.name)
                slot[1] = 0
                k = slot[0]
            slot[1] += 16
            ev = (k, slot[1])
            incv = 16
        elif inc:
            if E.count >= 60000:
                self._new_eng_sem(E)
            E.count += 1
            ev = (E.semkey, E.count)
            incv = 1
        else:
            ev = None
            incv = 0
        E.ops.append((fn, final, ev, incv, _snap(fn)))
        if ev is not None:
            for b in writes:
                b.w = ev
                b.r = []
            for b in reads:
                b.r.append(ev)
        return ev

    def mm_group(self, bank, mms, extra_reads=()):
        n = len(mms)
        allreads = list(extra_reads)
        for i, (fn, rb) in enumerate(mms):
            for b in rb:
                if b not in allreads:
                    allreads.append(b)
            last = i == n - 1
            self.emit("pe", fn, reads=(allreads if last else rb),
                      writes=[bank] if (i == 0 or last) else (), inc=last)

    def barrier(self):
        final = []
        for e in self.ENGS:
            for k, c in self.eng[e].slots:
                if c > 0:
                    final.append((k, c))
            if self.eng[e].count > 0:
                final.append((self.eng[e].semkey, self.eng[e].count))
        for e in self.ENGS:
            E = self.eng[e]
            w = []
            for k, v in final:
                if k == E.semkey and e != "sp" and False:
                    continue
                if E.seen.get(k, 0) >= v:
                    continue
                E.seen[k] = v
                w.append((k, v))
            if w:
                E.ops.append((None, w, None, 0, None))

    def finish(self):
        E = self.eng["sp"]
        final = []
        for e in self.ENGS:
            for k, c in self.eng[e].slots:
                if c > 0:
                    final.append((k, c))
            if self.eng[e].count > 0 and e != "sp":
                final.append((self.eng[e].semkey, self.eng[e].count))
        E.ops.append((None, final, None, 0, None))

    def replay(self, block):
        nc = self.nc
        handles = {"pe": (block.tensor, nc.tensor), "act": (block.scalar, nc.scalar),
                   "dve": (block.vector, nc.vector), "pool": (block.gpsimd, nc.gpsimd),
                   "sp": (block.sync, nc.sync)}
        sems = self.sems
        for e in self.ENGS:
            ops = self.eng[e].ops
            deco, _ = handles[e]

            def body(eng, ops=ops):
                for fn, waits, ev, incv, snap in ops:
                    for k, v in waits:
                        eng.wait_ge(sems[k], v)
                    if fn is None:
                        continue
                    _check_snap(fn, snap)
                    ins = fn(eng)
                    if ev is not None:
                        ins.then_inc(sems[ev[0]], incv)
            deco(body)


class Pool_:
    def __init__(self, tiles, nsub=0):
        self.tiles = tiles
        self.nsub = nsub
        self.bufs = [([Buf() for _ in range(nsub)] if nsub else Buf()) for _ in tiles]
        self.i = 0

    def next(self):
        t, b = self.tiles[self.i], self.bufs[self.i]
        self.i = (self.i + 1) % len(self.tiles)
        return t, b


def build_program(layers=(0, 1), final_norm=True, debug=None, phases=(1, 2, 3, 4)):
    debug = debug or set()
    nc = bass.Bass("TRN2", target_bir_lowering=False)
    stack = contextlib.ExitStack()
    with stack:
        _build(nc, stack, layers, final_norm, debug, phases)
    return nc


def _dram(nc, name, shape, dt, kind):
    return nc.dram_tensor(name, list(shape), dt, kind=kind).ap()


def _build(nc, stack, layers, final_norm, debug, phases):
    L = DEPTH
    x_in = _dram(nc, "x", [S, D], F32, "ExternalInput")
    out = _dram(nc, "out", [S, D], F32, "ExternalOutput")
    W = {}
    for name, shape in (
        ("norm1_g", [L, D]), ("w_in", [L, D, NIN]), ("gmlp_ln_g", [L, D]), ("gmlp_ln_b", [L, D]),
        ("gmlp_w_s", [L, 8, 128, 128]), ("gmlp_b_s", [L, 8, 128]), ("conv_w", [L, 4, D]),
        ("conv_b", [L, D]), ("lru_w_r", [L, 2, 8, 128, 128]), ("lru_b_r", [L, 2, D]),
        ("lru_w_i", [L, 2, 8, 128, 128]), ("lru_b_i", [L, 2, D]), ("lru_lambda", [L, 2, D]),
        ("w_out", [L, D, D]), ("norm2_g", [L, D]), ("w_ffn_in", [L, D, 2 * DFF]),
        ("w_ffn_out", [L, DFF, D]), ("final_g", [D]),
    ):
        W[name] = _dram(nc, name, shape, F32, "ExternalInput")

    def scratch(name, shape, dt):
        kind = "ExternalOutput" if name in debug else "Internal"
        return _dram(nc, name, shape, dt, kind)

    HT = scratch("HT", [128, KC, S], BF16)
    ZX = scratch("ZX", [KC, 128, S], F32)
    HS = scratch("HS", [KC, 128, S], F32)
    XM = scratch("XM", [S, D], F32)
    XL = scratch("XL", [S, D], F32)

    sc = Sched(nc, stack)
    sb = lambda name, shape, dt: stack.enter_context(nc.sbuf_tensor(name, list(shape), dt))
    ps = lambda name, shape, dt: stack.enter_context(nc.psum_tensor(name, list(shape), dt))

    pf = Pool_([ps(f"pf{i}", [128, 512], F32) for i in range(6)])
    pb = Pool_([ps(f"pb{i}", [128, 1024], BF16) for i in range(2)])

    ident_f = sb("ident_f", [128, 128], F32)
    ident_b = sb("ident_b", [128, 128], BF16)
    eps_c = sb("eps_c", [128, 1], F32)
    one_c = sb("one_c", [128, 1], F32)
    b_ident_f, b_ident_b, b_eps, b_one = Buf(), Buf(), Buf(), Buf()

    sc.emit("pool", lambda e: e.memset(ident_f[:], 0.0), writes=[b_ident_f])
    sc.emit("pool", lambda e: e.affine_select(out=ident_f[:], in_=ident_f[:], compare_op=ALU.not_equal, fill=1.0,
                                              base=0, pattern=[[-1, 128]], channel_multiplier=1),
            reads=[b_ident_f], writes=[b_ident_f])
    sc.emit("pool", lambda e: e.tensor_copy(out=ident_b[:], in_=ident_f[:]), reads=[b_ident_f], writes=[b_ident_b])
    sc.emit("pool", lambda e: e.memset(eps_c[:], EPS), writes=[b_eps])
    sc.emit("pool", lambda e: e.memset(one_c[:], 1.0), writes=[b_one])

    bHT = [Buf() for _ in range(NT)]
    bZX = [Buf() for _ in range(NT)]
    bHS = [Buf() for _ in range(KC)]
    bXM = [Buf() for _ in range(NT)]
    bXL = [Buf() for _ in range(NT)]

    def wload(dst, src, buf):
        sc.emit("pool", lambda e: e.dma_start(out=dst, in_=src), writes=[buf], dma=True)

    def load_bcast(dst, src_row, buf):
        sc.emit("sp", lambda e: e.dma_start(out=dst, in_=src_row.partition_broadcast(128)), writes=[buf], dma=True)

    def rmsnorm_tile(xt, bxt, g_bc, bg, hb, bhb, ss, bss, rstd, brstd):
        for s in range(NS):
            sc.emit("act", lambda e, s=s: e.activation(out=hb[:, s, :], in_=xt[:, s, :], func=AF.Square,
                                                        accum_out=ss[:, s:s + 1]),
                    reads=[bxt[s]], writes=[bhb[s], bss[s]])
        sc.emit("act", lambda e: e.activation(out=rstd[:], in_=ss[:], func=AF.Sqrt, scale=1.0 / D, bias=eps_c[:]),
                reads=bss + [b_eps], writes=[brstd])
        sc.emit("dve", lambda e: e.reciprocal(out=rstd[:], in_=rstd[:]), reads=[brstd], writes=[brstd])
        for s in range(NS):
            sc.emit("dve", lambda e, s=s: e.scalar_tensor_tensor(out=hb[:, s, :], in0=xt[:, s, :], scalar=rstd[:, s:s + 1],
                                                                  in1=g_bc[:], op0=ALU.mult, op1=ALU.mult),
                    reads=[bxt[s], brstd, bg], writes=[bhb[s]])

    def transpose_tile(hb, bhb, hT, bhT):
        for k2 in range(KC // 2):
            pt, bpt = pb.next()
            mms = []
            for kk in range(2):
                k = 2 * k2 + kk
                for s in range(NS):
                    mms.append((lambda e, k=k, kk=kk, s=s, pt=pt: e.transpose(
                        out=pt[:, kk * 512 + s * 128: kk * 512 + (s + 1) * 128],
                        in_=hb[:, s, k * 128:(k + 1) * 128], identity=ident_b[:]), [bhb[s], b_ident_b]))
            sc.mm_group(bpt, mms)
            sc.emit("act", lambda e, k2=k2, pt=pt: e.copy(out=hT[:, 2 * k2:2 * k2 + 2, :],
                                                          in_=pt[:].rearrange("p (a t) -> p a t", a=2)),
                    reads=[bpt], writes=[bhT[k2]])

    uniq = []

    def emit_layer(li, l):
        x_src, bx_src = (x_in, None) if li == 0 else (XL, bXL)
        last = li == len(layers) - 1
        if 1 in phases:
          with contextlib.ExitStack() as pstack:
            psb = lambda name, shape, dt: pstack.enter_context(nc.sbuf_tensor(f"{name}_L{l}P{len(uniq)}", list(shape), dt))
            uniq.append(0)
            wzx = psb("wzx", [128, KC, D], BF16)
            bwzx = Buf()
            wload(wzx[:], W["w_in"][l, :, 2048:3072].rearrange("(kc p) n -> p kc n", p=128), bwzx)
            g1 = psb("g1", [128, D], F32)
            bg1 = Buf()
            load_bcast(g1[:], W["norm1_g"][l:l + 1, :], bg1)
            xts = Pool_([psb(f"xt{i}", [128, NS, D], F32) for i in range(2)], nsub=NS)
            hbs = Pool_([psb(f"hb{i}", [128, NS, D], BF16) for i in range(2)], nsub=NS)
            hTs = Pool_([psb(f"hT{i}", [128, KC, TT], BF16) for i in range(2)], nsub=KC // 2)
            zxs = Pool_([psb(f"zxo{i}", [128, KC, TT], F32) for i in range(2)])
            ss = psb("ss", [128, NS], F32)
            rstd = psb("rstd", [128, NS], F32)
            bss = [Buf() for _ in range(NS)]
            brstd = Buf()
            for j in range(NT):
                xt, bxt = xts.next()
                rd = [bx_src[j]] if bx_src is not None else []
                sc.emit("sp", lambda e, j=j, xt=xt: e.dma_start(
                    out=xt[:], in_=x_src[j * TT:(j + 1) * TT, :].rearrange("(s p) d -> p s d", p=128)),
                    reads=rd, writes=bxt, dma=True)
                hb, bhb = hbs.next()
                rmsnorm_tile(xt, bxt, g1, bg1, hb, bhb, ss, bss, rstd, brstd)
                hT, bhT = hTs.next()
                transpose_tile(hb, bhb, hT, bhT)
                sc.emit("sp", lambda e, j=j, hT=hT: e.dma_start(out=HT[:, :, j * TT:(j + 1) * TT], in_=hT[:]),
                        reads=bhT, writes=[bHT[j]], dma=True)
                zxo, bzxo = zxs.next()
                for c in range(KC):
                    pz, bpz = pf.next()
                    mms = []
                    for k in range(KC):
                        mms.append((lambda e, c=c, k=k, pz=pz, hT=hT: e.matmul(
                            pz[:], lhsT=wzx[:, k, c * 128:(c + 1) * 128], rhs=hT[:, k, :],
                            start=(k == 0), stop=(k == KC - 1)), [bhT[k // 2], bwzx]))
                    sc.mm_group(bpz, mms)
                    if c % 2:
                        sc.emit("act", lambda e, c=c, pz=pz, zxo=zxo: e.copy(out=zxo[:, c, :], in_=pz[:]),
                                reads=[bpz], writes=[bzxo])
                    else:
                        sc.emit("dve", lambda e, c=c, pz=pz, zxo=zxo: e.tensor_copy(out=zxo[:, c, :], in_=pz[:]),
                                reads=[bpz], writes=[bzxo])
                sc.emit("sp", lambda e, j=j, zxo=zxo: e.dma_start(
                    out=ZX[:, :, j * TT:(j + 1) * TT].rearrange("c p t -> p c t"), in_=zxo[:]),
                    reads=[bzxo], writes=[bZX[j]], dma=True)
          sc.barrier()

        if 2 in phases:
          with contextlib.ExitStack() as pstack:
            psb = lambda name, shape, dt: pstack.enter_context(nc.sbuf_tensor(f"{name}_L{l}P{len(uniq)}", list(shape), dt))
            uniq.append(0)
            wg = psb("wg", [128, 2, 2, KC, 128], BF16)
            bwg = Buf()
            for t, nm in ((0, "lru_w_r"), (1, "lru_w_i")):
                for d in range(2):
                    wload(wg[:, t, d, :, :], W[nm][l, d].rearrange("h c o -> c h o"), bwg)
            vt = psb("vt", [128, 128], F32)
            cols = psb("cols", [128, 128], F32)
            bvt, bcols = Buf(), Buf()
            sc.emit("pool", lambda e: e.memset(vt[:], 0.0), writes=[bvt])
            sc.emit("sp", lambda e: e.dma_start(out=vt[0:32, :], in_=W["conv_w"][l].rearrange("k (c p) -> (k c) p", p=128)),
                    writes=[bvt], dma=True)
            sc.emit("sp", lambda e: e.dma_start(out=vt[32:40, :], in_=W["conv_b"][l].rearrange("(c p) -> c p", p=128)),
                    writes=[bvt], dma=True)
            for base, nm in ((40, "lru_b_r"), (56, "lru_b_i"), (72, "lru_lambda")):
                sc.emit("sp", lambda e, base=base, nm=nm: e.dma_start(
                    out=vt[base:base + 16, :], in_=W[nm][l].rearrange("d (c p) -> (d c) p", p=128)),
                    writes=[bvt], dma=True)
            pz, bpz = pf.next()
            sc.mm_group(bpz, [(lambda e, pz=pz: e.transpose(out=pz[:, 0:128], in_=vt[:], identity=ident_f[:]),
                               [bvt, b_ident_f])])
            sc.emit("dve", lambda e, pz=pz: e.tensor_copy(out=cols[:], in_=pz[:, 0:128]), reads=[bpz], writes=[bcols])
            tmp = psb("tmpc", [128, 4, 16], F32)
            btmp = Buf()
            cneg = psb("cneg", [128, 16], F32)
            bcneg = Buf()
            ee, zz, z2, pp = tmp[:, 0, :], tmp[:, 1, :], tmp[:, 2, :], tmp[:, 3, :]
            sc.emit("act", lambda e: e.activation(out=ee, in_=cols[:, 72:88], func=AF.Exp, scale=-1.0),
                    reads=[bcols], writes=[btmp])
            dv = lambda fn: sc.emit("dve", fn, reads=[btmp], writes=[btmp])
            dv(lambda e: e.tensor_scalar_add(out=zz, in0=ee, scalar1=2.0))
            dv(lambda e: e.reciprocal(out=zz, in_=zz))
            dv(lambda e: e.tensor_mul(out=zz, in0=zz, in1=ee))
            dv(lambda e: e.tensor_mul(out=z2, in0=zz, in1=zz))
            dv(lambda e: e.tensor_scalar(out=pp, in0=z2, scalar1=1.0 / 11.0, scalar2=1.0 / 9.0, op0=ALU.mult, op1=ALU.add))
            for cst in (1.0 / 7.0, 1.0 / 5.0, 1.0 / 3.0, 1.0):
                dv(lambda e: e.tensor_mul(out=pp, in0=pp, in1=z2))
                dv(lambda e, cst=cst: e.tensor_scalar_add(out=pp, in0=pp, scalar1=cst))
            dv(lambda e: e.tensor_mul(out=pp, in0=pp, in1=zz))
            sc.emit("dve", lambda e: e.tensor_scalar_mul(out=cneg[:], in0=pp, scalar1=-16.0), reads=[btmp], writes=[bcneg])

            zxps = Pool_([psb(f"zxp{i}", [128, S + 3], F32) for i in range(2)])
            for zt, bz in zip(zxps.tiles, zxps.bufs):
                sc.emit("pool", lambda e, zt=zt: e.memset(zt[:, 0:1], 0.0), writes=[bz])
                sc.emit("pool", lambda e, zt=zt: e.memset(zt[:, S + 1:S + 3], 0.0), writes=[bz])
            xc = psb("xc", [128, S], F32)
            xcb = psb("xcb", [128, S], BF16)
            bxc = [Buf() for _ in range(S // 512)]
            bxcb = [Buf() for _ in range(S // 512)]
            dgs = Pool_([psb(f"dg{i}", [128, 4, 128], F32) for i in range(2)])
            ra = [psb(f"ra{d}", [128, S], F32) for d in range(2)]
            ib = [psb(f"ib{d}", [128, S], F32) for d in range(2)]
            mh = [psb(f"mh{d}", [128, S], F32) for d in range(2)]
            bra, bib, bmh = [Buf(), Buf()], [Buf(), Buf()], [Buf(), Buf()]
            hso = psb("hso", [128, S], F32)
            bhso = Buf()
            NB = S // 512
            for c in range(KC):
                zxp, bzxp = zxps.next()
                sc.emit("sp", lambda e, c=c, zxp=zxp: e.dma_start(out=zxp[:, 1:S + 1], in_=ZX[c]),
                        reads=bZX, writes=[bzxp], dma=True)
                dg, bdg = dgs.next()
                for k in range(4):
                    sc.emit("dve", lambda e, c=c, k=k, dg=dg: e.tensor_scalar_mul(
                        out=dg[:, k, :], in0=ident_f[:], scalar1=cols[:, k * 8 + c:k * 8 + c + 1]),
                        reads=[b_ident_f, bcols], writes=[bdg])
                for tb in range(NB):
                    pz, bpz = pf.next()
                    sc.mm_group(bpz, [(lambda e, k=k, tb=tb, pz=pz, dg=dg, zxp=zxp: e.matmul(
                        pz[:], lhsT=dg[:, k, :], rhs=zxp[:, tb * 512 + k: tb * 512 + k + 512],
                        start=(k == 0), stop=(k == 3)), [bdg, bzxp]) for k in range(4)])
                    sc.emit("act", lambda e, c=c, tb=tb, pz=pz: e.activation(
                        out=xc[:, tb * 512:(tb + 1) * 512], in_=pz[:], func=AF.Identity, bias=cols[:, 32 + c:33 + c]),
                        reads=[bpz, bcols], writes=[bxc[tb]])
                    sc.emit("pool", lambda e, tb=tb: e.tensor_copy(out=xcb[:, tb * 512:(tb + 1) * 512],
                                                                   in_=xc[:, tb * 512:(tb + 1) * 512]),
                            reads=[bxc[tb]], writes=[bxcb[tb]])
                for d in range(2):
                    for t in range(2):
                        dst, bdst = (ra[d], bra[d]) if t == 0 else (ib[d], bib[d])
                        bcol = (40 if t == 0 else 56) + d * 8 + c
                        for tb in range(NB):
                            pz, bpz = pf.next()
                            sc.mm_group(bpz, [(lambda e, t=t, d=d, c=c, tb=tb, pz=pz: e.matmul(
                                pz[:], lhsT=wg[:, t, d, c, :], rhs=xcb[:, tb * 512:(tb + 1) * 512],
                                start=True, stop=True), [bwg, bxcb[tb]])])
                            sc.emit("act", lambda e, dst=dst, tb=tb, pz=pz, bcol=bcol: e.activation(
                                out=dst[:, tb * 512:(tb + 1) * 512], in_=pz[:], func=AF.Sigmoid,
                                bias=cols[:, bcol:bcol + 1]), reads=[bpz, bcols], writes=[bdst])
                for d in range(2):
                    sc.emit("act", lambda e, d=d, c=c: e.activation(out=ra[d][:], in_=ra[d][:], func=AF.Exp,
                                                                    scale=cneg[:, d * 8 + c:d * 8 + c + 1]),
                            reads=[bra[d], bcneg], writes=[bra[d]])
                for d in range(2):
                    sc.emit("pool", lambda e, d=d: e.tensor_mul(out=mh[d][:], in0=ra[d][:], in1=ra[d][:]),
                            reads=[bra[d]], writes=[bmh[d]])
                for d in range(2):
                    sc.emit("act", lambda e, d=d: e.activation(out=mh[d][:], in_=mh[d][:], func=AF.Sqrt,
                                                               scale=-1.0, bias=one_c[:]),
                            reads=[bmh[d], b_one], writes=[bmh[d]])
                for d in range(2):
                    sc.emit("dve", lambda e, d=d: e.tensor_mul(out=ib[d][:], in0=ib[d][:], in1=mh[d][:]),
                            reads=[bib[d], bmh[d]], writes=[bib[d]])
                    sc.emit("dve", lambda e, d=d: e.tensor_mul(out=ib[d][:], in0=ib[d][:], in1=xc[:]),
                            reads=[bib[d]] + bxc, writes=[bib[d]])
                    if d == 0:
                        sc.emit("dve", lambda e: e.tensor_tensor_scan(out=mh[0][:], data0=ra[0][:], data1=ib[0][:],
                                                                       initial=0.0, op0=ALU.mult, op1=ALU.add),
                                reads=[bra[0], bib[0]], writes=[bmh[0]])
                    else:
                        sc.emit("dve", lambda e: e.tensor_tensor_scan(out=mh[1][:, ::-1], data0=ra[1][:, ::-1],
                                                                       data1=ib[1][:, ::-1], initial=0.0,
                                                                       op0=ALU.mult, op1=ALU.add),
                                reads=[bra[1], bib[1]], writes=[bmh[1]])
                sc.emit("dve", lambda e: e.tensor_add(out=hso[:], in0=mh[0][:], in1=mh[1][:]),
                        reads=[bmh[0], bmh[1]], writes=[bhso])
                sc.emit("sp", lambda e, c=c: e.dma_start(out=HS[c], in_=hso[:]), reads=[bhso], writes=[bHS[c]], dma=True)
          sc.barrier()

        if 3 in phases:
          with contextlib.ExitStack() as pstack:
            psb = lambda name, shape, dt: pstack.enter_context(nc.sbuf_tensor(f"{name}_L{l}P{len(uniq)}", list(shape), dt))
            uniq.append(0)
            w3 = psb("w3", [128, KC, 5 * D], BF16)
            bw3 = [Buf() for _ in range(5)]
            wsrc = W["w_in"][l].rearrange("(kc p) n -> p kc n", p=128)
            for blk, c0 in ((1, 1024), (0, 0), (3, 4096), (2, 3072), (4, 5120)):
                wload(w3[:, :, blk * D:(blk + 1) * D], wsrc[:, :, c0:c0 + D], bw3[blk])
            wo = psb("wo", [128, KC, D], BF16)
            bwo = Buf()
            wload(wo[:], W["w_out"][l].rearrange("(kc p) n -> p kc n", p=128), bwo)
            wsn = psb("wsn", [128, 8, 128], F32)
            wsT = psb("wsT", [128, 8, 128], BF16)
            bwsn, bwsT = Buf(), Buf()
            sc.emit("sp", lambda e: e.dma_start(out=wsn[:], in_=W["gmlp_w_s"][l].rearrange("g p q -> p g q")),
                    writes=[bwsn], dma=True)
            for g2 in range(2):
                pz, bpz = pf.next()
                sc.mm_group(bpz, [(lambda e, g=g2 * 4 + gg, gg=gg, pz=pz: e.transpose(
                    out=pz[:, gg * 128:(gg + 1) * 128], in_=wsn[:, g, :], identity=ident_f[:]), [bwsn, b_ident_f])
                    for gg in range(4)])
                sc.emit("dve", lambda e, g2=g2, pz=pz: e.tensor_copy(
                    out=wsT[:, g2 * 4:(g2 + 1) * 4, :], in_=pz[:].rearrange("p (g q) -> p g q", g=4)),
                    reads=[bpz], writes=[bwsT])
            bsb = psb("bsb", [128, 8, 128], F32)
            lng = psb("lng", [128, D], F32)
            lnb = psb("lnb", [128, D], F32)
            bbsb, blng, blnb = Buf(), Buf(), Buf()
            load_bcast(bsb[:].rearrange("p g q -> p (g q)"), W["gmlp_b_s"][l:l + 1].rearrange("o g q -> o (g q)"), bbsb)
            load_bcast(lng[:], W["gmlp_ln_g"][l:l + 1, :], blng)
            load_bcast(lnb[:], W["gmlp_ln_b"][l:l + 1, :], blnb)

            hTs = Pool_([psb(f"hT{i}", [128, KC, TT], BF16) for i in range(2)])
            hss = Pool_([psb(f"hs{i}", [128, TT], F32) for i in range(3)])
            xss = Pool_([psb(f"xs{i}", [128, D], F32) for i in range(2)])
            vgs = Pool_([psb(f"vg{i}", [128, D], F32) for i in range(2)])
            vn = psb("vn", [128, NS, D], BF16)
            bvn = [Buf() for _ in range(NS)]
            mT = psb("mT", [128, KC, TT], BF16)
            bmT = [Buf() for _ in range(KC)]
            tA = Pool_([psb(f"tA{i}", [128, TT], F32) for i in range(2)])
            tB = Pool_([psb(f"tB{i}", [128, TT], F32) for i in range(2)])
            tC = Pool_([psb(f"tC{i}", [128, TT], F32) for i in range(2)])
            tD = Pool_([psb(f"tD{i}", [128, TT], F32) for i in range(2)])
            st = psb("lnst", [128, NS, 2, 6], F32)
            mv = psb("lnmv", [128, NS, 2], F32)
            lrs = psb("lnrs", [128, NS], F32)
            lnm = psb("lnnm", [128, NS], F32)
            bst = [Buf() for _ in range(NS)]
            bmv = [Buf() for _ in range(NS)]
            blrs, blnm = Buf(), Buf()

            def zmm(pz, blk, c, hT, bhT):
                return [(lambda e, k=k: e.matmul(pz[:], lhsT=w3[:, k, blk * D + c * 128: blk * D + (c + 1) * 128],
                                                  rhs=hT[:, k, :], start=(k == 0), stop=(k == KC - 1)),
                         [bhT, bw3[blk]]) for k in range(KC)]

            for j in range(NT):
                hT, bhT = hTs.next()
                sc.emit("sp", lambda e, j=j, hT=hT: e.dma_start(out=hT[:], in_=HT[:, :, j * TT:(j + 1) * TT]),
                        reads=[bHT[j]], writes=[bhT], dma=True)
                vgl = []
                for s in range(NS):
                    vg, bvg = vgs.next()
                    for nb in range(2):
                        pz, bpz = pf.next()
                        sc.mm_group(bpz, [(lambda e, k=k, s=s, nb=nb, pz=pz, hT=hT: e.matmul(
                            pz[:], lhsT=hT[:, k, s * 128:(s + 1) * 128],
                            rhs=w3[:, k, D + nb * 512: D + (nb + 1) * 512], start=(k == 0), stop=(k == KC - 1)),
                            [bhT, bw3[1]]) for k in range(KC)])
                        sc.emit("act", lambda e, nb=nb, pz=pz, vg=vg: e.activation(
                            out=vg[:, nb * 512:(nb + 1) * 512], in_=pz[:], func=AF.Gelu_apprx_tanh),
                            reads=[bpz], writes=[bvg])
                    for nb in range(2):
                        sc.emit("dve", lambda e, s=s, nb=nb, vg=vg: e.bn_stats(out=st[:, s, nb, :], in_=vg[:, nb * 512:(nb + 1) * 512]),
                                reads=[bvg], writes=[bst[s]])
                    sc.emit("dve", lambda e, s=s: e.bn_aggr(out=mv[:, s, :], in_=st[:, s, :, :]), reads=[bst[s]], writes=[bmv[s]])
                    vgl.append((vg, bvg))
                    if s % 2 == 1:
                        s0 = s - 1
                        sc.emit("act", lambda e, s0=s0: e.activation(out=lrs[:, s0:s0 + 2], in_=mv[:, s0:s0 + 2, 1], func=AF.Sqrt,
                                                                      bias=eps_c[:]),
                                reads=[bmv[s0], bmv[s0 + 1], b_eps], writes=[blrs])
                        sc.emit("dve", lambda e, s0=s0: e.reciprocal(out=lrs[:, s0:s0 + 2], in_=lrs[:, s0:s0 + 2]),
                                reads=[blrs], writes=[blrs])
                        for s1 in (s0, s0 + 1):
                            vg1, bvg1 = vgl[s1]
                            sc.emit("dve", lambda e, s1=s1, vg1=vg1: e.tensor_scalar(
                                out=vg1[:], in0=vg1[:], scalar1=mv[:, s1, 0:1], scalar2=lrs[:, s1:s1 + 1],
                                op0=ALU.subtract, op1=ALU.mult), reads=[bvg1, bmv[s1], blrs], writes=[bvg1])
                            sc.emit("dve", lambda e, vg1=vg1: e.tensor_mul(out=vg1[:], in0=vg1[:], in1=lng[:]),
                                    reads=[bvg1, blng], writes=[bvg1])
                            sc.emit("dve", lambda e, s1=s1, vg1=vg1: e.tensor_add(out=vn[:, s1, :], in0=vg1[:], in1=lnb[:]),
                                    reads=[bvg1, blnb], writes=[bvn[s1]])
                for c in range(KC):
                    hs, bhs = hss.next()
                    sc.emit("sp", lambda e, c=c, j=j, hs=hs: e.dma_start(out=hs[:], in_=HS[c, :, j * TT:(j + 1) * TT]),
                            reads=bHS, writes=[bhs], dma=True)
                    a_, ba = tA.next()
                    b_, bb = tB.next()
                    c_, bc = tC.next()
                    d_, bd = tD.next()
                    pu, bpu = pf.next()
                    sc.mm_group(bpu, zmm(pu, 0, c, hT, bhT))
                    sc.emit("act", lambda e, pu=pu, a_=a_: e.activation(out=a_[:], in_=pu[:], func=AF.Gelu_apprx_tanh),
                            reads=[bpu], writes=[ba])
                    pm, bpm = pf.next()
                    sc.mm_group(bpm, [(lambda e, s=s, c=c, pm=pm: e.matmul(
                        pm[:, s * 128:(s + 1) * 128], lhsT=vn[:, s, c * 128:(c + 1) * 128], rhs=wsT[:, c, :],
                        start=True, stop=True), [bvn[s], bwsT]) for s in range(NS)])
                    pa, bpa = pf.next()
                    sc.mm_group(bpa, zmm(pa, 3, c, hT, bhT))
                    sc.emit("act", lambda e, pa=pa, b_=b_: e.activation(out=b_[:], in_=pa[:], func=AF.Sigmoid),
                            reads=[bpa], writes=[bb])
                    pg, bpg = pf.next()
                    sc.mm_group(bpg, zmm(pg, 2, c, hT, bhT))
                    sc.emit("act", lambda e, pg=pg, c_=c_: e.activation(out=c_[:], in_=pg[:], func=AF.Gelu_apprx_tanh),
                            reads=[bpg], writes=[bc])
                    pB, bpB = pf.next()
                    sc.mm_group(bpB, zmm(pB, 4, c, hT, bhT))
                    sc.emit("act", lambda e, pB=pB, d_=d_: e.activation(out=d_[:], in_=pB[:], func=AF.Sigmoid),
                            reads=[bpB], writes=[bd])
                    sc.emit("dve", lambda e, a_=a_, b_=b_: e.tensor_mul(out=a_[:], in0=a_[:], in1=b_[:]),
                            reads=[ba, bb], writes=[ba])
                    sc.emit("dve", lambda e, c=c, pm=pm, b_=b_: e.tensor_tensor(
                        out=b_[:].rearrange("p (s q) -> p s q", s=NS), in0=pm[:].rearrange("p (s q) -> p s q", s=NS),
                        in1=bsb[:, c:c + 1, :].to_broadcast([128, NS, 128]), op=ALU.add),
                        reads=[bpm, bbsb, bb], writes=[bb])
                    sc.emit("dve", lambda e, a_=a_, b_=b_: e.tensor_mul(out=a_[:], in0=a_[:], in1=b_[:]),
                            reads=[ba, bb], writes=[ba])
                    sc.emit("pool", lambda e, c_=c_, d_=d_: e.tensor_mul(out=c_[:], in0=c_[:], in1=d_[:]),
                            reads=[bc, bd], writes=[bc])
                    sc.emit("pool", lambda e, c_=c_, hs=hs: e.tensor_mul(out=c_[:], in0=c_[:], in1=hs[:]),
                            reads=[bc, bhs], writes=[bc])
                    sc.emit("dve", lambda e, c=c, a_=a_, c_=c_: e.tensor_add(out=mT[:, c, :], in0=a_[:], in1=c_[:]),
                            reads=[ba, bc], writes=[bmT[c]])
                for s in range(NS):
                    xs, bxs = xss.next()
                    rd = [bx_src[j]] if bx_src is not None else []
                    sc.emit("sp", lambda e, j=j, s=s, xs=xs: e.dma_start(
                        out=xs[:], in_=x_src[j * TT + s * 128: j * TT + (s + 1) * 128, :]),
                        reads=rd, writes=[bxs], dma=True)
                    for nb in range(2):
                        po, bpo = pf.next()
                        sc.mm_group(bpo, [(lambda e, k=k, s=s, nb=nb, po=po: e.matmul(
                            po[:], lhsT=mT[:, k, s * 128:(s + 1) * 128], rhs=wo[:, k, nb * 512:(nb + 1) * 512],
                            start=(k == 0), stop=(k == KC - 1)), [bmT[k], bwo]) for k in range(KC)])
                        sc.emit("dve", lambda e, nb=nb, po=po, xs=xs: e.tensor_add(
                            out=xs[:, nb * 512:(nb + 1) * 512], in0=po[:], in1=xs[:, nb * 512:(nb + 1) * 512]),
                            reads=[bpo, bxs], writes=[bxs])
                    sc.emit("sp", lambda e, j=j, s=s, xs=xs: e.dma_start(
                        out=XM[j * TT + s * 128: j * TT + (s + 1) * 128, :], in_=xs[:]),
                        reads=[bxs], writes=[bXM[j]], dma=True)
          sc.barrier()

        if 4 in phases:
          with contextlib.ExitStack() as pstack:
            psb = lambda name, shape, dt: pstack.enter_context(nc.sbuf_tensor(f"{name}_L{l}P{len(uniq)}", list(shape), dt))
            uniq.append(0)
            wfi = psb("wfi", [128, KC, 2 * DFF], BF16)
            bwfi = [Buf() for _ in range(2 * FC)]
            fsrc = W["w_ffn_in"][l].rearrange("(kc p) n -> p kc n", p=128)
            for f2 in range(FC // 2):
                for half in range(2):
                    c0 = half * DFF + f2 * 256
                    sc.emit("pool", lambda e, c0=c0: e.dma_start(out=wfi[:, :, c0:c0 + 256], in_=fsrc[:, :, c0:c0 + 256]),
                            writes=[bwfi[half * FC + 2 * f2], bwfi[half * FC + 2 * f2 + 1]], dma=True)
            wfo = psb("wfo", [128, FC, D], BF16)
            bwfo = Buf()
            osrc = W["w_ffn_out"][l].rearrange("(fc p) n -> p fc n", p=128)
            for f0 in range(0, FC, 8):
                f1 = min(FC, f0 + 8)
                wload(wfo[:, f0:f1, :], osrc[:, f0:f1, :], bwfo)
            g2 = psb("g2", [128, D], F32)
            bg2 = Buf()
            load_bcast(g2[:], W["norm2_g"][l:l + 1, :], bg2)
            if last and final_norm:
                gf = psb("gf", [128, D], F32)
                bgf = Buf()
                load_bcast(gf[:], W["final_g"].rearrange("(o d) -> o d", o=1), bgf)
            xt = psb("xt4", [128, NS, D], F32)
            bxt = [Buf() for _ in range(NS)]
            hb = psb("hb4", [128, NS, D], BF16)
            bhb = [Buf() for _ in range(NS)]
            hT = psb("hT4", [128, KC, TT], BF16)
            bhT = [Buf() for _ in range(KC // 2)]
            sgs = Pool_([psb(f"sg{i}", [128, TT], F32) for i in range(2)])
            ffT = psb("ffT", [128, FC, TT], BF16)
            bff = [Buf() for _ in range(FC)]
            ss = psb("ss4", [128, NS], F32)
            rstd = psb("rstd4", [128, NS], F32)
            bss = [Buf() for _ in range(NS)]
            brstd = Buf()
            dst, bdst = (out, None) if last else (XL, bXL)
            for j in range(NT):
                for s in range(NS):
                    sc.emit("sp", lambda e, j=j, s=s: e.dma_start(
                        out=xt[:, s, :], in_=XM[j * TT + s * 128: j * TT + (s + 1) * 128, :]),
                        reads=[bXM[j]], writes=[bxt[s]], dma=True)
                rmsnorm_tile(xt, bxt, g2, bg2, hb, bhb, ss, bss, rstd, brstd)
                transpose_tile(hb, bhb, hT, bhT)
                for f in range(FC):
                    pg, bpg = pf.next()
                    sc.mm_group(bpg, [(lambda e, k=k, f=f, pg=pg: e.matmul(
                        pg[:], lhsT=wfi[:, k, f * 128:(f + 1) * 128], rhs=hT[:, k, :],
                        start=(k == 0), stop=(k == KC - 1)), [bhT[k // 2], bwfi[f]]) for k in range(KC)])
                    pu, bpu = pf.next()
                    sc.mm_group(bpu, [(lambda e, k=k, f=f, pu=pu: e.matmul(
                        pu[:], lhsT=wfi[:, k, DFF + f * 128: DFF + (f + 1) * 128], rhs=hT[:, k, :],
                        start=(k == 0), stop=(k == KC - 1)), [bhT[k // 2], bwfi[FC + f]]) for k in range(KC)])
                    sg, bsg = sgs.next()
                    sc.emit("act", lambda e, pg=pg, sg=sg: e.activation(out=sg[:], in_=pg[:], func=AF.Silu),
                            reads=[bpg], writes=[bsg])
                    sc.emit("dve", lambda e, f=f, pu=pu, sg=sg: e.tensor_mul(out=ffT[:, f, :], in0=pu[:], in1=sg[:]),
                            reads=[bpu, bsg], writes=[bff[f]])
                for s in range(NS):
                    for nb in range(2):
                        po, bpo = pf.next()
                        sc.mm_group(bpo, [(lambda e, f=f, s=s, nb=nb, po=po: e.matmul(
                            po[:], lhsT=ffT[:, f, s * 128:(s + 1) * 128], rhs=wfo[:, f, nb * 512:(nb + 1) * 512],
                            start=(f == 0), stop=(f == FC - 1)), [bff[f], bwfo]) for f in range(FC)])
                        sc.emit("dve", lambda e, s=s, nb=nb, po=po: e.tensor_add(
                            out=xt[:, s, nb * 512:(nb + 1) * 512], in0=po[:], in1=xt[:, s, nb * 512:(nb + 1) * 512]),
                            reads=[bpo, bxt[s]], writes=[bxt[s]])
                    if last and final_norm:
                        sc.emit("act", lambda e, s=s: e.activation(out=hb[:, s, :], in_=xt[:, s, :], func=AF.Square,
                                                                    accum_out=ss[:, s:s + 1]),
                                reads=[bxt[s]], writes=[bhb[s], bss[s]])
                        sc.emit("act", lambda e, s=s: e.activation(out=rstd[:, s:s + 1], in_=ss[:, s:s + 1], func=AF.Sqrt,
                                                                    scale=1.0 / D, bias=eps_c[:]),
                                reads=[bss[s], b_eps], writes=[brstd])
                        sc.emit("dve", lambda e, s=s: e.reciprocal(out=rstd[:, s:s + 1], in_=rstd[:, s:s + 1]),
                                reads=[brstd], writes=[brstd])
                        sc.emit("dve", lambda e, s=s: e.scalar_tensor_tensor(
                            out=xt[:, s, :], in0=xt[:, s, :], scalar=rstd[:, s:s + 1], in1=gf[:],
                            op0=ALU.mult, op1=ALU.mult), reads=[bxt[s], brstd, bgf], writes=[bxt[s]])
                    sc.emit("sp", lambda e, j=j, s=s: e.dma_start(
                        out=dst[j * TT + s * 128: j * TT + (s + 1) * 128, :], in_=xt[:, s, :]),
                        reads=[bxt[s]], writes=([bdst[j]] if bdst is not None else []), dma=True)
          sc.barrier()

    for li, l in enumerate(layers):
        emit_layer(li, l)
    sc.finish()
    with nc.Block() as block:
        sc.replay(block)


LAUNCH_PLAN = [((0, 1), True)]


def kernel(**inputs):
    x = np.ascontiguousarray(inputs["x"], dtype=np.float32)
    wnames = [k for k in inputs if k != "x"]
    wts = {k: np.ascontiguousarray(inputs[k], dtype=np.float32) for k in wnames}
    cur = [x[c] for c in range(N_CORES)]
    for layers, fn in LAUNCH_PLAN:
        nc = build_program(layers=layers, final_norm=fn)
        in_maps = []
        for c in range(N_CORES):
            m = {"x": cur[c]}
            m.update(wts)
            in_maps.append(m)
        res = run_bass_kernel_spmd(nc, in_maps, core_ids=list(range(N_CORES)))
        cur = [res.results[c]["out"] for c in range(N_CORES)]
    return np.stack(cur, axis=0)
```

```python
import contextlib
import numpy as np
import concourse.bass as bass
import concourse.mybir as mybir
from concourse.bass_utils import run_bass_kernel_spmd

F32 = mybir.dt.float32
BF16 = mybir.dt.bfloat16
AF = mybir.ActivationFunctionType
ALU = mybir.AluOpType

D = 1024
S = 4096
DEPTH = 2
KC = 8
DFF = 2816
FC = 22
NIN = 6144
TT = 512
NT = S // TT
NS = TT // 128
EPS = 1e-6
N_CORES = 8


def _snap(fn):
    out = []
    for c in (fn.__closure__ or ()):
        try:
            out.append(c.cell_contents)
        except ValueError:
            out.append(None)
    return out


def _check_snap(fn, snap):
    for name, c, old in zip(fn.__code__.co_freevars, fn.__closure__ or (), snap):
        new = c.cell_contents
        if new is not old and not (isinstance(new, (int, float, str, tuple)) and new == old):
            raise RuntimeError(f"late-binding bug: closure var {name!r} changed between emit and replay "
                               f"(line {fn.__code__.co_firstlineno})")


class Buf:
    __slots__ = ("name", "w", "r")

    def __init__(self, name=""):
        self.name = name
        self.w = None
        self.r = []


class _Eng:
    def __init__(self, name):
        self.name = name
        self.ops = []
        self.seen = {}
        self.semkey = None
        self.count = 0
        self.slots = []
        self.slot_i = 0


class Sched:
    ENGS = ("pe", "act", "dve", "pool", "sp")

    def __init__(self, nc, stack):
        self.nc = nc
        self.stack = stack
        self.sems = {}
        self.eng = {e: _Eng(e) for e in self.ENGS}
        self.nsem = 0
        for e in self.ENGS:
            self._new_eng_sem(self.eng[e])
        for e, n in (("sp", 12), ("pool", 8), ("act", 4)):
            for i in range(n):
                k = self._new_sem(f"d_{e}{i}")
                self.eng[e].slots.append([k, 0])

    def _new_sem(self, name):
        h = self.stack.enter_context(self.nc.semaphore(f"{name}_{self.nsem}"))
        self.nsem += 1
        self.sems[name + str(self.nsem)] = h
        return name + str(self.nsem)

    def _new_eng_sem(self, E):
        E.semkey = self._new_sem("e_" + E.name)
        E.count = 0

    def emit(self, eng, fn, reads=(), writes=(), dma=False, inc=True):
        E = self.eng[eng]
        waits = {}

        def need(ev):
            if ev is None:
                return
            k, v = ev
            if waits.get(k, 0) < v:
                waits[k] = v

        for b in reads:
            need(b.w)
        for b in writes:
            need(b.w)
            for ev in b.r:
                need(ev)
        final = []
        for k, v in waits.items():
            if E.seen.get(k, 0) >= v:
                continue
            E.seen[k] = v
            final.append((k, v))
        if dma:
            slot = E.slots[E.slot_i]
            E.slot_i = (E.slot_i + 1) % len(E.slots)
            k = slot[0]
            if slot[1] > 0 and E.seen.get(k, 0) < slot[1]:
                final.append((k, slot[1]))
                E.seen[k] = slot[1]
            if slot[1] + 16 > 65000:
                slot[0] = self._new_sem("d_" + E.name)
                slot[1] = 0
                k = slot[0]
            slot[1] += 16
            ev = (k, slot[1])
            incv = 16
        elif inc:
            if E.count >= 60000:
                self._new_eng_sem(E)
            E.count += 1
            ev = (E.semkey, E.count)
            incv = 1
        else:
            ev = None
            incv = 0
        E.ops.append((fn, final, ev, incv, _snap(fn)))
        if ev is not None:
            for b in writes:
                b.w = ev
                b.r = []
            for b in reads:
                b.r.append(ev)
        return ev

    def mm_group(self, bank, mms, extra_reads=()):
        n = len(mms)
        allreads = list(extra_reads)
        for i, (fn, rb) in enumerate(mms):
            for b in rb:
                if b not in allreads:
                    allreads.append(b)
            last = i == n - 1
            self.emit("pe", fn, reads=(allreads if last else rb),
                      writes=[bank] if (i == 0 or last) else (), inc=last)

    def barrier(self):
        final = []
        for e in self.ENGS:
            for k, c in self.eng[e].slots:
                if c > 0:
                    final.append((k, c))
            if self.eng[e].count > 0:
                final.append((self.eng[e].semkey, self.eng[e].count))
        for e in self.ENGS:
            E = self.eng[e]
            w = []
            for k, v in final:
                if k == E.semkey and e != "sp" and False:
                    continue
                if E.seen.get(k, 0) >= v:
                    continue
                E.seen[k] = v
                w.append((k, v))
            if w:
                E.ops.append((None, w, None, 0, None))

    def finish(self):
        E = self.eng["sp"]
        final = []
        for e in self.ENGS:
            for k, c in self.eng[e].slots:
                if c > 0:
                    final.append((k, c))
            if self.eng[e].count > 0 and e != "sp":
                final.append((self.eng[e].semkey, self.eng[e].count))
        E.ops.append((None, final, None, 0, None))

    def replay(self, block):
        nc = self.nc
        handles = {"pe": (block.tensor, nc.tensor), "act": (block.scalar, nc.scalar),
                   "dve": (block.vector, nc.vector), "pool": (block.gpsimd, nc.gpsimd),
                   "sp": (block.sync, nc.sync)}
        sems = self.sems
        for e in self.ENGS:
            ops = self.eng[e].ops
            deco, _ = handles[e]

            def body(eng, ops=ops):
                for fn, waits, ev, incv, snap in ops:
                    for k, v in waits:
                        eng.wait_ge(sems[k], v)
                    if fn is None:
                        continue
                    _check_snap(fn, snap)
                    ins = fn(eng)
                    if ev is not None:
                        ins.then_inc(sems[ev[0]], incv)
            deco(body)


class Pool_:
    def __init__(self, tiles, nsub=0):
        self.tiles = tiles
        self.nsub = nsub
        self.bufs = [([Buf() for _ in range(nsub)] if nsub else Buf()) for _ in tiles]
        self.i = 0

    def next(self):
        t, b = self.tiles[self.i], self.bufs[self.i]
        self.i = (self.i + 1) % len(self.tiles)
        return t, b


def build_program(layers=(0, 1), final_norm=True, debug=None, phases=(1, 2, 3, 4)):
    debug = debug or set()
    nc = bass.Bass("TRN2", target_bir_lowering=False)
    stack = contextlib.ExitStack()
    with stack:
        _build(nc, stack, layers, final_norm, debug, phases)
    return nc


def _dram(nc, name, shape, dt, kind):
    return nc.dram_tensor(name, list(shape), dt, kind=kind).ap()


def _build(nc, stack, layers, final_norm, debug, phases):
    L = DEPTH
    x_in = _dram(nc, "x", [S, D], F32, "ExternalInput")
    out = _dram(nc, "out", [S, D], F32, "ExternalOutput")
    W = {}
    for name, shape in (
        ("norm1_g", [L, D]), ("w_in", [L, D, NIN]), ("gmlp_ln_g", [L, D]), ("gmlp_ln_b", [L, D]),
        ("gmlp_w_s", [L, 8, 128, 128]), ("gmlp_b_s", [L, 8, 128]), ("conv_w", [L, 4, D]),
        ("conv_b", [L, D]), ("lru_w_r", [L, 2, 8, 128, 128]), ("lru_b_r", [L, 2, D]),
        ("lru_w_i", [L, 2, 8, 128, 128]), ("lru_b_i", [L, 2, D]), ("lru_lambda", [L, 2, D]),
        ("w_out", [L, D, D]), ("norm2_g", [L, D]), ("w_ffn_in", [L, D, 2 * DFF]),
        ("w_ffn_out", [L, DFF, D]), ("final_g", [D]),
    ):
        W[name] = _dram(nc, name, shape, F32, "ExternalInput")

    def scratch(name, shape, dt):
        kind = "ExternalOutput" if name in debug else "Internal"
        return _dram(nc, name, shape, dt, kind)

    HT = scratch("HT", [128, KC, S], BF16)
    XC = scratch("XC", [KC, 128, S], F32)
    HS = scratch("HS", [KC, 128, S], F32)
    XM = scratch("XM", [S, D], F32)
    XL = scratch("XL", [S, D], F32)

    sc = Sched(nc, stack)
    sb = lambda name, shape, dt: stack.enter_context(nc.sbuf_tensor(name, list(shape), dt))
    ps = lambda name, shape, dt: stack.enter_context(nc.psum_tensor(name, list(shape), dt))

    pfall = ps("pfall", [128, 6 * 512], F32)
    pf = Pool_([pfall[:, i * 512:(i + 1) * 512] for i in range(6)])
    pf2 = Pool_([pfall[:, i * 1024:(i + 1) * 1024] for i in range(3)])
    pb = Pool_([ps(f"pb{i}", [128, 1024], BF16) for i in range(2)])

    ident_f = sb("ident_f", [128, 128], F32)
    ident_b = sb("ident_b", [128, 128], BF16)
    eps_c = sb("eps_c", [128, 1], F32)
    one_c = sb("one_c", [128, 1], F32)
    b_ident_f, b_ident_b, b_eps, b_one = Buf(), Buf(), Buf(), Buf()

    sc.emit("pool", lambda e: e.memset(ident_f[:], 0.0), writes=[b_ident_f])
    sc.emit("pool", lambda e: e.affine_select(out=ident_f[:], in_=ident_f[:], compare_op=ALU.not_equal, fill=1.0,
                                              base=0, pattern=[[-1, 128]], channel_multiplier=1),
            reads=[b_ident_f], writes=[b_ident_f])
    sc.emit("pool", lambda e: e.tensor_copy(out=ident_b[:], in_=ident_f[:]), reads=[b_ident_f], writes=[b_ident_b])
    sc.emit("pool", lambda e: e.memset(eps_c[:], EPS), writes=[b_eps])
    sc.emit("pool", lambda e: e.memset(one_c[:], 1.0), writes=[b_one])

    bHT = [Buf() for _ in range(NT)]
    bXC = [Buf() for _ in range(NT + 1)]
    bHS = [Buf() for _ in range(KC)]
    bXM = [Buf() for _ in range(NT)]
    bXL = [Buf() for _ in range(NT)]

    def wload(dst, src, buf):
        sc.emit("pool", lambda e: e.dma_start(out=dst, in_=src), writes=[buf], dma=True)

    def load_bcast(dst, src_row, buf):
        sc.emit("sp", lambda e: e.dma_start(out=dst, in_=src_row.partition_broadcast(128)), writes=[buf], dma=True)

    def rmsnorm_tile(xt, bxt, g_bc, bg, hb, bhb, ss, bss, rstd, brstd):
        for s in range(NS):
            sc.emit("act", lambda e, s=s: e.activation(out=hb[:, s, :], in_=xt[:, s, :], func=AF.Square,
                                                        accum_out=ss[:, s:s + 1]),
                    reads=[bxt[s]], writes=[bhb[s], bss[s]])
        sc.emit("act", lambda e: e.activation(out=rstd[:], in_=ss[:], func=AF.Sqrt, scale=1.0 / D, bias=eps_c[:]),
                reads=bss + [b_eps], writes=[brstd])
        sc.emit("dve", lambda e: e.reciprocal(out=rstd[:], in_=rstd[:]), reads=[brstd], writes=[brstd])
        for s in range(NS):
            sc.emit("dve", lambda e, s=s: e.scalar_tensor_tensor(out=hb[:, s, :], in0=xt[:, s, :], scalar=rstd[:, s:s + 1],
                                                                  in1=g_bc[:], op0=ALU.mult, op1=ALU.mult),
                    reads=[bxt[s], brstd, bg], writes=[bhb[s]])

    def transpose_tile(hb, bhb, hT, bhT):
        for k2 in range(KC // 2):
            pt, bpt = pb.next()
            mms = []
            for kk in range(2):
                k = 2 * k2 + kk
                for s in range(NS):
                    mms.append((lambda e, k=k, kk=kk, s=s, pt=pt: e.transpose(
                        out=pt[:, kk * 512 + s * 128: kk * 512 + (s + 1) * 128],
                        in_=hb[:, s, k * 128:(k + 1) * 128], identity=ident_b[:]), [bhb[s], b_ident_b]))
            sc.mm_group(bpt, mms)
            sc.emit("act", lambda e, k2=k2, pt=pt: e.copy(out=hT[:, 2 * k2:2 * k2 + 2, :],
                                                          in_=pt[:].rearrange("p (a t) -> p a t", a=2)),
                    reads=[bpt], writes=[bhT[k2]])

    uniq = []

    def emit_layer(li, l):
        x_src, bx_src = (x_in, None) if li == 0 else (XL, bXL)
        last = li == len(layers) - 1
        lstack = contextlib.ExitStack()
        lsb = lambda name, shape, dt: lstack.enter_context(nc.sbuf_tensor(f"{name}_L{l}", list(shape), dt))
        vt = lsb("vt", [128, 128], F32)
        cols = lsb("cols", [128, 128], F32)
        bvt, bcols = Buf(), Buf()
        sc.emit("pool", lambda e: e.memset(vt[:], 0.0), writes=[bvt])
        sc.emit("sp", lambda e: e.dma_start(out=vt[0:32, :], in_=W["conv_w"][l].rearrange("k (c p) -> (k c) p", p=128)),
                writes=[bvt], dma=True)
        sc.emit("sp", lambda e: e.dma_start(out=vt[32:40, :], in_=W["conv_b"][l].rearrange("(c p) -> c p", p=128)),
                writes=[bvt], dma=True)
        for base, nm in ((40, "lru_b_r"), (56, "lru_b_i"), (72, "lru_lambda")):
            sc.emit("sp", lambda e, base=base, nm=nm: e.dma_start(
                out=vt[base:base + 16, :], in_=W[nm][l].rearrange("d (c p) -> (d c) p", p=128)),
                writes=[bvt], dma=True)
        pz, bpz = pf.next()
        sc.mm_group(bpz, [(lambda e, pz=pz: e.transpose(out=pz[:, 0:128], in_=vt[:], identity=ident_f[:]),
                           [bvt, b_ident_f])])
        sc.emit("dve", lambda e, pz=pz: e.tensor_copy(out=cols[:], in_=pz[:, 0:128]), reads=[bpz], writes=[bcols])
        tmp = lsb("tmpc", [128, 4, 16], F32)
        btmp = Buf()
        cneg = lsb("cneg", [128, 16], F32)
        bcneg = Buf()
        ee, zz, z2, pp = tmp[:, 0, :], tmp[:, 1, :], tmp[:, 2, :], tmp[:, 3, :]
        sc.emit("act", lambda e: e.activation(out=ee, in_=cols[:, 72:88], func=AF.Exp, scale=-1.0),
                reads=[bcols], writes=[btmp])
        dv = lambda fn: sc.emit("dve", fn, reads=[btmp], writes=[btmp])
        dv(lambda e: e.tensor_scalar_add(out=zz, in0=ee, scalar1=2.0))
        dv(lambda e: e.reciprocal(out=zz, in_=zz))
        dv(lambda e: e.tensor_mul(out=zz, in0=zz, in1=ee))
        dv(lambda e: e.tensor_mul(out=z2, in0=zz, in1=zz))
        dv(lambda e: e.tensor_scalar(out=pp, in0=z2, scalar1=1.0 / 11.0, scalar2=1.0 / 9.0, op0=ALU.mult, op1=ALU.add))
        for cst in (1.0 / 7.0, 1.0 / 5.0, 1.0 / 3.0, 1.0):
            dv(lambda e: e.tensor_mul(out=pp, in0=pp, in1=z2))
            dv(lambda e, cst=cst: e.tensor_scalar_add(out=pp, in0=pp, scalar1=cst))
        dv(lambda e: e.tensor_mul(out=pp, in0=pp, in1=zz))
        sc.emit("dve", lambda e: e.tensor_scalar_mul(out=cneg[:], in0=pp, scalar1=-16.0), reads=[btmp], writes=[bcneg])

        hbias = lsb("hbias", [128, 32], F32)
        hcneg = lsb("hcneg", [128, 16], F32)
        bhbias, bhcneg = Buf(), Buf()
        sc.emit("dve", lambda e: e.tensor_scalar_mul(out=hbias[:], in0=cols[:, 40:72], scalar1=0.5),
                reads=[bcols], writes=[bhbias])
        sc.emit("dve", lambda e: e.tensor_scalar_mul(out=hcneg[:], in0=cneg[:], scalar1=0.5),
                reads=[bcneg], writes=[bhcneg])

        if 1 in phases:
          with contextlib.ExitStack() as pstack:
            psb = lambda name, shape, dt: pstack.enter_context(nc.sbuf_tensor(f"{name}_L{l}P{len(uniq)}", list(shape), dt))
            uniq.append(0)
            wzx = psb("wzx", [128, KC, D], BF16)
            bwzx = Buf()
            wload(wzx[:], W["w_in"][l, :, 2048:3072].rearrange("(kc p) n -> p kc n", p=128), bwzx)
            g1 = psb("g1", [128, D], F32)
            bg1 = Buf()
            load_bcast(g1[:], W["norm1_g"][l:l + 1, :], bg1)
            xts = Pool_([psb(f"xt{i}", [128, NS, D], F32) for i in range(2)], nsub=NS)
            hbs = Pool_([psb(f"hb{i}", [128, NS, D], BF16) for i in range(2)], nsub=NS)
            hTs = Pool_([psb(f"hT{i}", [128, KC, TT], BF16) for i in range(2)], nsub=KC // 2)
            zxbs = Pool_([psb(f"zxb{i}", [128, KC, TT + 3], F32) for i in range(2)], nsub=KC + 1)
            xcos = Pool_([psb(f"xco{i}", [128, KC, TT], F32) for i in range(2)])
            ss = psb("ss", [128, NS], F32)
            rstd = psb("rstd", [128, NS], F32)
            bss = [Buf() for _ in range(NS)]
            brstd = Buf()
            xloads = {}
            prev = None

            def xload(j):
                xt, bxt = xts.next()
                rd = [bx_src[j]] if bx_src is not None else []
                sc.emit("sp", lambda e, j=j, xt=xt: e.dma_start(
                    out=xt[:], in_=x_src[j * TT:(j + 1) * TT, :].rearrange("(s p) d -> p s d", p=128)),
                    reads=rd, writes=bxt, dma=True)
                xloads[j] = (xt, bxt)
            xload(0)
            for j in range(NT):
                if j + 1 < NT:
                    xload(j + 1)
                xt, bxt = xloads.pop(j)
                hb, bhb = hbs.next()
                rmsnorm_tile(xt, bxt, g1, bg1, hb, bhb, ss, bss, rstd, brstd)
                hT, bhT = hTs.next()
                transpose_tile(hb, bhb, hT, bhT)
                sc.emit("sp", lambda e, j=j, hT=hT: e.dma_start(out=HT[:, :, j * TT:(j + 1) * TT], in_=hT[:]),
                        reads=bhT, writes=[bHT[j]], dma=True)
                zxb, bzxb = zxbs.next()
                if j == 0:
                    sc.emit("pool", lambda e, zxb=zxb: e.memset(zxb[:, :, 0:3], 0.0), writes=[bzxb[KC]])
                else:
                    sc.emit("pool", lambda e, zxb=zxb, pzxb=prev[0]: e.tensor_copy(out=zxb[:, :, 0:3], in_=pzxb[:, :, TT:TT + 3]),
                            reads=prev[1][0:KC], writes=[bzxb[KC]])
                xco, bxco = xcos.next()
                for c in range(KC):
                    pz, bpz = pf.next()
                    mms = []
                    for k in range(KC):
                        mms.append((lambda e, c=c, k=k, pz=pz, hT=hT: e.matmul(
                            pz[:], lhsT=wzx[:, k, c * 128:(c + 1) * 128], rhs=hT[:, k, :],
                            start=(k == 0), stop=(k == KC - 1)), [bhT[k // 2], bwzx]))
                    sc.mm_group(bpz, mms)
                    sc.emit("act", lambda e, c=c, pz=pz, zxb=zxb: e.copy(out=zxb[:, c, 3:3 + TT], in_=pz[:]),
                            reads=[bpz], writes=[bzxb[c]])
                    sc.emit("dve", lambda e, c=c, zxb=zxb, xco=xco: e.tensor_scalar(
                        out=xco[:, c, :], in0=zxb[:, c, 0:TT], scalar1=cols[:, c:c + 1], scalar2=cols[:, 32 + c:33 + c],
                        op0=ALU.mult, op1=ALU.add), reads=[bzxb[c], bzxb[KC], bcols], writes=[bxco])
                    for k in range(1, 4):
                        sc.emit("dve", lambda e, c=c, k=k, zxb=zxb, xco=xco: e.scalar_tensor_tensor(
                            out=xco[:, c, :], in0=zxb[:, c, k:k + TT], scalar=cols[:, k * 8 + c:k * 8 + c + 1],
                            in1=xco[:, c, :], op0=ALU.mult, op1=ALU.add), reads=[bzxb[c], bzxb[KC], bcols, bxco], writes=[bxco])
                if j == 0:
                    sc.emit("sp", lambda e, xco=xco: e.dma_start(
                        out=XC[:, :, 0:TT - 2].rearrange("c p t -> p c t"), in_=xco[:, :, 2:TT]),
                        reads=[bxco], writes=[bXC[j]], dma=True)
                else:
                    sc.emit("sp", lambda e, j=j, xco=xco: e.dma_start(
                        out=XC[:, :, j * TT - 2:(j + 1) * TT - 2].rearrange("c p t -> p c t"), in_=xco[:]),
                        reads=[bxco], writes=[bXC[j]], dma=True)
                prev = (zxb, bzxb)
            zxb, bzxb = zxbs.next()
            sc.emit("pool", lambda e, zxb=zxb, pzxb=prev[0]: e.tensor_copy(out=zxb[:, :, 0:3], in_=pzxb[:, :, TT:TT + 3]),
                    reads=prev[1][0:KC], writes=[bzxb[KC]])
            sc.emit("pool", lambda e, zxb=zxb: e.memset(zxb[:, :, 3:5], 0.0), writes=bzxb[0:KC])
            xco, bxco = xcos.next()
            for c in range(KC):
                sc.emit("dve", lambda e, c=c, zxb=zxb, xco=xco: e.tensor_scalar(
                    out=xco[:, c, 0:2], in0=zxb[:, c, 0:2], scalar1=cols[:, c:c + 1], scalar2=cols[:, 32 + c:33 + c],
                    op0=ALU.mult, op1=ALU.add), reads=[bzxb[c], bzxb[KC], bcols], writes=[bxco])
                for k in range(1, 4):
                    sc.emit("dve", lambda e, c=c, k=k, zxb=zxb, xco=xco: e.scalar_tensor_tensor(
                        out=xco[:, c, 0:2], in0=zxb[:, c, k:k + 2], scalar=cols[:, k * 8 + c:k * 8 + c + 1],
                        in1=xco[:, c, 0:2], op0=ALU.mult, op1=ALU.add), reads=[bzxb[c], bzxb[KC], bcols, bxco], writes=[bxco])
            sc.emit("sp", lambda e, xco=xco: e.dma_start(
                out=XC[:, :, S - 2:S].rearrange("c p t -> p c t"), in_=xco[:, :, 0:2]),
                reads=[bxco], writes=[bXC[NT]], dma=True)
          sc.barrier()

        if 2 in phases:
          with contextlib.ExitStack() as pstack:
            psb = lambda name, shape, dt: pstack.enter_context(nc.sbuf_tensor(f"{name}_L{l}P{len(uniq)}", list(shape), dt))
            uniq.append(0)
            wg = psb("wg", [128, 2, 2, KC, 128], BF16)
            bwg = Buf()
            for t, nm in ((0, "lru_w_r"), (1, "lru_w_i")):
                for d in range(2):
                    wload(wg[:, t, d, :, :], W[nm][l, d].rearrange("h c o -> c h o"), bwg)
            NB = S // 512
            xcs = Pool_([psb(f"xc{i}", [128, S], F32) for i in range(2)], nsub=NB)
            xcbs = Pool_([psb(f"xcb{i}", [128, S], BF16) for i in range(2)], nsub=NB)
            ra = [psb(f"ra{d}", [128, S], F32) for d in range(2)]
            ib = [psb(f"ib{d}", [128, S], F32) for d in range(2)]
            mh = [psb(f"mh{d}", [128, S], F32) for d in range(2)]
            bra, bib, bmh = [Buf(), Buf()], [Buf(), Buf()], [Buf(), Buf()]
            hso = psb("hso", [128, S], F32)
            bhso = Buf()

            def conv_stage(c):
                xc, bxc = xcs.next()
                xcb, bxcb = xcbs.next()
                sc.emit("sp", lambda e, c=c, xc=xc: e.dma_start(out=xc[:], in_=XC[c]), reads=bXC, writes=bxc, dma=True)
                for tb in range(NB):
                    sc.emit("pool", lambda e, tb=tb, xc=xc, xcb=xcb: e.tensor_copy(
                        out=xcb[:, tb * 512:(tb + 1) * 512], in_=xc[:, tb * 512:(tb + 1) * 512]),
                        reads=[bxc[tb]], writes=[bxcb[tb]])
                return xc, bxc, xcb, bxcb

            staged = {0: conv_stage(0)}
            for c in range(KC):
                xc, bxc, xcb, bxcb = staged.pop(c)
                for d in range(2):
                    for t in range(2):
                        dst, bdst = (ra[d], bra[d]) if t == 0 else (ib[d], bib[d])
                        bcol = (0 if t == 0 else 16) + d * 8 + c
                        for tb2 in range(NB // 2):
                            pz, bpz = pf2.next()
                            sc.mm_group(bpz, [(lambda e, t=t, d=d, c=c, tb=2 * tb2 + h, h=h, pz=pz, xcb=xcb: e.matmul(
                                pz[:, h * 512:(h + 1) * 512], lhsT=wg[:, t, d, c, :], rhs=xcb[:, tb * 512:(tb + 1) * 512],
                                start=True, stop=True), [bwg, bxcb[2 * tb2 + h]]) for h in range(2)])
                            sc.emit("act", lambda e, dst=dst, tb2=tb2, pz=pz, bcol=bcol: e.activation(
                                out=dst[:, tb2 * 1024:(tb2 + 1) * 1024], in_=pz[:], func=AF.Tanh, scale=0.5,
                                bias=hbias[:, bcol:bcol + 1]), reads=[bpz, bhbias], writes=[bdst])
                    sc.emit("act", lambda e, d=d, c=c: e.activation(
                        out=mh[d][:], in_=ra[d][:], func=AF.Exp, scale=cneg[:, d * 8 + c:d * 8 + c + 1],
                        bias=cneg[:, d * 8 + c:d * 8 + c + 1]), reads=[bra[d], bcneg], writes=[bmh[d]])
                    sc.emit("act", lambda e, d=d, c=c: e.activation(
                        out=ra[d][:], in_=ra[d][:], func=AF.Exp, scale=hcneg[:, d * 8 + c:d * 8 + c + 1],
                        bias=hcneg[:, d * 8 + c:d * 8 + c + 1]), reads=[bra[d], bhcneg], writes=[bra[d]])
                    sc.emit("act", lambda e, d=d: e.activation(out=mh[d][:], in_=mh[d][:], func=AF.Sqrt,
                                                               scale=-1.0, bias=one_c[:]),
                            reads=[bmh[d], b_one], writes=[bmh[d]])
                    if d == 0 and c + 1 < KC:
                        staged[c + 1] = conv_stage(c + 1)
                for d in range(2):
                    sc.emit("dve", lambda e, d=d, xc=xc: e.scalar_tensor_tensor(
                        out=mh[d][:], in0=mh[d][:], scalar=0.5, in1=xc[:], op0=ALU.mult, op1=ALU.mult),
                        reads=[bmh[d]] + bxc, writes=[bmh[d]])
                    sc.emit("dve", lambda e, d=d: e.scalar_tensor_tensor(
                        out=ib[d][:], in0=ib[d][:], scalar=1.0, in1=mh[d][:], op0=ALU.add, op1=ALU.mult),
                        reads=[bib[d], bmh[d]], writes=[bib[d]])
                    if d == 0:
                        sc.emit("dve", lambda e: e.tensor_tensor_scan(out=hso[:], data0=ra[0][:], data1=ib[0][:],
                                                                       initial=0.0, op0=ALU.mult, op1=ALU.add),
                                reads=[bra[0], bib[0]], writes=[bhso])
                    else:
                        sc.emit("dve", lambda e: e.tensor_tensor_scan(out=mh[1][:, ::-1], data0=ra[1][:, ::-1],
                                                                       data1=ib[1][:, ::-1], initial=0.0,
                                                                       op0=ALU.mult, op1=ALU.add),
                                reads=[bra[1], bib[1], bmh[1]], writes=[bmh[1]])
                sc.emit("dve", lambda e: e.tensor_add(out=hso[:], in0=hso[:], in1=mh[1][:]),
                        reads=[bhso, bmh[1]], writes=[bhso])
                sc.emit("sp", lambda e, c=c: e.dma_start(out=HS[c], in_=hso[:]), reads=[bhso], writes=[bHS[c]], dma=True)
          sc.barrier()

        lstack.close()
        if 3 in phases:
          with contextlib.ExitStack() as pstack:
            psb = lambda name, shape, dt: pstack.enter_context(nc.sbuf_tensor(f"{name}_L{l}P{len(uniq)}", list(shape), dt))
            uniq.append(0)
            w3 = psb("w3", [128, KC, 5 * D], BF16)
            bw3 = [Buf() for _ in range(5)]
            wsrc = W["w_in"][l].rearrange("(kc p) n -> p kc n", p=128)
            for blk, c0 in ((1, 1024), (0, 0), (3, 4096), (2, 3072), (4, 5120)):
                wload(w3[:, :, blk * D:(blk + 1) * D], wsrc[:, :, c0:c0 + D], bw3[blk])
            wo = psb("wo", [128, KC, D], BF16)
            bwo = Buf()
            wload(wo[:], W["w_out"][l].rearrange("(kc p) n -> p kc n", p=128), bwo)
            wsn = psb("wsn", [128, 8, 128], F32)
            wsT = psb("wsT", [128, 8, 128], BF16)
            bwsn, bwsT = Buf(), Buf()
            sc.emit("sp", lambda e: e.dma_start(out=wsn[:], in_=W["gmlp_w_s"][l].rearrange("g p q -> p g q")),
                    writes=[bwsn], dma=True)
            for g2 in range(2):
                pz, bpz = pf.next()
                sc.mm_group(bpz, [(lambda e, g=g2 * 4 + gg, gg=gg, pz=pz: e.transpose(
                    out=pz[:, gg * 128:(gg + 1) * 128], in_=wsn[:, g, :], identity=ident_f[:]), [bwsn, b_ident_f])
                    for gg in range(4)])
                sc.emit("dve", lambda e, g2=g2, pz=pz: e.tensor_copy(
                    out=wsT[:, g2 * 4:(g2 + 1) * 4, :], in_=pz[:].rearrange("p (g q) -> p g q", g=4)),
                    reads=[bpz], writes=[bwsT])
            bsb = psb("bsb", [128, 8, 128], F32)
            lng = psb("lng", [128, D], F32)
            lnb = psb("lnb", [128, D], F32)
            bbsb, blng, blnb = Buf(), Buf(), Buf()
            load_bcast(bsb[:].rearrange("p g q -> p (g q)"), W["gmlp_b_s"][l:l + 1].rearrange("o g q -> o (g q)"), bbsb)
            load_bcast(lng[:], W["gmlp_ln_g"][l:l + 1, :], blng)
            load_bcast(lnb[:], W["gmlp_ln_b"][l:l + 1, :], blnb)

            hTs = Pool_([psb(f"hT{i}", [128, KC, TT], BF16) for i in range(2)])
            hss = Pool_([psb(f"hs{i}", [128, TT], F32) for i in range(3)])
            xss = Pool_([psb(f"xs{i}", [128, D], F32) for i in range(2)])
            vgs = Pool_([psb(f"vg{i}", [128, D], F32) for i in range(2)])
            vn = psb("vn", [128, NS, D], BF16)
            vn2 = psb("vn2", [128, NS, D], BF16)
            mT = psb("mT", [128, KC, TT], BF16)
            bmT = [Buf() for _ in range(KC)]
            tA = Pool_([psb(f"tA{i}", [128, TT], F32) for i in range(2)])
            tB = Pool_([psb(f"tB{i}", [128, TT], F32) for i in range(2)])
            tC = Pool_([psb(f"tC{i}", [128, TT], F32) for i in range(2)])
            tD = Pool_([psb(f"tD{i}", [128, TT], F32) for i in range(2)])
            st = psb("lnst", [128, NS, 2, 6], F32)
            mv = psb("lnmv", [128, NS, 2], F32)
            lrs = psb("lnrs", [128, NS], F32)
            lnm = psb("lnnm", [128, NS], F32)
            bst = [Buf() for _ in range(NS)]
            bmv = [Buf() for _ in range(NS)]
            blrs, blnm = Buf(), Buf()

            def zmm(pz, blk, c, hT, bhT):
                return [(lambda e, k=k: e.matmul(pz[:], lhsT=w3[:, k, blk * D + c * 128: blk * D + (c + 1) * 128],
                                                  rhs=hT[:, k, :], start=(k == 0), stop=(k == KC - 1)),
                         [bhT, bw3[blk]]) for k in range(KC)]

            vns = Pool_([vn, vn2], nsub=NS)
            hloads, vstate = {}, {}

            def hload(j):
                hT, bhT = hTs.next()
                sc.emit("sp", lambda e, j=j, hT=hT: e.dma_start(out=hT[:], in_=HT[:, :, j * TT:(j + 1) * TT]),
                        reads=[bHT[j]], writes=[bhT], dma=True)
                hloads[j] = (hT, bhT)

            def vpath_sub(j, s):
                hT, bhT = hloads[j]
                if s == 0:
                    vstate[j] = (vns.next(), [])
                (vnj, bvnj), vgl = vstate[j]
                vg, bvg = vgs.next()
                for nb in range(2):
                    pz, bpz = pf.next()
                    sc.mm_group(bpz, [(lambda e, k=k, s=s, nb=nb, pz=pz, hT=hT: e.matmul(
                        pz[:], lhsT=hT[:, k, s * 128:(s + 1) * 128],
                        rhs=w3[:, k, D + nb * 512: D + (nb + 1) * 512], start=(k == 0), stop=(k == KC - 1)),
                        [bhT, bw3[1]]) for k in range(KC)])
                    sc.emit("act", lambda e, nb=nb, pz=pz, vg=vg: e.activation(
                        out=vg[:, nb * 512:(nb + 1) * 512], in_=pz[:], func=AF.Gelu_apprx_tanh),
                        reads=[bpz], writes=[bvg])
                for nb in range(2):
                    sc.emit("dve", lambda e, s=s, nb=nb, vg=vg: e.bn_stats(out=st[:, s, nb, :], in_=vg[:, nb * 512:(nb + 1) * 512]),
                            reads=[bvg], writes=[bst[s]])
                sc.emit("dve", lambda e, s=s: e.bn_aggr(out=mv[:, s, :], in_=st[:, s, :, :]), reads=[bst[s]], writes=[bmv[s]])
                vgl.append((vg, bvg))
                if s % 2 == 1:
                    s0 = s - 1
                    sc.emit("act", lambda e, s0=s0: e.activation(out=lrs[:, s0:s0 + 2], in_=mv[:, s0:s0 + 2, 1], func=AF.Sqrt,
                                                                  bias=eps_c[:]),
                            reads=[bmv[s0], bmv[s0 + 1], b_eps], writes=[blrs])
                    sc.emit("dve", lambda e, s0=s0: e.reciprocal(out=lrs[:, s0:s0 + 2], in_=lrs[:, s0:s0 + 2]),
                            reads=[blrs], writes=[blrs])
                    for s1 in (s0, s0 + 1):
                        vg1, bvg1 = vgl[s1]
                        sc.emit("dve", lambda e, s1=s1, vg1=vg1: e.tensor_scalar(
                            out=vg1[:], in0=vg1[:], scalar1=mv[:, s1, 0:1], scalar2=lrs[:, s1:s1 + 1],
                            op0=ALU.subtract, op1=ALU.mult), reads=[bvg1, bmv[s1], blrs], writes=[bvg1])
                        sc.emit("dve", lambda e, vg1=vg1: e.tensor_mul(out=vg1[:], in0=vg1[:], in1=lng[:]),
                                reads=[bvg1, blng], writes=[bvg1])
                        sc.emit("dve", lambda e, s1=s1, vg1=vg1, vnj=vnj: e.tensor_add(out=vnj[:, s1, :], in0=vg1[:], in1=lnb[:]),
                                reads=[bvg1, blnb], writes=[bvnj[s1]])

            hload(0)
            for s in range(NS):
                vpath_sub(0, s)
            for j in range(NT):
                if j + 1 < NT:
                    hload(j + 1)
                hT, bhT = hloads[j]
                (vnj, bvn), _ = vstate[j]
                for c in range(KC):
                    hs, bhs = hss.next()
                    sc.emit("sp", lambda e, c=c, j=j, hs=hs: e.dma_start(out=hs[:], in_=HS[c, :, j * TT:(j + 1) * TT]),
                            reads=bHS, writes=[bhs], dma=True)
                    a_, ba = tA.next()
                    b_, bb = tB.next()
                    c_, bc = tC.next()
                    d_, bd = tD.next()
                    pu, bpu = pf.next()
                    sc.mm_group(bpu, zmm(pu, 0, c, hT, bhT))
                    sc.emit("act", lambda e, pu=pu, a_=a_: e.activation(out=a_[:], in_=pu[:], func=AF.Gelu_apprx_tanh),
                            reads=[bpu], writes=[ba])
                    pa, bpa = pf.next()
                    sc.mm_group(bpa, zmm(pa, 3, c, hT, bhT))
                    sc.emit("act", lambda e, pa=pa, b_=b_: e.activation(out=b_[:], in_=pa[:], func=AF.Tanh, scale=0.5),
                            reads=[bpa], writes=[bb])
                    pg, bpg = pf.next()
                    sc.mm_group(bpg, zmm(pg, 2, c, hT, bhT))
                    sc.emit("act", lambda e, pg=pg, c_=c_: e.activation(out=c_[:], in_=pg[:], func=AF.Gelu_apprx_tanh),
                            reads=[bpg], writes=[bc])
                    pB, bpB = pf.next()
                    sc.mm_group(bpB, zmm(pB, 4, c, hT, bhT))
                    sc.emit("act", lambda e, pB=pB, d_=d_: e.activation(out=d_[:], in_=pB[:], func=AF.Tanh, scale=0.5),
                            reads=[bpB], writes=[bd])
                    pm, bpm = pf.next()
                    sc.mm_group(bpm, [(lambda e, s=s, c=c, pm=pm, vnj=vnj: e.matmul(
                        pm[:, s * 128:(s + 1) * 128], lhsT=vnj[:, s, c * 128:(c + 1) * 128], rhs=wsT[:, c, :],
                        start=True, stop=True), [bvn[s], bwsT]) for s in range(NS)])
                    sc.emit("dve", lambda e, a_=a_, b_=b_: e.scalar_tensor_tensor(
                        out=a_[:], in0=b_[:], scalar=1.0, in1=a_[:], op0=ALU.add, op1=ALU.mult),
                        reads=[ba, bb], writes=[ba])
                    sc.emit("dve", lambda e, c=c, pm=pm, b_=b_: e.tensor_tensor(
                        out=b_[:].rearrange("p (s q) -> p s q", s=NS), in0=pm[:].rearrange("p (s q) -> p s q", s=NS),
                        in1=bsb[:, c:c + 1, :].to_broadcast([128, NS, 128]), op=ALU.add),
                        reads=[bpm, bbsb, bb], writes=[bb])
                    sc.emit("dve", lambda e, a_=a_, b_=b_: e.tensor_mul(out=a_[:], in0=a_[:], in1=b_[:]),
                            reads=[ba, bb], writes=[ba])
                    sc.emit("dve", lambda e, c_=c_, d_=d_: e.scalar_tensor_tensor(
                        out=c_[:], in0=d_[:], scalar=1.0, in1=c_[:], op0=ALU.add, op1=ALU.mult),
                        reads=[bc, bd], writes=[bc])
                    sc.emit("pool", lambda e, c_=c_, hs=hs: e.tensor_mul(out=c_[:], in0=c_[:], in1=hs[:]),
                            reads=[bc, bhs], writes=[bc])
                    sc.emit("dve", lambda e, c=c, a_=a_, c_=c_: e.tensor_add(out=mT[:, c, :], in0=a_[:], in1=c_[:]),
                            reads=[ba, bc], writes=[bmT[c]])
                    if j + 1 < NT and c % 2 == 1:
                        vpath_sub(j + 1, c // 2)
                for s in range(NS):
                    xs, bxs = xss.next()
                    rd = [bx_src[j]] if bx_src is not None else []
                    sc.emit("sp", lambda e, j=j, s=s, xs=xs: e.dma_start(
                        out=xs[:], in_=x_src[j * TT + s * 128: j * TT + (s + 1) * 128, :]),
                        reads=rd, writes=[bxs], dma=True)
                    for nb in range(2):
                        po, bpo = pf.next()
                        sc.mm_group(bpo, [(lambda e, k=k, s=s, nb=nb, po=po: e.matmul(
                            po[:], lhsT=mT[:, k, s * 128:(s + 1) * 128], rhs=wo[:, k, nb * 512:(nb + 1) * 512],
                            start=(k == 0), stop=(k == KC - 1)), [bmT[k], bwo]) for k in range(KC)])
                        sc.emit("dve", lambda e, nb=nb, po=po, xs=xs: e.scalar_tensor_tensor(
                            out=xs[:, nb * 512:(nb + 1) * 512], in0=po[:], scalar=0.5, in1=xs[:, nb * 512:(nb + 1) * 512],
                            op0=ALU.mult, op1=ALU.add),
                            reads=[bpo, bxs], writes=[bxs])
                    sc.emit("sp", lambda e, j=j, s=s, xs=xs: e.dma_start(
                        out=XM[j * TT + s * 128: j * TT + (s + 1) * 128, :], in_=xs[:]),
                        reads=[bxs], writes=[bXM[j]], dma=True)
                hloads.pop(j)
                vstate.pop(j)
          sc.barrier()

        if 4 in phases:
          with contextlib.ExitStack() as pstack:
            psb = lambda name, shape, dt: pstack.enter_context(nc.sbuf_tensor(f"{name}_L{l}P{len(uniq)}", list(shape), dt))
            uniq.append(0)
            wfi = psb("wfi", [128, KC, 2 * DFF], BF16)
            bwfi = [Buf() for _ in range(2 * FC)]
            fsrc = W["w_ffn_in"][l].rearrange("(kc p) n -> p kc n", p=128)
            for f2 in range(FC // 2):
                for half in range(2):
                    c0 = half * DFF + f2 * 256
                    sc.emit("pool", lambda e, c0=c0: e.dma_start(out=wfi[:, :, c0:c0 + 256], in_=fsrc[:, :, c0:c0 + 256]),
                            writes=[bwfi[half * FC + 2 * f2], bwfi[half * FC + 2 * f2 + 1]], dma=True)
            wfo = psb("wfo", [128, FC, D], BF16)
            bwfo = Buf()
            osrc = W["w_ffn_out"][l].rearrange("(fc p) n -> p fc n", p=128)
            for f0 in range(0, FC, 8):
                f1 = min(FC, f0 + 8)
                wload(wfo[:, f0:f1, :], osrc[:, f0:f1, :], bwfo)
            g2 = psb("g2", [128, D], F32)
            bg2 = Buf()
            load_bcast(g2[:], W["norm2_g"][l:l + 1, :], bg2)
            if last and final_norm:
                gf = psb("gf", [128, D], F32)
                bgf = Buf()
                load_bcast(gf[:], W["final_g"].rearrange("(o d) -> o d", o=1), bgf)
                fss = psb("fss", [128, 4], F32)
                bfss = [Buf() for _ in range(4)]
            xss4 = Pool_([psb(f"xs4{i}", [128, D], F32) for i in range(2)])
            xrs4 = Pool_([psb(f"xr4{i}", [128, D], F32) for i in range(2)])
            hb = psb("hb4", [128, NS, D], BF16)
            bhb = [Buf() for _ in range(NS)]
            hTs4 = Pool_([psb(f"hT4{i}", [128, KC, TT], BF16) for i in range(2)], nsub=KC // 2)
            sgs = Pool_([psb(f"sg{i}", [128, TT], F32) for i in range(2)])
            ffT = psb("ffT", [128, FC, TT], BF16)
            bff = [Buf() for _ in range(FC)]
            ss = psb("ss4", [128, NS], F32)
            rstd = psb("rstd4", [128, NS], F32)
            bss = [Buf() for _ in range(NS)]
            brs = [Buf() for _ in range(NS)]
            dst, bdst = (out, None) if last else (XL, bXL)
            staged = {}

            def stage(j):
                hT, bhT = hTs4.next()
                for s in range(NS):
                    xs, bxs = xss4.next()
                    sc.emit("sp", lambda e, j=j, s=s, xs=xs: e.dma_start(
                        out=xs[:], in_=XM[j * TT + s * 128: j * TT + (s + 1) * 128, :]),
                        reads=[bXM[j]], writes=[bxs], dma=True)
                    sc.emit("act", lambda e, s=s, xs=xs: e.activation(out=hb[:, s, :], in_=xs[:], func=AF.Square,
                                                                       accum_out=ss[:, s:s + 1]),
                            reads=[bxs], writes=[bhb[s], bss[s]])
                    sc.emit("act", lambda e, s=s: e.activation(out=rstd[:, s:s + 1], in_=ss[:, s:s + 1], func=AF.Sqrt,
                                                                scale=1.0 / D, bias=eps_c[:]),
                            reads=[bss[s], b_eps], writes=[brs[s]])
                    sc.emit("dve", lambda e, s=s: e.reciprocal(out=rstd[:, s:s + 1], in_=rstd[:, s:s + 1]),
                            reads=[brs[s]], writes=[brs[s]])
                    sc.emit("dve", lambda e, s=s, xs=xs: e.scalar_tensor_tensor(
                        out=hb[:, s, :], in0=xs[:], scalar=rstd[:, s:s + 1], in1=g2[:], op0=ALU.mult, op1=ALU.mult),
                        reads=[bxs, brs[s], bg2], writes=[bhb[s]])
                transpose_tile(hb, bhb, hT, bhT)
                staged[j] = (hT, bhT)

            stage(0)
            for j in range(NT):
                hT, bhT = staged.pop(j)
                for f in range(FC):
                    pg, bpg = pf.next()
                    sc.mm_group(bpg, [(lambda e, k=k, f=f, pg=pg, hT=hT: e.matmul(
                        pg[:], lhsT=wfi[:, k, f * 128:(f + 1) * 128], rhs=hT[:, k, :],
                        start=(k == 0), stop=(k == KC - 1)), [bhT[k // 2], bwfi[f]]) for k in range(KC)])
                    pu, bpu = pf.next()
                    sc.mm_group(bpu, [(lambda e, k=k, f=f, pu=pu, hT=hT: e.matmul(
                        pu[:], lhsT=wfi[:, k, DFF + f * 128: DFF + (f + 1) * 128], rhs=hT[:, k, :],
                        start=(k == 0), stop=(k == KC - 1)), [bhT[k // 2], bwfi[FC + f]]) for k in range(KC)])
                    sg, bsg = sgs.next()
                    sc.emit("act", lambda e, pg=pg, sg=sg: e.activation(out=sg[:], in_=pg[:], func=AF.Silu),
                            reads=[bpg], writes=[bsg])
                    sc.emit("dve", lambda e, f=f, pu=pu, sg=sg: e.tensor_mul(out=ffT[:, f, :], in0=pu[:], in1=sg[:]),
                            reads=[bpu, bsg], writes=[bff[f]])
                    if f == FC // 2 and j + 1 < NT:
                        stage(j + 1)
                for s in range(NS):
                    xr, bxr = xrs4.next()
                    sc.emit("sp", lambda e, j=j, s=s, xr=xr: e.dma_start(
                        out=xr[:], in_=XM[j * TT + s * 128: j * TT + (s + 1) * 128, :]),
                        reads=[bXM[j]], writes=[bxr], dma=True)
                    for nb in range(2):
                        po, bpo = pf.next()
                        sc.mm_group(bpo, [(lambda e, f=f, s=s, nb=nb, po=po: e.matmul(
                            po[:], lhsT=ffT[:, f, s * 128:(s + 1) * 128], rhs=wfo[:, f, nb * 512:(nb + 1) * 512],
                            start=(f == 0), stop=(f == FC - 1)), [bff[f], bwfo]) for f in range(FC)])
                        sc.emit("dve", lambda e, nb=nb, po=po, xr=xr: e.tensor_add(
                            out=xr[:, nb * 512:(nb + 1) * 512], in0=po[:], in1=xr[:, nb * 512:(nb + 1) * 512]),
                            reads=[bpo, bxr], writes=[bxr])
                    if last and final_norm:
                        fj, bfj = sgs.next()
                        fs, bfs = Buf(), Buf()
                        sc.emit("act", lambda e, xr=xr, fj=fj: e.activation(
                            out=fj[:].rearrange("p (a t) -> p a t", a=1)[:, 0, :], in_=xr[:, 0:TT], func=AF.Square,
                            accum_out=fss[:, 0:1]), reads=[bxr], writes=[bfj, bfss[0]])
                        sc.emit("act", lambda e, xr=xr, fj=fj: e.activation(
                            out=fj[:], in_=xr[:, TT:D], func=AF.Square, accum_out=fss[:, 1:2]),
                            reads=[bxr], writes=[bfj, bfss[1]])
                        sc.emit("dve", lambda e: e.tensor_add(out=fss[:, 2:3], in0=fss[:, 0:1], in1=fss[:, 1:2]),
                                reads=bfss[0:2], writes=[bfss[2]])
                        sc.emit("act", lambda e: e.activation(out=fss[:, 3:4], in_=fss[:, 2:3], func=AF.Sqrt,
                                                              scale=1.0 / D, bias=eps_c[:]),
                                reads=[bfss[2], b_eps], writes=[bfss[3]])
                        sc.emit("dve", lambda e: e.reciprocal(out=fss[:, 3:4], in_=fss[:, 3:4]),
                                reads=[bfss[3]], writes=[bfss[3]])
                        sc.emit("dve", lambda e, xr=xr: e.scalar_tensor_tensor(
                            out=xr[:], in0=xr[:], scalar=fss[:, 3:4], in1=gf[:], op0=ALU.mult, op1=ALU.mult),
                            reads=[bxr, bfss[3], bgf], writes=[bxr])
                    sc.emit("sp", lambda e, j=j, s=s, xr=xr: e.dma_start(
                        out=dst[j * TT + s * 128: j * TT + (s + 1) * 128, :], in_=xr[:]),
                        reads=[bxr], writes=([bdst[j]] if bdst is not None else []), dma=True)
          sc.barrier()

    for li, l in enumerate(layers):
        emit_layer(li, l)
    sc.finish()
    with nc.Block() as block:
        sc.replay(block)


LAUNCH_PLAN = [((0, 1), True)]


def kernel(**inputs):
    x = np.ascontiguousarray(inputs["x"], dtype=np.float32)
    wnames = [k for k in inputs if k != "x"]
    wts = {k: np.ascontiguousarray(inputs[k], dtype=np.float32) for k in wnames}
    cur = [x[c] for c in range(N_CORES)]
    for layers, fn in LAUNCH_PLAN:
        nc = build_program(layers=layers, final_norm=fn)
        in_maps = []
        for c in range(N_CORES):
            m = {"x": cur[c]}
            m.update(wts)
            in_maps.append(m)
        res = run_bass_kernel_spmd(nc, in_maps, core_ids=list(range(N_CORES)))
        cur = [res.results[c]["out"] for c in range(N_CORES)]
    return np.stack(cur, axis=0)
```

```python
import contextlib
import numpy as np
import concourse.bass as bass
import concourse.mybir as mybir
from concourse.bass_utils import run_bass_kernel_spmd

F32 = mybir.dt.float32
BF16 = mybir.dt.bfloat16
AF = mybir.ActivationFunctionType
ALU = mybir.AluOpType

D = 1024
S = 4096
DEPTH = 2
KC = 8
DFF = 2816
FC = 22
NIN = 6144
TT = 512
NT = S // TT
NS = TT // 128
EPS = 1e-6
N_CORES = 8


def _snap(fn):
    out = []
    for c in (fn.__closure__ or ()):
        try:
            out.append(c.cell_contents)
        except ValueError:
            out.append(None)
    return out


def _check_snap(fn, snap):
    for name, c, old in zip(fn.__code__.co_freevars, fn.__closure__ or (), snap):
        new = c.cell_contents
        if new is not old and not (isinstance(new, (int, float, str, tuple)) and new == old):
            raise RuntimeError(f"late-binding bug: closure var {name!r} changed between emit and replay "
                               f"(line {fn.__code__.co_firstlineno})")


class Buf:
    __slots__ = ("name", "w", "r")

    def __init__(self, name=""):
        self.name = name
        self.w = None
        self.r = []


class _Eng:
    def __init__(self, name):
        self.name = name
        self.ops = []
        self.seen = {}
        self.semkey = None
        self.count = 0
        self.slots = []
        self.slot_i = 0


class Sched:
    ENGS = ("pe", "act", "dve", "pool", "sp")

    def __init__(self, nc, stack):
        self.nc = nc
        self.stack = stack
        self.sems = {}
        self.eng = {e: _Eng(e) for e in self.ENGS}
        self.nsem = 0
        for e in self.ENGS:
            self._new_eng_sem(self.eng[e])
        for e, n in (("sp", 12), ("pool", 8), ("act", 4)):
            for i in range(n):
                k = self._new_sem(f"d_{e}{i}")
                self.eng[e].slots.append([k, 0])

    def _new_sem(self, name):
        h = self.stack.enter_context(self.nc.semaphore(f"{name}_{self.nsem}"))
        self.nsem += 1
        self.sems[name + str(self.nsem)] = h
        return name + str(self.nsem)

    def _new_eng_sem(self, E):
        E.semkey = self._new_sem("e_" + E.name)
        E.count = 0

    def emit(self, eng, fn, reads=(), writes=(), dma=False, inc=True):
        E = self.eng[eng]
        waits = {}

        def need(ev):
            if ev is None:
                return
            k, v = ev
            if waits.get(k, 0) < v:
                waits[k] = v

        for b in reads:
            need(b.w)
        for b in writes:
            need(b.w)
            for ev in b.r:
                need(ev)
        final = []
        for k, v in waits.items():
            if E.seen.get(k, 0) >= v:
                continue
            E.seen[k] = v
            final.append((k, v))
        if dma:
            slot = E.slots[E.slot_i]
            E.slot_i = (E.slot_i + 1) % len(E.slots)
            k = slot[0]
            if slot[1] > 0 and E.seen.get(k, 0) < slot[1]:
                final.append((k, slot[1]))
                E.seen[k] = slot[1]
            if slot[1] + 16 > 65000:
                slot[0] = self._new_sem("d_" + E.name)
                slot[1] = 0
                k = slot[0]
            slot[1] += 16
            ev = (k, slot[1])
            incv = 16
        elif inc:
            if E.count >= 60000:
                self._new_eng_sem(E)
            E.count += 1
            ev = (E.semkey, E.count)
            incv = 1
        else:
            ev = None
            incv = 0
        E.ops.append((fn, final, ev, incv, _snap(fn)))
        if ev is not None:
            for b in writes:
                b.w = ev
                b.r = []
            for b in reads:
                b.r.append(ev)
        return ev

    def mm_group(self, bank, mms, extra_reads=()):
        n = len(mms)
        allreads = list(extra_reads)
        for i, (fn, rb) in enumerate(mms):
            for b in rb:
                if b not in allreads:
                    allreads.append(b)
            last = i == n - 1
            self.emit("pe", fn, reads=(allreads if last else rb),
                      writes=[bank] if (i == 0 or last) else (), inc=last)

    def barrier(self):
        final = []
        for e in self.ENGS:
            for k, c in self.eng[e].slots:
                if c > 0:
                    final.append((k, c))
            if self.eng[e].count > 0:
                final.append((self.eng[e].semkey, self.eng[e].count))
        for e in self.ENGS:
            E = self.eng[e]
            w = []
            for k, v in final:
                if k == E.semkey and e != "sp" and False:
                    continue
                if E.seen.get(k, 0) >= v:
                    continue
                E.seen[k] = v
                w.append((k, v))
            if w:
                E.ops.append((None, w, None, 0, None))

    def finish(self):
        E = self.eng["sp"]
        final = []
        for e in self.ENGS:
            for k, c in self.eng[e].slots:
                if c > 0:
                    final.append((k, c))
            if self.eng[e].count > 0 and e != "sp":
                final.append((self.eng[e].semkey, self.eng[e].count))
        E.ops.append((None, final, None, 0, None))

    def replay(self, block):
        nc = self.nc
        handles = {"pe": (block.tensor, nc.tensor), "act": (block.scalar, nc.scalar),
                   "dve": (block.vector, nc.vector), "pool": (block.gpsimd, nc.gpsimd),
                   "sp": (block.sync, nc.sync)}
        sems = self.sems
        for e in self.ENGS:
            ops = self.eng[e].ops
            deco, _ = handles[e]

            def body(eng, ops=ops):
                for fn, waits, ev, incv, snap in ops:
                    for k, v in waits:
                        eng.wait_ge(sems[k], v)
                    if fn is None:
                        continue
                    _check_snap(fn, snap)
                    ins = fn(eng)
                    if ev is not None:
                        ins.then_inc(sems[ev[0]], incv)
            deco(body)


class Pool_:
    def __init__(self, tiles, nsub=0):
        self.tiles = tiles
        self.nsub = nsub
        self.bufs = [([Buf() for _ in range(nsub)] if nsub else Buf()) for _ in tiles]
        self.i = 0

    def next(self):
        t, b = self.tiles[self.i], self.bufs[self.i]
        self.i = (self.i + 1) % len(self.tiles)
        return t, b


def build_program(layers=(0, 1), final_norm=True, debug=None, phases=(1, 2, 3, 4)):
    debug = debug or set()
    nc = bass.Bass("TRN2", target_bir_lowering=False)
    stack = contextlib.ExitStack()
    with stack:
        _build(nc, stack, layers, final_norm, debug, phases)
    return nc


def _dram(nc, name, shape, dt, kind):
    return nc.dram_tensor(name, list(shape), dt, kind=kind).ap()


def _build(nc, stack, layers, final_norm, debug, phases):
    L = DEPTH
    x_in = _dram(nc, "x", [S, D], F32, "ExternalInput")
    out = _dram(nc, "out", [S, D], F32, "ExternalOutput")
    W = {}
    for name, shape in (
        ("norm1_g", [L, D]), ("w_in", [L, D, NIN]), ("gmlp_ln_g", [L, D]), ("gmlp_ln_b", [L, D]),
        ("gmlp_w_s", [L, 8, 128, 128]), ("gmlp_b_s", [L, 8, 128]), ("conv_w", [L, 4, D]),
        ("conv_b", [L, D]), ("lru_w_r", [L, 2, 8, 128, 128]), ("lru_b_r", [L, 2, D]),
        ("lru_w_i", [L, 2, 8, 128, 128]), ("lru_b_i", [L, 2, D]), ("lru_lambda", [L, 2, D]),
        ("w_out", [L, D, D]), ("norm2_g", [L, D]), ("w_ffn_in", [L, D, 2 * DFF]),
        ("w_ffn_out", [L, DFF, D]), ("final_g", [D]),
    ):
        W[name] = _dram(nc, name, shape, F32, "ExternalInput")

    def scratch(name, shape, dt):
        kind = "ExternalOutput" if name in debug else "Internal"
        return _dram(nc, name, shape, dt, kind)

    HT = scratch("HT", [128, KC, S], BF16)
    XC = scratch("XC", [KC, 128, S], F32)
    XCB = scratch("XCB", [KC, 128, S], BF16)
    HS = scratch("HS", [KC, 128, S], F32)
    XM = scratch("XM", [S, D], F32)
    XL = scratch("XL", [S, D], F32)

    sc = Sched(nc, stack)
    sb = lambda name, shape, dt: stack.enter_context(nc.sbuf_tensor(name, list(shape), dt))
    ps = lambda name, shape, dt: stack.enter_context(nc.psum_tensor(name, list(shape), dt))

    pfall = ps("pfall", [128, 6 * 512], F32)
    pf = Pool_([pfall[:, i * 512:(i + 1) * 512] for i in range(6)])
    pf2 = Pool_([pfall[:, i * 1024:(i + 1) * 1024] for i in range(3)])
    pb = Pool_([ps(f"pb{i}", [128, 1024], BF16) for i in range(2)])

    ident_f = sb("ident_f", [128, 128], F32)
    ident_b = sb("ident_b", [128, 128], BF16)
    eps_c = sb("eps_c", [128, 1], F32)
    one_c = sb("one_c", [128, 1], F32)
    b_ident_f, b_ident_b, b_eps, b_one = Buf(), Buf(), Buf(), Buf()

    sc.emit("pool", lambda e: e.memset(ident_f[:], 0.0), writes=[b_ident_f])
    sc.emit("pool", lambda e: e.affine_select(out=ident_f[:], in_=ident_f[:], compare_op=ALU.not_equal, fill=1.0,
                                              base=0, pattern=[[-1, 128]], channel_multiplier=1),
            reads=[b_ident_f], writes=[b_ident_f])
    sc.emit("pool", lambda e: e.tensor_copy(out=ident_b[:], in_=ident_f[:]), reads=[b_ident_f], writes=[b_ident_b])
    sc.emit("pool", lambda e: e.memset(eps_c[:], EPS), writes=[b_eps])
    sc.emit("pool", lambda e: e.memset(one_c[:], 1.0), writes=[b_one])

    bHT = [Buf() for _ in range(NT)]
    bXC = [Buf() for _ in range(NT + 1)]
    bXCB = [Buf() for _ in range(NT + 1)]
    bHS = [Buf() for _ in range(KC)]
    bXM = [Buf() for _ in range(NT)]
    bXL = [Buf() for _ in range(NT)]

    def wload(dst, src, buf):
        sc.emit("pool", lambda e: e.dma_start(out=dst, in_=src), writes=[buf], dma=True)

    def load_bcast(dst, src_row, buf):
        sc.emit("sp", lambda e: e.dma_start(out=dst, in_=src_row.partition_broadcast(128)), writes=[buf], dma=True)

    def rmsnorm_tile(xt, bxt, g_bc, bg, hb, bhb, ss, bss, rstd, brstd):
        for s in range(NS):
            sc.emit("act", lambda e, s=s: e.activation(out=hb[:, s, :], in_=xt[:, s, :], func=AF.Square,
                                                        accum_out=ss[:, s:s + 1]),
                    reads=[bxt[s]], writes=[bhb[s], bss[s]])
        sc.emit("act", lambda e: e.activation(out=rstd[:], in_=ss[:], func=AF.Sqrt, scale=1.0 / D, bias=eps_c[:]),
                reads=bss + [b_eps], writes=[brstd])
        sc.emit("dve", lambda e: e.reciprocal(out=rstd[:], in_=rstd[:]), reads=[brstd], writes=[brstd])
        for s in range(NS):
            sc.emit("dve", lambda e, s=s: e.scalar_tensor_tensor(out=hb[:, s, :], in0=xt[:, s, :], scalar=rstd[:, s:s + 1],
                                                                  in1=g_bc[:], op0=ALU.mult, op1=ALU.mult),
                    reads=[bxt[s], brstd, bg], writes=[bhb[s]])

    def transpose_tile(hb, bhb, hT, bhT):
        for k2 in range(KC // 2):
            pt, bpt = pb.next()
            mms = []
            for kk in range(2):
                k = 2 * k2 + kk
                for s in range(NS):
                    mms.append((lambda e, k=k, kk=kk, s=s, pt=pt: e.transpose(
                        out=pt[:, kk * 512 + s * 128: kk * 512 + (s + 1) * 128],
                        in_=hb[:, s, k * 128:(k + 1) * 128], identity=ident_b[:]), [bhb[s], b_ident_b]))
            sc.mm_group(bpt, mms)
            sc.emit("act", lambda e, k2=k2, pt=pt: e.copy(out=hT[:, 2 * k2:2 * k2 + 2, :],
                                                          in_=pt[:].rearrange("p (a t) -> p a t", a=2)),
                    reads=[bpt], writes=[bhT[k2]])

    uniq = []

    def emit_layer(li, l):
        x_src, bx_src = (x_in, None) if li == 0 else (XL, bXL)
        last = li == len(layers) - 1
        lstack = contextlib.ExitStack()
        lsb = lambda name, shape, dt: lstack.enter_context(nc.sbuf_tensor(f"{name}_L{l}", list(shape), dt))
        vt = lsb("vt", [128, 128], F32)
        cols = lsb("cols", [128, 128], F32)
        bvt, bcols = Buf(), Buf()
        sc.emit("pool", lambda e: e.memset(vt[:], 0.0), writes=[bvt])
        sc.emit("sp", lambda e: e.dma_start(out=vt[0:32, :], in_=W["conv_w"][l].rearrange("k (c p) -> (k c) p", p=128)),
                writes=[bvt], dma=True)
        sc.emit("sp", lambda e: e.dma_start(out=vt[32:40, :], in_=W["conv_b"][l].rearrange("(c p) -> c p", p=128)),
                writes=[bvt], dma=True)
        for base, nm in ((40, "lru_b_r"), (56, "lru_b_i"), (72, "lru_lambda")):
            sc.emit("sp", lambda e, base=base, nm=nm: e.dma_start(
                out=vt[base:base + 16, :], in_=W[nm][l].rearrange("d (c p) -> (d c) p", p=128)),
                writes=[bvt], dma=True)
        pz, bpz = pf.next()
        sc.mm_group(bpz, [(lambda e, pz=pz: e.transpose(out=pz[:, 0:128], in_=vt[:], identity=ident_f[:]),
                           [bvt, b_ident_f])])
        sc.emit("dve", lambda e, pz=pz: e.tensor_copy(out=cols[:], in_=pz[:, 0:128]), reads=[bpz], writes=[bcols])
        tmp = lsb("tmpc", [128, 4, 16], F32)
        btmp = Buf()
        cneg = lsb("cneg", [128, 16], F32)
        bcneg = Buf()
        ee, zz, z2, pp = tmp[:, 0, :], tmp[:, 1, :], tmp[:, 2, :], tmp[:, 3, :]
        sc.emit("act", lambda e: e.activation(out=ee, in_=cols[:, 72:88], func=AF.Exp, scale=-1.0),
                reads=[bcols], writes=[btmp])
        dv = lambda fn: sc.emit("dve", fn, reads=[btmp], writes=[btmp])
        dv(lambda e: e.tensor_scalar_add(out=zz, in0=ee, scalar1=2.0))
        dv(lambda e: e.reciprocal(out=zz, in_=zz))
        dv(lambda e: e.tensor_mul(out=zz, in0=zz, in1=ee))
        dv(lambda e: e.tensor_mul(out=z2, in0=zz, in1=zz))
        dv(lambda e: e.tensor_scalar(out=pp, in0=z2, scalar1=1.0 / 11.0, scalar2=1.0 / 9.0, op0=ALU.mult, op1=ALU.add))
        for cst in (1.0 / 7.0, 1.0 / 5.0, 1.0 / 3.0, 1.0):
            dv(lambda e: e.tensor_mul(out=pp, in0=pp, in1=z2))
            dv(lambda e, cst=cst: e.tensor_scalar_add(out=pp, in0=pp, scalar1=cst))
        dv(lambda e: e.tensor_mul(out=pp, in0=pp, in1=zz))
        sc.emit("dve", lambda e: e.tensor_scalar_mul(out=cneg[:], in0=pp, scalar1=-16.0), reads=[btmp], writes=[bcneg])

        hbias = lsb("hbias", [128, 32], F32)
        hcneg = lsb("hcneg", [128, 16], F32)
        bhbias, bhcneg = Buf(), Buf()
        sc.emit("dve", lambda e: e.tensor_scalar_mul(out=hbias[:], in0=cols[:, 40:72], scalar1=0.5),
                reads=[bcols], writes=[bhbias])
        sc.emit("dve", lambda e: e.tensor_scalar_mul(out=hcneg[:], in0=cneg[:], scalar1=0.5),
                reads=[bcneg], writes=[bhcneg])

        if 1 in phases:
          with contextlib.ExitStack() as pstack:
            psb = lambda name, shape, dt: pstack.enter_context(nc.sbuf_tensor(f"{name}_L{l}P{len(uniq)}", list(shape), dt))
            uniq.append(0)
            wzx = psb("wzx", [128, KC, D], BF16)
            bwzx = Buf()
            wload(wzx[:], W["w_in"][l, :, 2048:3072].rearrange("(kc p) n -> p kc n", p=128), bwzx)
            g1 = psb("g1", [128, D], F32)
            bg1 = Buf()
            load_bcast(g1[:], W["norm1_g"][l:l + 1, :], bg1)
            xts = Pool_([psb(f"xt{i}", [128, NS, D], F32) for i in range(2)], nsub=NS)
            hbs = Pool_([psb(f"hb{i}", [128, NS, D], BF16) for i in range(2)], nsub=NS)
            hTs = Pool_([psb(f"hT{i}", [128, KC, TT], BF16) for i in range(2)], nsub=KC // 2)
            zxbs = Pool_([psb(f"zxb{i}", [128, KC, TT + 3], F32) for i in range(2)], nsub=KC + 1)
            xcos = Pool_([psb(f"xco{i}", [128, KC, TT], F32) for i in range(2)], nsub=KC)
            xcbos = Pool_([psb(f"xcbo{i}", [128, KC, TT], BF16) for i in range(2)])
            ss = psb("ss", [128, NS], F32)
            rstd = psb("rstd", [128, NS], F32)
            bss = [Buf() for _ in range(NS)]
            brstd = Buf()
            xloads = {}
            prev = None

            def xload(j):
                xt, bxt = xts.next()
                rd = [bx_src[j]] if bx_src is not None else []
                sc.emit("sp", lambda e, j=j, xt=xt: e.dma_start(
                    out=xt[:], in_=x_src[j * TT:(j + 1) * TT, :].rearrange("(s p) d -> p s d", p=128)),
                    reads=rd, writes=bxt, dma=True)
                xloads[j] = (xt, bxt)
            stagedA = {}

            def stageA(j):
                if j + 1 < NT:
                    xload(j + 1)
                xt, bxt = xloads.pop(j)
                hb, bhb = hbs.next()
                rmsnorm_tile(xt, bxt, g1, bg1, hb, bhb, ss, bss, rstd, brstd)
                hT, bhT = hTs.next()
                transpose_tile(hb, bhb, hT, bhT)
                sc.emit("sp", lambda e, j=j, hT=hT: e.dma_start(out=HT[:, :, j * TT:(j + 1) * TT], in_=hT[:]),
                        reads=bhT, writes=[bHT[j]], dma=True)
                stagedA[j] = (hT, bhT)

            xload(0)
            stageA(0)
            for j in range(NT):
                hT, bhT = stagedA.pop(j)
                zxb, bzxb = zxbs.next()
                if j == 0:
                    sc.emit("pool", lambda e, zxb=zxb: e.memset(zxb[:, :, 0:3], 0.0), writes=[bzxb[KC]])
                else:
                    sc.emit("pool", lambda e, zxb=zxb, pzxb=prev[0]: e.tensor_copy(out=zxb[:, :, 0:3], in_=pzxb[:, :, TT:TT + 3]),
                            reads=prev[1][0:KC], writes=[bzxb[KC]])
                xco, bxco = xcos.next()
                xcbo, bxcbo = xcbos.next()
                for c in range(KC):
                    pz, bpz = pf.next()
                    mms = []
                    for k in range(KC):
                        mms.append((lambda e, c=c, k=k, pz=pz, hT=hT: e.matmul(
                            pz[:], lhsT=wzx[:, k, c * 128:(c + 1) * 128], rhs=hT[:, k, :],
                            start=(k == 0), stop=(k == KC - 1)), [bhT[k // 2], bwzx]))
                    sc.mm_group(bpz, mms)
                    sc.emit("act", lambda e, c=c, pz=pz, zxb=zxb: e.copy(out=zxb[:, c, 3:3 + TT], in_=pz[:]),
                            reads=[bpz], writes=[bzxb[c]])
                    sc.emit("act", lambda e, c=c, pz=pz, xco=xco: e.activation(
                        out=xco[:, c, :], in_=pz[:], func=AF.Identity, scale=cols[:, 24 + c:25 + c],
                        bias=cols[:, 32 + c:33 + c]), reads=[bpz, bcols], writes=[bxco[c]])
                for c2 in range(KC // 2):
                    if c2 == KC // 4 and j + 1 < NT:
                        stageA(j + 1)
                    for k in range(3):
                        for c in (2 * c2, 2 * c2 + 1):
                            sc.emit("dve", lambda e, c=c, k=k, zxb=zxb, xco=xco: e.scalar_tensor_tensor(
                                out=xco[:, c, :], in0=zxb[:, c, k:k + TT], scalar=cols[:, k * 8 + c:k * 8 + c + 1],
                                in1=xco[:, c, :], op0=ALU.mult, op1=ALU.add),
                                reads=[bzxb[c], bzxb[KC], bcols, bxco[c]], writes=[bxco[c]])
                for c in range(KC):
                    sc.emit("act", lambda e, c=c, xco=xco, xcbo=xcbo: e.copy(out=xcbo[:, c, :], in_=xco[:, c, :]),
                            reads=[bxco[c]], writes=[bxcbo])
                if j == 0:
                    sc.emit("sp", lambda e, xco=xco: e.dma_start(
                        out=XC[:, :, 0:TT - 2].rearrange("c p t -> p c t"), in_=xco[:, :, 2:TT]),
                        reads=bxco, writes=[bXC[j]], dma=True)
                    sc.emit("sp", lambda e, xcbo=xcbo: e.dma_start(
                        out=XCB[:, :, 0:TT - 2].rearrange("c p t -> p c t"), in_=xcbo[:, :, 2:TT]),
                        reads=[bxcbo], writes=[bXCB[j]], dma=True)
                else:
                    sc.emit("sp", lambda e, j=j, xco=xco: e.dma_start(
                        out=XC[:, :, j * TT - 2:(j + 1) * TT - 2].rearrange("c p t -> p c t"), in_=xco[:]),
                        reads=bxco, writes=[bXC[j]], dma=True)
                    sc.emit("sp", lambda e, j=j, xcbo=xcbo: e.dma_start(
                        out=XCB[:, :, j * TT - 2:(j + 1) * TT - 2].rearrange("c p t -> p c t"), in_=xcbo[:]),
                        reads=[bxcbo], writes=[bXCB[j]], dma=True)
                prev = (zxb, bzxb)
            zxb, bzxb = zxbs.next()
            sc.emit("pool", lambda e, zxb=zxb, pzxb=prev[0]: e.tensor_copy(out=zxb[:, :, 0:3], in_=pzxb[:, :, TT:TT + 3]),
                    reads=prev[1][0:KC], writes=[bzxb[KC]])
            sc.emit("pool", lambda e, zxb=zxb: e.memset(zxb[:, :, 3:5], 0.0), writes=bzxb[0:KC])
            xco, bxco = xcos.next()
            for c in range(KC):
                sc.emit("dve", lambda e, c=c, zxb=zxb, xco=xco: e.tensor_scalar(
                    out=xco[:, c, 0:2], in0=zxb[:, c, 0:2], scalar1=cols[:, c:c + 1], scalar2=cols[:, 32 + c:33 + c],
                    op0=ALU.mult, op1=ALU.add), reads=[bzxb[c], bzxb[KC], bcols], writes=[bxco[c]])
                for k in range(1, 4):
                    sc.emit("dve", lambda e, c=c, k=k, zxb=zxb, xco=xco: e.scalar_tensor_tensor(
                        out=xco[:, c, 0:2], in0=zxb[:, c, k:k + 2], scalar=cols[:, k * 8 + c:k * 8 + c + 1],
                        in1=xco[:, c, 0:2], op0=ALU.mult, op1=ALU.add), reads=[bzxb[c], bzxb[KC], bcols, bxco[c]], writes=[bxco[c]])
            sc.emit("sp", lambda e, xco=xco: e.dma_start(
                out=XC[:, :, S - 2:S].rearrange("c p t -> p c t"), in_=xco[:, :, 0:2]),
                reads=bxco, writes=[bXC[NT]], dma=True)
            xcbo, bxcbo = xcbos.next()
            sc.emit("act", lambda e, xco=xco, xcbo=xcbo: e.copy(out=xcbo[:, :, 0:2], in_=xco[:, :, 0:2]),
                    reads=bxco, writes=[bxcbo])
            sc.emit("sp", lambda e, xcbo=xcbo: e.dma_start(
                out=XCB[:, :, S - 2:S].rearrange("c p t -> p c t"), in_=xcbo[:, :, 0:2]),
                reads=[bxcbo], writes=[bXCB[NT]], dma=True)
          sc.barrier()

        if 2 in phases:
          with contextlib.ExitStack() as pstack:
            psb = lambda name, shape, dt: pstack.enter_context(nc.sbuf_tensor(f"{name}_L{l}P{len(uniq)}", list(shape), dt))
            uniq.append(0)
            wg = psb("wg", [128, 2, 2, KC, 128], BF16)
            bwg = Buf()
            for t, nm in ((0, "lru_w_r"), (1, "lru_w_i")):
                for d in range(2):
                    wload(wg[:, t, d, :, :], W[nm][l, d].rearrange("h c o -> c h o"), bwg)
            NB = S // 512
            xcs = Pool_([psb(f"xc{i}", [128, S], F32) for i in range(2)], nsub=NB)
            xcbs = Pool_([psb(f"xcb{i}", [128, S], BF16) for i in range(2)], nsub=NB)
            ra = [psb(f"ra{d}", [128, S], F32) for d in range(2)]
            ib = [psb(f"ib{d}", [128, S], F32) for d in range(2)]
            mh = [psb(f"mh{d}", [128, S], F32) for d in range(2)]
            bra, bib, bmh = [Buf(), Buf()], [Buf(), Buf()], [Buf(), Buf()]
            hso = psb("hso", [128, S], F32)
            bhso = Buf()

            def conv_stage(c):
                xc, bxc = xcs.next()
                xcb, bxcb = xcbs.next()
                sc.emit("sp", lambda e, c=c, xcb=xcb: e.dma_start(out=xcb[:], in_=XCB[c]), reads=bXCB, writes=bxcb, dma=True)
                sc.emit("sp", lambda e, c=c, xc=xc: e.dma_start(out=xc[:], in_=XC[c]), reads=bXC, writes=bxc, dma=True)
                return xc, bxc, xcb, bxcb

            staged = {0: conv_stage(0)}
            for c in range(KC):
                xc, bxc, xcb, bxcb = staged.pop(c)
                for d in range(2):
                    for t in range(2):
                        dst, bdst = (ra[d], bra[d]) if t == 0 else (ib[d], bib[d])
                        bcol = (0 if t == 0 else 16) + d * 8 + c
                        for tb2 in range(NB // 2):
                            pz, bpz = pf2.next()
                            sc.mm_group(bpz, [(lambda e, t=t, d=d, c=c, tb=2 * tb2 + h, h=h, pz=pz, xcb=xcb: e.matmul(
                                pz[:, h * 512:(h + 1) * 512], lhsT=wg[:, t, d, c, :], rhs=xcb[:, tb * 512:(tb + 1) * 512],
                                start=True, stop=True), [bwg, bxcb[2 * tb2 + h]]) for h in range(2)])
                            sc.emit("act", lambda e, dst=dst, tb2=tb2, pz=pz, bcol=bcol: e.activation(
                                out=dst[:, tb2 * 1024:(tb2 + 1) * 1024], in_=pz[:], func=AF.Tanh, scale=0.5,
                                bias=hbias[:, bcol:bcol + 1]), reads=[bpz, bhbias], writes=[bdst])
                    sc.emit("act", lambda e, d=d, c=c: e.activation(
                        out=mh[d][:], in_=ra[d][:], func=AF.Exp, scale=cneg[:, d * 8 + c:d * 8 + c + 1],
                        bias=cneg[:, d * 8 + c:d * 8 + c + 1]), reads=[bra[d], bcneg], writes=[bmh[d]])
                    sc.emit("act", lambda e, d=d, c=c: e.activation(
                        out=ra[d][:], in_=ra[d][:], func=AF.Exp, scale=hcneg[:, d * 8 + c:d * 8 + c + 1],
                        bias=hcneg[:, d * 8 + c:d * 8 + c + 1]), reads=[bra[d], bhcneg], writes=[bra[d]])
                    sc.emit("act", lambda e, d=d: e.activation(out=mh[d][:], in_=mh[d][:], func=AF.Sqrt,
                                                               scale=-1.0, bias=one_c[:]),
                            reads=[bmh[d], b_one], writes=[bmh[d]])
                    if d == 0 and c + 1 < KC:
                        staged[c + 1] = conv_stage(c + 1)
                for d in range(2):
                    sc.emit("dve", lambda e, d=d, xc=xc: e.scalar_tensor_tensor(
                        out=mh[d][:], in0=mh[d][:], scalar=0.5, in1=xc[:], op0=ALU.mult, op1=ALU.mult),
                        reads=[bmh[d]] + bxc, writes=[bmh[d]])
                    sc.emit("dve", lambda e, d=d: e.scalar_tensor_tensor(
                        out=ib[d][:], in0=ib[d][:], scalar=1.0, in1=mh[d][:], op0=ALU.add, op1=ALU.mult),
                        reads=[bib[d], bmh[d]], writes=[bib[d]])
                    if d == 0:
                        sc.emit("dve", lambda e: e.tensor_tensor_scan(out=hso[:], data0=ra[0][:], data1=ib[0][:],
                                                                       initial=0.0, op0=ALU.mult, op1=ALU.add),
                                reads=[bra[0], bib[0]], writes=[bhso])
                    else:
                        sc.emit("dve", lambda e: e.tensor_tensor_scan(out=mh[1][:, ::-1], data0=ra[1][:, ::-1],
                                                                       data1=ib[1][:, ::-1], initial=0.0,
                                                                       op0=ALU.mult, op1=ALU.add),
                                reads=[bra[1], bib[1], bmh[1]], writes=[bmh[1]])
                sc.emit("dve", lambda e: e.tensor_add(out=hso[:], in0=hso[:], in1=mh[1][:]),
                        reads=[bhso, bmh[1]], writes=[bhso])
                sc.emit("sp", lambda e, c=c: e.dma_start(out=HS[c], in_=hso[:]), reads=[bhso], writes=[bHS[c]], dma=True)
          sc.barrier()

        lstack.close()
        if 3 in phases:
          with contextlib.ExitStack() as pstack:
            psb = lambda name, shape, dt: pstack.enter_context(nc.sbuf_tensor(f"{name}_L{l}P{len(uniq)}", list(shape), dt))
            uniq.append(0)
            w3 = psb("w3", [128, KC, 5 * D], BF16)
            bw3 = [Buf() for _ in range(5)]
            wsrc = W["w_in"][l].rearrange("(kc p) n -> p kc n", p=128)
            for blk, c0 in ((1, 1024), (0, 0), (3, 4096), (2, 3072), (4, 5120)):
                wload(w3[:, :, blk * D:(blk + 1) * D], wsrc[:, :, c0:c0 + D], bw3[blk])
            wo = psb("wo", [128, KC, D], BF16)
            bwo = Buf()
            wload(wo[:], W["w_out"][l].rearrange("(kc p) n -> p kc n", p=128), bwo)
            wsn = psb("wsn", [128, 8, 128], F32)
            wsT = psb("wsT", [128, 8, 128], BF16)
            bwsn, bwsT = Buf(), Buf()
            sc.emit("sp", lambda e: e.dma_start(out=wsn[:], in_=W["gmlp_w_s"][l].rearrange("g p q -> p g q")),
                    writes=[bwsn], dma=True)
            for g2 in range(2):
                pz, bpz = pf.next()
                sc.mm_group(bpz, [(lambda e, g=g2 * 4 + gg, gg=gg, pz=pz: e.transpose(
                    out=pz[:, gg * 128:(gg + 1) * 128], in_=wsn[:, g, :], identity=ident_f[:]), [bwsn, b_ident_f])
                    for gg in range(4)])
                sc.emit("dve", lambda e, g2=g2, pz=pz: e.tensor_copy(
                    out=wsT[:, g2 * 4:(g2 + 1) * 4, :], in_=pz[:].rearrange("p (g q) -> p g q", g=4)),
                    reads=[bpz], writes=[bwsT])
            bsb = psb("bsb", [128, 8, 128], F32)
            lng = psb("lng", [128, D], F32)
            lnb = psb("lnb", [128, D], F32)
            bbsb, blng, blnb = Buf(), Buf(), Buf()
            load_bcast(bsb[:].rearrange("p g q -> p (g q)"), W["gmlp_b_s"][l:l + 1].rearrange("o g q -> o (g q)"), bbsb)
            load_bcast(lng[:], W["gmlp_ln_g"][l:l + 1, :], blng)
            load_bcast(lnb[:], W["gmlp_ln_b"][l:l + 1, :], blnb)

            hTs = Pool_([psb(f"hT{i}", [128, KC, TT], BF16) for i in range(2)])
            hss = Pool_([psb(f"hs{i}", [128, TT], F32) for i in range(3)])
            xss = Pool_([psb(f"xs{i}", [128, D], F32) for i in range(2)])
            vgs = Pool_([psb(f"vg{i}", [128, D], F32) for i in range(2)])
            vn = psb("vn", [128, NS, D], BF16)
            vn2 = psb("vn2", [128, NS, D], BF16)
            mT = psb("mT", [128, KC, TT], BF16)
            bmT = [Buf() for _ in range(KC)]
            tA = Pool_([psb(f"tA{i}", [128, TT], F32) for i in range(2)])
            tB = Pool_([psb(f"tB{i}", [128, TT], F32) for i in range(2)])
            tC = Pool_([psb(f"tC{i}", [128, TT], F32) for i in range(2)])
            tD = Pool_([psb(f"tD{i}", [128, TT], F32) for i in range(2)])
            st = psb("lnst", [128, NS, 2, 6], F32)
            mv = psb("lnmv", [128, NS, 2], F32)
            lrs = psb("lnrs", [128, NS], F32)
            lnm = psb("lnnm", [128, NS], F32)
            bst = [Buf() for _ in range(NS)]
            bmv = [Buf() for _ in range(NS)]
            blrs, blnm = Buf(), Buf()

            def zmm(pz, blk, c, hT, bhT):
                return [(lambda e, k=k: e.matmul(pz[:], lhsT=w3[:, k, blk * D + c * 128: blk * D + (c + 1) * 128],
                                                  rhs=hT[:, k, :], start=(k == 0), stop=(k == KC - 1)),
                         [bhT, bw3[blk]]) for k in range(KC)]

            vns = Pool_([vn, vn2], nsub=NS)
            hloads, vstate = {}, {}

            def hload(j):
                hT, bhT = hTs.next()
                sc.emit("sp", lambda e, j=j, hT=hT: e.dma_start(out=hT[:], in_=HT[:, :, j * TT:(j + 1) * TT]),
                        reads=[bHT[j]], writes=[bhT], dma=True)
                hloads[j] = (hT, bhT)

            def vpath_sub(j, s):
                hT, bhT = hloads[j]
                if s == 0:
                    vstate[j] = (vns.next(), [])
                (vnj, bvnj), vgl = vstate[j]
                vg, bvg = vgs.next()
                for nb in range(2):
                    pz, bpz = pf.next()
                    sc.mm_group(bpz, [(lambda e, k=k, s=s, nb=nb, pz=pz, hT=hT: e.matmul(
                        pz[:], lhsT=hT[:, k, s * 128:(s + 1) * 128],
                        rhs=w3[:, k, D + nb * 512: D + (nb + 1) * 512], start=(k == 0), stop=(k == KC - 1)),
                        [bhT, bw3[1]]) for k in range(KC)])
                    sc.emit("act", lambda e, nb=nb, pz=pz, vg=vg: e.activation(
                        out=vg[:, nb * 512:(nb + 1) * 512], in_=pz[:], func=AF.Gelu_apprx_tanh),
                        reads=[bpz], writes=[bvg])
                for nb in range(2):
                    sc.emit("dve", lambda e, s=s, nb=nb, vg=vg: e.bn_stats(out=st[:, s, nb, :], in_=vg[:, nb * 512:(nb + 1) * 512]),
                            reads=[bvg], writes=[bst[s]])
                sc.emit("dve", lambda e, s=s: e.bn_aggr(out=mv[:, s, :], in_=st[:, s, :, :]), reads=[bst[s]], writes=[bmv[s]])
                vgl.append((vg, bvg))
                if s % 2 == 1:
                    s0 = s - 1
                    sc.emit("act", lambda e, s0=s0: e.activation(out=lrs[:, s0:s0 + 2], in_=mv[:, s0:s0 + 2, 1], func=AF.Sqrt,
                                                                  bias=eps_c[:]),
                            reads=[bmv[s0], bmv[s0 + 1], b_eps], writes=[blrs])
                    sc.emit("dve", lambda e, s0=s0: e.reciprocal(out=lrs[:, s0:s0 + 2], in_=lrs[:, s0:s0 + 2]),
                            reads=[blrs], writes=[blrs])
                    for s1 in (s0, s0 + 1):
                        vg1, bvg1 = vgl[s1]
                        sc.emit("dve", lambda e, s1=s1, vg1=vg1: e.tensor_scalar(
                            out=vg1[:], in0=vg1[:], scalar1=mv[:, s1, 0:1], scalar2=lrs[:, s1:s1 + 1],
                            op0=ALU.subtract, op1=ALU.mult), reads=[bvg1, bmv[s1], blrs], writes=[bvg1])
                        sc.emit("dve", lambda e, vg1=vg1: e.tensor_mul(out=vg1[:], in0=vg1[:], in1=lng[:]),
                                reads=[bvg1, blng], writes=[bvg1])
                        sc.emit("dve", lambda e, s1=s1, vg1=vg1, vnj=vnj: e.tensor_add(out=vnj[:, s1, :], in0=vg1[:], in1=lnb[:]),
                                reads=[bvg1, blnb], writes=[bvnj[s1]])

            hload(0)
            for s in range(NS):
                vpath_sub(0, s)
            for j in range(NT):
                if j + 1 < NT:
                    hload(j + 1)
                hT, bhT = hloads[j]
                (vnj, bvn), _ = vstate[j]
                for c in range(KC):
                    hs, bhs = hss.next()
                    sc.emit("sp", lambda e, c=c, j=j, hs=hs: e.dma_start(out=hs[:], in_=HS[c, :, j * TT:(j + 1) * TT]),
                            reads=bHS, writes=[bhs], dma=True)
                    a_, ba = tA.next()
                    b_, bb = tB.next()
                    c_, bc = tC.next()
                    d_, bd = tD.next()
                    pu, bpu = pf.next()
                    sc.mm_group(bpu, zmm(pu, 0, c, hT, bhT))
                    sc.emit("act", lambda e, pu=pu, a_=a_: e.activation(out=a_[:], in_=pu[:], func=AF.Gelu_apprx_tanh),
                            reads=[bpu], writes=[ba])
                    pa, bpa = pf.next()
                    sc.mm_group(bpa, zmm(pa, 3, c, hT, bhT))
                    sc.emit("act", lambda e, pa=pa, b_=b_: e.activation(out=b_[:], in_=pa[:], func=AF.Tanh, scale=0.5),
                            reads=[bpa], writes=[bb])
                    pg, bpg = pf.next()
                    sc.mm_group(bpg, zmm(pg, 2, c, hT, bhT))
                    sc.emit("act", lambda e, pg=pg, c_=c_: e.activation(out=c_[:], in_=pg[:], func=AF.Gelu_apprx_tanh),
                            reads=[bpg], writes=[bc])
                    pB, bpB = pf.next()
                    sc.mm_group(bpB, zmm(pB, 4, c, hT, bhT))
                    sc.emit("act", lambda e, pB=pB, d_=d_: e.activation(out=d_[:], in_=pB[:], func=AF.Tanh, scale=0.5),
                            reads=[bpB], writes=[bd])
                    pm, bpm = pf.next()
                    sc.mm_group(bpm, [(lambda e, s=s, c=c, pm=pm, vnj=vnj: e.matmul(
                        pm[:, s * 128:(s + 1) * 128], lhsT=vnj[:, s, c * 128:(c + 1) * 128], rhs=wsT[:, c, :],
                        start=True, stop=True), [bvn[s], bwsT]) for s in range(NS)])
                    sc.emit("dve", lambda e, a_=a_, b_=b_: e.scalar_tensor_tensor(
                        out=a_[:], in0=b_[:], scalar=1.0, in1=a_[:], op0=ALU.add, op1=ALU.mult),
                        reads=[ba, bb], writes=[ba])
                    sc.emit("dve", lambda e, c=c, pm=pm, b_=b_: e.tensor_tensor(
                        out=b_[:].rearrange("p (s q) -> p s q", s=NS), in0=pm[:].rearrange("p (s q) -> p s q", s=NS),
                        in1=bsb[:, c:c + 1, :].to_broadcast([128, NS, 128]), op=ALU.add),
                        reads=[bpm, bbsb, bb], writes=[bb])
                    sc.emit("dve", lambda e, a_=a_, b_=b_: e.tensor_mul(out=a_[:], in0=a_[:], in1=b_[:]),
                            reads=[ba, bb], writes=[ba])
                    sc.emit("dve", lambda e, c_=c_, d_=d_: e.scalar_tensor_tensor(
                        out=c_[:], in0=d_[:], scalar=1.0, in1=c_[:], op0=ALU.add, op1=ALU.mult),
                        reads=[bc, bd], writes=[bc])
                    sc.emit("pool", lambda e, c_=c_, hs=hs: e.tensor_mul(out=c_[:], in0=c_[:], in1=hs[:]),
                            reads=[bc, bhs], writes=[bc])
                    sc.emit("dve", lambda e, c=c, a_=a_, c_=c_: e.tensor_add(out=mT[:, c, :], in0=a_[:], in1=c_[:]),
                            reads=[ba, bc], writes=[bmT[c]])
                    if j + 1 < NT and c % 2 == 1:
                        vpath_sub(j + 1, c // 2)
                for s in range(NS):
                    xs, bxs = xss.next()
                    rd = [bx_src[j]] if bx_src is not None else []
                    sc.emit("sp", lambda e, j=j, s=s, xs=xs: e.dma_start(
                        out=xs[:], in_=x_src[j * TT + s * 128: j * TT + (s + 1) * 128, :]),
                        reads=rd, writes=[bxs], dma=True)
                    for nb in range(2):
                        po, bpo = pf.next()
                        sc.mm_group(bpo, [(lambda e, k=k, s=s, nb=nb, po=po: e.matmul(
                            po[:], lhsT=mT[:, k, s * 128:(s + 1) * 128], rhs=wo[:, k, nb * 512:(nb + 1) * 512],
                            start=(k == 0), stop=(k == KC - 1)), [bmT[k], bwo]) for k in range(KC)])
                        sc.emit("dve", lambda e, nb=nb, po=po, xs=xs: e.scalar_tensor_tensor(
                            out=xs[:, nb * 512:(nb + 1) * 512], in0=po[:], scalar=0.5, in1=xs[:, nb * 512:(nb + 1) * 512],
                            op0=ALU.mult, op1=ALU.add),
                            reads=[bpo, bxs], writes=[bxs])
                    sc.emit("sp", lambda e, j=j, s=s, xs=xs: e.dma_start(
                        out=XM[j * TT + s * 128: j * TT + (s + 1) * 128, :], in_=xs[:]),
                        reads=[bxs], writes=[bXM[j]], dma=True)
                hloads.pop(j)
                vstate.pop(j)
          sc.barrier()

        if 4 in phases:
          with contextlib.ExitStack() as pstack:
            psb = lambda name, shape, dt: pstack.enter_context(nc.sbuf_tensor(f"{name}_L{l}P{len(uniq)}", list(shape), dt))
            uniq.append(0)
            wfi = psb("wfi", [128, KC, 2 * DFF], BF16)
            bwfi = [Buf() for _ in range(2 * FC)]
            fsrc = W["w_ffn_in"][l].rearrange("(kc p) n -> p kc n", p=128)
            for f2 in range(FC // 2):
                for half in range(2):
                    c0 = half * DFF + f2 * 256
                    sc.emit("pool", lambda e, c0=c0: e.dma_start(out=wfi[:, :, c0:c0 + 256], in_=fsrc[:, :, c0:c0 + 256]),
                            writes=[bwfi[half * FC + 2 * f2], bwfi[half * FC + 2 * f2 + 1]], dma=True)
            wfo = psb("wfo", [128, FC, D], BF16)
            bwfo = Buf()
            osrc = W["w_ffn_out"][l].rearrange("(fc p) n -> p fc n", p=128)
            for f0 in range(0, FC, 8):
                f1 = min(FC, f0 + 8)
                wload(wfo[:, f0:f1, :], osrc[:, f0:f1, :], bwfo)
            g2 = psb("g2", [128, D], F32)
            bg2 = Buf()
            load_bcast(g2[:], W["norm2_g"][l:l + 1, :], bg2)
            if last and final_norm:
                gf = psb("gf", [128, D], F32)
                bgf = Buf()
                load_bcast(gf[:], W["final_g"].rearrange("(o d) -> o d", o=1), bgf)
                fss = psb("fss", [128, 4], F32)
                bfss = [Buf() for _ in range(4)]
            xss4 = Pool_([psb(f"xs4{i}", [128, D], F32) for i in range(2)])
            xrs4 = Pool_([psb(f"xr4{i}", [128, D], F32) for i in range(2)])
            hb = psb("hb4", [128, NS, D], BF16)
            bhb = [Buf() for _ in range(NS)]
            hTs4 = Pool_([psb(f"hT4{i}", [128, KC, TT], BF16) for i in range(2)], nsub=KC // 2)
            sgs = Pool_([psb(f"sg{i}", [128, TT], F32) for i in range(2)])
            ffT = psb("ffT", [128, FC, TT], BF16)
            bff = [Buf() for _ in range(FC)]
            ss = psb("ss4", [128, NS], F32)
            rstd = psb("rstd4", [128, NS], F32)
            bss = [Buf() for _ in range(NS)]
            brs = [Buf() for _ in range(NS)]
            dst, bdst = (out, None) if last else (XL, bXL)
            staged = {}

            def stage(j):
                hT, bhT = hTs4.next()
                for s in range(NS):
                    xs, bxs = xss4.next()
                    sc.emit("sp", lambda e, j=j, s=s, xs=xs: e.dma_start(
                        out=xs[:], in_=XM[j * TT + s * 128: j * TT + (s + 1) * 128, :]),
                        reads=[bXM[j]], writes=[bxs], dma=True)
                    sc.emit("act", lambda e, s=s, xs=xs: e.activation(out=hb[:, s, :], in_=xs[:], func=AF.Square,
                                                                       accum_out=ss[:, s:s + 1]),
                            reads=[bxs], writes=[bhb[s], bss[s]])
                    sc.emit("act", lambda e, s=s: e.activation(out=rstd[:, s:s + 1], in_=ss[:, s:s + 1], func=AF.Sqrt,
                                                                scale=1.0 / D, bias=eps_c[:]),
                            reads=[bss[s], b_eps], writes=[brs[s]])
                    sc.emit("dve", lambda e, s=s: e.reciprocal(out=rstd[:, s:s + 1], in_=rstd[:, s:s + 1]),
                            reads=[brs[s]], writes=[brs[s]])
                    sc.emit("dve", lambda e, s=s, xs=xs: e.scalar_tensor_tensor(
                        out=hb[:, s, :], in0=xs[:], scalar=rstd[:, s:s + 1], in1=g2[:], op0=ALU.mult, op1=ALU.mult),
                        reads=[bxs, brs[s], bg2], writes=[bhb[s]])
                transpose_tile(hb, bhb, hT, bhT)
                staged[j] = (hT, bhT)

            stage(0)
            for j in range(NT):
                hT, bhT = staged.pop(j)
                for f in range(FC):
                    pg, bpg = pf.next()
                    sc.mm_group(bpg, [(lambda e, k=k, f=f, pg=pg, hT=hT: e.matmul(
                        pg[:], lhsT=wfi[:, k, f * 128:(f + 1) * 128], rhs=hT[:, k, :],
                        start=(k == 0), stop=(k == KC - 1)), [bhT[k // 2], bwfi[f]]) for k in range(KC)])
                    pu, bpu = pf.next()
                    sc.mm_group(bpu, [(lambda e, k=k, f=f, pu=pu, hT=hT: e.matmul(
                        pu[:], lhsT=wfi[:, k, DFF + f * 128: DFF + (f + 1) * 128], rhs=hT[:, k, :],
                        start=(k == 0), stop=(k == KC - 1)), [bhT[k // 2], bwfi[FC + f]]) for k in range(KC)])
                    sg, bsg = sgs.next()
                    sc.emit("act", lambda e, pg=pg, sg=sg: e.activation(out=sg[:], in_=pg[:], func=AF.Silu),
                            reads=[bpg], writes=[bsg])
                    sc.emit("dve", lambda e, f=f, pu=pu, sg=sg: e.tensor_mul(out=ffT[:, f, :], in0=pu[:], in1=sg[:]),
                            reads=[bpu, bsg], writes=[bff[f]])
                    if f == FC // 2 and j + 1 < NT:
                        stage(j + 1)
                for s in range(NS):
                    xr, bxr = xrs4.next()
                    sc.emit("sp", lambda e, j=j, s=s, xr=xr: e.dma_start(
                        out=xr[:], in_=XM[j * TT + s * 128: j * TT + (s + 1) * 128, :]),
                        reads=[bXM[j]], writes=[bxr], dma=True)
                    for nb in range(2):
                        po, bpo = pf.next()
                        sc.mm_group(bpo, [(lambda e, f=f, s=s, nb=nb, po=po: e.matmul(
                            po[:], lhsT=ffT[:, f, s * 128:(s + 1) * 128], rhs=wfo[:, f, nb * 512:(nb + 1) * 512],
                            start=(f == 0), stop=(f == FC - 1)), [bff[f], bwfo]) for f in range(FC)])
                        sc.emit("dve", lambda e, nb=nb, po=po, xr=xr: e.tensor_add(
                            out=xr[:, nb * 512:(nb + 1) * 512], in0=po[:], in1=xr[:, nb * 512:(nb + 1) * 512]),
                            reads=[bpo, bxr], writes=[bxr])
                    if last and final_norm:
                        fj, bfj = sgs.next()
                        fs, bfs = Buf(), Buf()
                        sc.emit("act", lambda e, xr=xr, fj=fj: e.activation(
                            out=fj[:].rearrange("p (a t) -> p a t", a=1)[:, 0, :], in_=xr[:, 0:TT], func=AF.Square,
                            accum_out=fss[:, 0:1]), reads=[bxr], writes=[bfj, bfss[0]])
                        sc.emit("act", lambda e, xr=xr, fj=fj: e.activation(
                            out=fj[:], in_=xr[:, TT:D], func=AF.Square, accum_out=fss[:, 1:2]),
                            reads=[bxr], writes=[bfj, bfss[1]])
                        sc.emit("dve", lambda e: e.tensor_add(out=fss[:, 2:3], in0=fss[:, 0:1], in1=fss[:, 1:2]),
                                reads=bfss[0:2], writes=[bfss[2]])
                        sc.emit("act", lambda e: e.activation(out=fss[:, 3:4], in_=fss[:, 2:3], func=AF.Sqrt,
                                                              scale=1.0 / D, bias=eps_c[:]),
                                reads=[bfss[2], b_eps], writes=[bfss[3]])
                        sc.emit("dve", lambda e: e.reciprocal(out=fss[:, 3:4], in_=fss[:, 3:4]),
                                reads=[bfss[3]], writes=[bfss[3]])
                        sc.emit("dve", lambda e, xr=xr: e.scalar_tensor_tensor(
                            out=xr[:], in0=xr[:], scalar=fss[:, 3:4], in1=gf[:], op0=ALU.mult, op1=ALU.mult),
                            reads=[bxr, bfss[3], bgf], writes=[bxr])
                    sc.emit("sp", lambda e, j=j, s=s, xr=xr: e.dma_start(
                        out=dst[j * TT + s * 128: j * TT + (s + 1) * 128, :], in_=xr[:]),
                        reads=[bxr], writes=([bdst[j]] if bdst is not None else []), dma=True)
          sc.barrier()

    for li, l in enumerate(layers):
        emit_layer(li, l)
    sc.finish()
    with nc.Block() as block:
        sc.replay(block)


LAUNCH_PLAN = [((0, 1), True)]


def kernel(**inputs):
    x = np.ascontiguousarray(inputs["x"], dtype=np.float32)
    wnames = [k for k in inputs if k != "x"]
    wts = {k: np.ascontiguousarray(inputs[k], dtype=np.float32) for k in wnames}
    cur = [x[c] for c in range(N_CORES)]
    for layers, fn in LAUNCH_PLAN:
        nc = build_program(layers=layers, final_norm=fn)
        in_maps = []
        for c in range(N_CORES):
            m = {"x": cur[c]}
            m.update(wts)
            in_maps.append(m)
        res = run_bass_kernel_spmd(nc, in_maps, core_ids=list(range(N_CORES)))
        cur = [res.results[c]["out"] for c in range(N_CORES)]
    return np.stack(cur, axis=0)
```

```python
import contextlib
import numpy as np
import concourse.bass as bass
import concourse.mybir as mybir
from concourse.bass_utils import run_bass_kernel_spmd

F32 = mybir.dt.float32
BF16 = mybir.dt.bfloat16
AF = mybir.ActivationFunctionType
ALU = mybir.AluOpType

D = 1024
S = 4096
DEPTH = 2
KC = 8
DFF = 2816
FC = 22
NIN = 6144
TT = 512
NT = S // TT
NS = TT // 128
EPS = 1e-6
N_CORES = 8


def _snap(fn):
    out = []
    for c in (fn.__closure__ or ()):
        try:
            out.append(c.cell_contents)
        except ValueError:
            out.append(None)
    return out


def _check_snap(fn, snap):
    for name, c, old in zip(fn.__code__.co_freevars, fn.__closure__ or (), snap):
        new = c.cell_contents
        if new is not old and not (isinstance(new, (int, float, str, tuple)) and new == old):
            raise RuntimeError(f"late-binding bug: closure var {name!r} changed between emit and replay "
                               f"(line {fn.__code__.co_firstlineno})")


class Buf:
    __slots__ = ("name", "w", "r")

    def __init__(self, name=""):
        self.name = name
        self.w = None
        self.r = []


class _Eng:
    def __init__(self, name):
        self.name = name
        self.ops = []
        self.seen = {}
        self.semkey = None
        self.count = 0
        self.slots = []
        self.slot_i = 0


class Sched:
    ENGS = ("pe", "act", "dve", "pool", "sp")

    def __init__(self, nc, stack):
        self.nc = nc
        self.stack = stack
        self.sems = {}
        self.eng = {e: _Eng(e) for e in self.ENGS}
        self.nsem = 0
        for e in self.ENGS:
            self._new_eng_sem(self.eng[e])
        for e, n in (("sp", 12), ("pool", 8), ("act", 4)):
            for i in range(n):
                k = self._new_sem(f"d_{e}{i}")
                self.eng[e].slots.append([k, 0])

    def _new_sem(self, name):
        h = self.stack.enter_context(self.nc.semaphore(f"{name}_{self.nsem}"))
        self.nsem += 1
        self.sems[name + str(self.nsem)] = h
        return name + str(self.nsem)

    def _new_eng_sem(self, E):
        E.semkey = self._new_sem("e_" + E.name)
        E.count = 0

    def emit(self, eng, fn, reads=(), writes=(), dma=False, inc=True):
        E = self.eng[eng]
        waits = {}

        def need(ev):
            if ev is None:
                return
            k, v = ev
            if waits.get(k, 0) < v:
                waits[k] = v

        for b in reads:
            need(b.w)
        for b in writes:
            need(b.w)
            for ev in b.r:
                need(ev)
        final = []
        for k, v in waits.items():
            if E.seen.get(k, 0) >= v:
                continue
            E.seen[k] = v
            final.append((k, v))
        if dma:
            slot = E.slots[E.slot_i]
            E.slot_i = (E.slot_i + 1) % len(E.slots)
            k = slot[0]
            if slot[1] > 0 and E.seen.get(k, 0) < slot[1]:
                final.append((k, slot[1]))
                E.seen[k] = slot[1]
            if slot[1] + 16 > 65000:
                slot[0] = self._new_sem("d_" + E.name)
                slot[1] = 0
                k = slot[0]
            slot[1] += 16
            ev = (k, slot[1])
            incv = 16
        elif inc:
            if E.count >= 60000:
                self._new_eng_sem(E)
            E.count += 1
            ev = (E.semkey, E.count)
            incv = 1
        else:
            ev = None
            incv = 0
        E.ops.append((fn, final, ev, incv, _snap(fn)))
        if ev is not None:
            for b in writes:
                b.w = ev
                b.r = []
            for b in reads:
                b.r.append(ev)
        return ev

    def mm_group(self, bank, mms, extra_reads=()):
        n = len(mms)
        allreads = list(extra_reads)
        for i, (fn, rb) in enumerate(mms):
            for b in rb:
                if b not in allreads:
                    allreads.append(b)
            last = i == n - 1
            self.emit("pe", fn, reads=(allreads if last else rb),
                      writes=[bank] if (i == 0 or last) else (), inc=last)

    def barrier(self):
        final = []
        for e in self.ENGS:
            for k, c in self.eng[e].slots:
                if c > 0:
                    final.append((k, c))
            if self.eng[e].count > 0:
                final.append((self.eng[e].semkey, self.eng[e].count))
        for e in self.ENGS:
            E = self.eng[e]
            w = []
            for k, v in final:
                if k == E.semkey and e != "sp" and False:
                    continue
                if E.seen.get(k, 0) >= v:
                    continue
                E.seen[k] = v
                w.append((k, v))
            if w:
                E.ops.append((None, w, None, 0, None))

    def finish(self):
        E = self.eng["sp"]
        final = []
        for e in self.ENGS:
            for k, c in self.eng[e].slots:
                if c > 0:
                    final.append((k, c))
            if self.eng[e].count > 0 and e != "sp":
                final.append((self.eng[e].semkey, self.eng[e].count))
        E.ops.append((None, final, None, 0, None))

    def replay(self, block):
        nc = self.nc
        handles = {"pe": (block.tensor, nc.tensor), "act": (block.scalar, nc.scalar),
                   "dve": (block.vector, nc.vector), "pool": (block.gpsimd, nc.gpsimd),
                   "sp": (block.sync, nc.sync)}
        sems = self.sems
        for e in self.ENGS:
            ops = self.eng[e].ops
            deco, _ = handles[e]

            def body(eng, ops=ops):
                for fn, waits, ev, incv, snap in ops:
                    for k, v in waits:
                        eng.wait_ge(sems[k], v)
                    if fn is None:
                        continue
                    _check_snap(fn, snap)
                    ins = fn(eng)
                    if ev is not None:
                        ins.then_inc(sems[ev[0]], incv)
            deco(body)


class Pool_:
    def __init__(self, tiles, nsub=0):
        self.tiles = tiles
        self.nsub = nsub
        self.bufs = [([Buf() for _ in range(nsub)] if nsub else Buf()) for _ in tiles]
        self.i = 0

    def next(self):
        t, b = self.tiles[self.i], self.bufs[self.i]
        self.i = (self.i + 1) % len(self.tiles)
        return t, b


def build_program(layers=(0, 1), final_norm=True, debug=None, phases=(1, 2, 3, 4)):
    debug = debug or set()
    nc = bass.Bass("TRN2", target_bir_lowering=False)
    stack = contextlib.ExitStack()
    with stack:
        _build(nc, stack, layers, final_norm, debug, phases)
    return nc


def _dram(nc, name, shape, dt, kind):
    return nc.dram_tensor(name, list(shape), dt, kind=kind).ap()


def _build(nc, stack, layers, final_norm, debug, phases):
    L = DEPTH
    x_in = _dram(nc, "x", [S, D], F32, "ExternalInput")
    out = _dram(nc, "out", [S, D], F32, "ExternalOutput")
    W = {}
    for name, shape in (
        ("norm1_g", [L, D]), ("w_in", [L, D, NIN]), ("gmlp_ln_g", [L, D]), ("gmlp_ln_b", [L, D]),
        ("gmlp_w_s", [L, 8, 128, 128]), ("gmlp_b_s", [L, 8, 128]), ("conv_w", [L, 4, D]),
        ("conv_b", [L, D]), ("lru_w_r", [L, 2, 8, 128, 128]), ("lru_b_r", [L, 2, D]),
        ("lru_w_i", [L, 2, 8, 128, 128]), ("lru_b_i", [L, 2, D]), ("lru_lambda", [L, 2, D]),
        ("w_out", [L, D, D]), ("norm2_g", [L, D]), ("w_ffn_in", [L, D, 2 * DFF]),
        ("w_ffn_out", [L, DFF, D]), ("final_g", [D]),
    ):
        W[name] = _dram(nc, name, shape, F32, "ExternalInput")

    def scratch(name, shape, dt):
        kind = "ExternalOutput" if name in debug else "Internal"
        return _dram(nc, name, shape, dt, kind)

    HT = scratch("HT", [128, KC, S], BF16)
    XC = scratch("XC", [KC, 128, S], F32)
    XCB = scratch("XCB", [KC, 128, S], BF16)
    HS = scratch("HS", [KC, 128, S], F32)
    XM = scratch("XM", [S, D], F32)
    XL = scratch("XL", [S, D], F32)

    sc = Sched(nc, stack)
    sb = lambda name, shape, dt: stack.enter_context(nc.sbuf_tensor(name, list(shape), dt))
    ps = lambda name, shape, dt: stack.enter_context(nc.psum_tensor(name, list(shape), dt))

    pfall = ps("pfall", [128, 6 * 512], F32)
    pf = Pool_([pfall[:, i * 512:(i + 1) * 512] for i in range(6)])
    pf2 = Pool_([pfall[:, i * 1024:(i + 1) * 1024] for i in range(3)])
    pb = Pool_([ps(f"pb{i}", [128, 1024], BF16) for i in range(2)])

    ident_f = sb("ident_f", [128, 128], F32)
    ident_b = sb("ident_b", [128, 128], BF16)
    eps_c = sb("eps_c", [128, 1], F32)
    one_c = sb("one_c", [128, 1], F32)
    b_ident_f, b_ident_b, b_eps, b_one = Buf(), Buf(), Buf(), Buf()

    sc.emit("pool", lambda e: e.memset(ident_f[:], 0.0), writes=[b_ident_f])
    sc.emit("pool", lambda e: e.affine_select(out=ident_f[:], in_=ident_f[:], compare_op=ALU.not_equal, fill=1.0,
                                              base=0, pattern=[[-1, 128]], channel_multiplier=1),
            reads=[b_ident_f], writes=[b_ident_f])
    sc.emit("pool", lambda e: e.tensor_copy(out=ident_b[:], in_=ident_f[:]), reads=[b_ident_f], writes=[b_ident_b])
    sc.emit("pool", lambda e: e.memset(eps_c[:], EPS), writes=[b_eps])
    sc.emit("pool", lambda e: e.memset(one_c[:], 1.0), writes=[b_one])

    bHT = [Buf() for _ in range(NT)]
    bXC = [Buf() for _ in range(NT + 1)]
    bXCB = [Buf() for _ in range(NT + 1)]
    bHS = [Buf() for _ in range(KC)]
    bXM = [Buf() for _ in range(NT)]
    bXL = [Buf() for _ in range(NT)]

    def wload(dst, src, buf):
        sc.emit("pool", lambda e: e.dma_start(out=dst, in_=src), writes=[buf], dma=True)

    def load_bcast(dst, src_row, buf):
        sc.emit("sp", lambda e: e.dma_start(out=dst, in_=src_row.partition_broadcast(128)), writes=[buf], dma=True)

    def rmsnorm_tile(xt, bxt, g_bc, bg, hb, bhb, ss, bss, rstd, brstd):
        for s in range(NS):
            sc.emit("act", lambda e, s=s: e.activation(out=hb[:, s, :], in_=xt[:, s, :], func=AF.Square,
                                                        accum_out=ss[:, s:s + 1]),
                    reads=[bxt[s]], writes=[bhb[s], bss[s]])
        sc.emit("act", lambda e: e.activation(out=rstd[:], in_=ss[:], func=AF.Sqrt, scale=1.0 / D, bias=eps_c[:]),
                reads=bss + [b_eps], writes=[brstd])
        sc.emit("dve", lambda e: e.reciprocal(out=rstd[:], in_=rstd[:]), reads=[brstd], writes=[brstd])
        for s in range(NS):
            sc.emit("dve", lambda e, s=s: e.scalar_tensor_tensor(out=hb[:, s, :], in0=xt[:, s, :], scalar=rstd[:, s:s + 1],
                                                                  in1=g_bc[:], op0=ALU.mult, op1=ALU.mult),
                    reads=[bxt[s], brstd, bg], writes=[bhb[s]])

    def transpose_tile(hb, bhb, hT, bhT):
        for k2 in range(KC // 2):
            pt, bpt = pb.next()
            mms = []
            for kk in range(2):
                k = 2 * k2 + kk
                for s in range(NS):
                    mms.append((lambda e, k=k, kk=kk, s=s, pt=pt: e.transpose(
                        out=pt[:, kk * 512 + s * 128: kk * 512 + (s + 1) * 128],
                        in_=hb[:, s, k * 128:(k + 1) * 128], identity=ident_b[:]), [bhb[s], b_ident_b]))
            sc.mm_group(bpt, mms)
            sc.emit("act", lambda e, k2=k2, pt=pt: e.copy(out=hT[:, 2 * k2:2 * k2 + 2, :],
                                                          in_=pt[:].rearrange("p (a t) -> p a t", a=2)),
                    reads=[bpt], writes=[bhT[k2]])

    uniq = []

    def emit_layer(li, l):
        x_src, bx_src = (x_in, None) if li == 0 else (XL, bXL)
        last = li == len(layers) - 1
        lstack = contextlib.ExitStack()
        lsb = lambda name, shape, dt: lstack.enter_context(nc.sbuf_tensor(f"{name}_L{l}", list(shape), dt))
        vt = lsb("vt", [128, 128], F32)
        cols = lsb("cols", [128, 128], F32)
        bvt, bcols = Buf(), Buf()
        sc.emit("pool", lambda e: e.memset(vt[:], 0.0), writes=[bvt])
        sc.emit("sp", lambda e: e.dma_start(out=vt[0:32, :], in_=W["conv_w"][l].rearrange("k (c p) -> (k c) p", p=128)),
                writes=[bvt], dma=True)
        sc.emit("sp", lambda e: e.dma_start(out=vt[32:40, :], in_=W["conv_b"][l].rearrange("(c p) -> c p", p=128)),
                writes=[bvt], dma=True)
        for base, nm in ((40, "lru_b_r"), (56, "lru_b_i"), (72, "lru_lambda")):
            sc.emit("sp", lambda e, base=base, nm=nm: e.dma_start(
                out=vt[base:base + 16, :], in_=W[nm][l].rearrange("d (c p) -> (d c) p", p=128)),
                writes=[bvt], dma=True)
        pz, bpz = pf.next()
        sc.mm_group(bpz, [(lambda e, pz=pz: e.transpose(out=pz[:, 0:128], in_=vt[:], identity=ident_f[:]),
                           [bvt, b_ident_f])])
        sc.emit("dve", lambda e, pz=pz: e.tensor_copy(out=cols[:], in_=pz[:, 0:128]), reads=[bpz], writes=[bcols])
        tmp = lsb("tmpc", [128, 4, 16], F32)
        btmp = Buf()
        cneg = lsb("cneg", [128, 16], F32)
        bcneg = Buf()
        ee, zz, z2, pp = tmp[:, 0, :], tmp[:, 1, :], tmp[:, 2, :], tmp[:, 3, :]
        sc.emit("act", lambda e: e.activation(out=ee, in_=cols[:, 72:88], func=AF.Exp, scale=-1.0),
                reads=[bcols], writes=[btmp])
        dv = lambda fn: sc.emit("dve", fn, reads=[btmp], writes=[btmp])
        dv(lambda e: e.tensor_scalar_add(out=zz, in0=ee, scalar1=2.0))
        dv(lambda e: e.reciprocal(out=zz, in_=zz))
        dv(lambda e: e.tensor_mul(out=zz, in0=zz, in1=ee))
        dv(lambda e: e.tensor_mul(out=z2, in0=zz, in1=zz))
        dv(lambda e: e.tensor_scalar(out=pp, in0=z2, scalar1=1.0 / 11.0, scalar2=1.0 / 9.0, op0=ALU.mult, op1=ALU.add))
        for cst in (1.0 / 7.0, 1.0 / 5.0, 1.0 / 3.0, 1.0):
            dv(lambda e: e.tensor_mul(out=pp, in0=pp, in1=z2))
            dv(lambda e, cst=cst: e.tensor_scalar_add(out=pp, in0=pp, scalar1=cst))
        dv(lambda e: e.tensor_mul(out=pp, in0=pp, in1=zz))
        sc.emit("dve", lambda e: e.tensor_scalar_mul(out=cneg[:], in0=pp, scalar1=-16.0), reads=[btmp], writes=[bcneg])

        hbias = lsb("hbias", [128, 32], F32)
        hcneg = lsb("hcneg", [128, 16], F32)
        bhbias, bhcneg = Buf(), Buf()
        sc.emit("dve", lambda e: e.tensor_scalar_mul(out=hbias[:], in0=cols[:, 40:72], scalar1=0.5),
                reads=[bcols], writes=[bhbias])
        sc.emit("dve", lambda e: e.tensor_scalar_mul(out=hcneg[:], in0=cneg[:], scalar1=0.5),
                reads=[bcneg], writes=[bhcneg])

        if 1 in phases:
          with contextlib.ExitStack() as pstack:
            psb = lambda name, shape, dt: pstack.enter_context(nc.sbuf_tensor(f"{name}_L{l}P{len(uniq)}", list(shape), dt))
            uniq.append(0)
            wzx = psb("wzx", [128, KC, D], BF16)
            bwzx = Buf()
            wload(wzx[:], W["w_in"][l, :, 2048:3072].rearrange("(kc p) n -> p kc n", p=128), bwzx)
            g1 = psb("g1", [128, D], F32)
            bg1 = Buf()
            load_bcast(g1[:], W["norm1_g"][l:l + 1, :], bg1)
            xts = Pool_([psb(f"xt{i}", [128, NS, D], F32) for i in range(2)], nsub=NS)
            hbs = Pool_([psb(f"hb{i}", [128, NS, D], BF16) for i in range(2)], nsub=NS)
            hTs = Pool_([psb(f"hT{i}", [128, KC, TT], BF16) for i in range(2)], nsub=KC // 2)
            zxbs = Pool_([psb(f"zxb{i}", [128, KC, TT + 3], F32) for i in range(2)], nsub=KC + 1)
            xcos = Pool_([psb(f"xco{i}", [128, KC, TT], F32) for i in range(2)], nsub=KC)
            xcbos = Pool_([psb(f"xcbo{i}", [128, KC, TT], BF16) for i in range(2)])
            ss = psb("ss", [128, NS], F32)
            rstd = psb("rstd", [128, NS], F32)
            bss = [Buf() for _ in range(NS)]
            brstd = Buf()
            xloads = {}
            prev = None

            def xload(j):
                xt, bxt = xts.next()
                rd = [bx_src[j]] if bx_src is not None else []
                sc.emit("sp", lambda e, j=j, xt=xt: e.dma_start(
                    out=xt[:], in_=x_src[j * TT:(j + 1) * TT, :].rearrange("(s p) d -> p s d", p=128)),
                    reads=rd, writes=bxt, dma=True)
                xloads[j] = (xt, bxt)
            stagedA = {}

            def stageA(j):
                if j + 1 < NT:
                    xload(j + 1)
                xt, bxt = xloads.pop(j)
                hb, bhb = hbs.next()
                rmsnorm_tile(xt, bxt, g1, bg1, hb, bhb, ss, bss, rstd, brstd)
                hT, bhT = hTs.next()
                transpose_tile(hb, bhb, hT, bhT)
                sc.emit("sp", lambda e, j=j, hT=hT: e.dma_start(out=HT[:, :, j * TT:(j + 1) * TT], in_=hT[:]),
                        reads=bhT, writes=[bHT[j]], dma=True)
                stagedA[j] = (hT, bhT)

            xload(0)
            stageA(0)
            for j in range(NT):
                hT, bhT = stagedA.pop(j)
                zxb, bzxb = zxbs.next()
                if j == 0:
                    sc.emit("pool", lambda e, zxb=zxb: e.memset(zxb[:, :, 0:3], 0.0), writes=[bzxb[KC]])
                else:
                    sc.emit("pool", lambda e, zxb=zxb, pzxb=prev[0]: e.tensor_copy(out=zxb[:, :, 0:3], in_=pzxb[:, :, TT:TT + 3]),
                            reads=prev[1][0:KC], writes=[bzxb[KC]])
                xco, bxco = xcos.next()
                xcbo, bxcbo = xcbos.next()
                for c in range(KC):
                    pz, bpz = pf.next()
                    mms = []
                    for k in range(KC):
                        mms.append((lambda e, c=c, k=k, pz=pz, hT=hT: e.matmul(
                            pz[:], lhsT=wzx[:, k, c * 128:(c + 1) * 128], rhs=hT[:, k, :],
                            start=(k == 0), stop=(k == KC - 1)), [bhT[k // 2], bwzx]))
                    sc.mm_group(bpz, mms)
                    sc.emit("act", lambda e, c=c, pz=pz, zxb=zxb: e.copy(out=zxb[:, c, 3:3 + TT], in_=pz[:]),
                            reads=[bpz], writes=[bzxb[c]])
                    sc.emit("act", lambda e, c=c, pz=pz, xco=xco: e.activation(
                        out=xco[:, c, :], in_=pz[:], func=AF.Identity, scale=cols[:, 24 + c:25 + c],
                        bias=cols[:, 32 + c:33 + c]), reads=[bpz, bcols], writes=[bxco[c]])
                for c2 in range(KC // 2):
                    if c2 == KC // 4 and j + 1 < NT:
                        stageA(j + 1)
                    for k in range(3):
                        for c in (2 * c2, 2 * c2 + 1):
                            sc.emit("dve", lambda e, c=c, k=k, zxb=zxb, xco=xco: e.scalar_tensor_tensor(
                                out=xco[:, c, :], in0=zxb[:, c, k:k + TT], scalar=cols[:, k * 8 + c:k * 8 + c + 1],
                                in1=xco[:, c, :], op0=ALU.mult, op1=ALU.add),
                                reads=[bzxb[c], bzxb[KC], bcols, bxco[c]], writes=[bxco[c]])
                for c in range(KC):
                    sc.emit("act", lambda e, c=c, xco=xco, xcbo=xcbo: e.copy(out=xcbo[:, c, :], in_=xco[:, c, :]),
                            reads=[bxco[c]], writes=[bxcbo])
                if j == 0:
                    sc.emit("sp", lambda e, xco=xco: e.dma_start(
                        out=XC[:, :, 0:TT - 2].rearrange("c p t -> p c t"), in_=xco[:, :, 2:TT]),
                        reads=bxco, writes=[bXC[j]], dma=True)
                    sc.emit("sp", lambda e, xcbo=xcbo: e.dma_start(
                        out=XCB[:, :, 0:TT - 2].rearrange("c p t -> p c t"), in_=xcbo[:, :, 2:TT]),
                        reads=[bxcbo], writes=[bXCB[j]], dma=True)
                else:
                    sc.emit("sp", lambda e, j=j, xco=xco: e.dma_start(
                        out=XC[:, :, j * TT - 2:(j + 1) * TT - 2].rearrange("c p t -> p c t"), in_=xco[:]),
                        reads=bxco, writes=[bXC[j]], dma=True)
                    sc.emit("sp", lambda e, j=j, xcbo=xcbo: e.dma_start(
                        out=XCB[:, :, j * TT - 2:(j + 1) * TT - 2].rearrange("c p t -> p c t"), in_=xcbo[:]),
                        reads=[bxcbo], writes=[bXCB[j]], dma=True)
                prev = (zxb, bzxb)
            zxb, bzxb = zxbs.next()
            sc.emit("pool", lambda e, zxb=zxb, pzxb=prev[0]: e.tensor_copy(out=zxb[:, :, 0:3], in_=pzxb[:, :, TT:TT + 3]),
                    reads=prev[1][0:KC], writes=[bzxb[KC]])
            sc.emit("pool", lambda e, zxb=zxb: e.memset(zxb[:, :, 3:5], 0.0), writes=bzxb[0:KC])
            xco, bxco = xcos.next()
            for c in range(KC):
                sc.emit("dve", lambda e, c=c, zxb=zxb, xco=xco: e.tensor_scalar(
                    out=xco[:, c, 0:2], in0=zxb[:, c, 0:2], scalar1=cols[:, c:c + 1], scalar2=cols[:, 32 + c:33 + c],
                    op0=ALU.mult, op1=ALU.add), reads=[bzxb[c], bzxb[KC], bcols], writes=[bxco[c]])
                for k in range(1, 4):
                    sc.emit("dve", lambda e, c=c, k=k, zxb=zxb, xco=xco: e.scalar_tensor_tensor(
                        out=xco[:, c, 0:2], in0=zxb[:, c, k:k + 2], scalar=cols[:, k * 8 + c:k * 8 + c + 1],
                        in1=xco[:, c, 0:2], op0=ALU.mult, op1=ALU.add), reads=[bzxb[c], bzxb[KC], bcols, bxco[c]], writes=[bxco[c]])
            sc.emit("sp", lambda e, xco=xco: e.dma_start(
                out=XC[:, :, S - 2:S].rearrange("c p t -> p c t"), in_=xco[:, :, 0:2]),
                reads=bxco, writes=[bXC[NT]], dma=True)
            xcbo, bxcbo = xcbos.next()
            sc.emit("act", lambda e, xco=xco, xcbo=xcbo: e.copy(out=xcbo[:, :, 0:2], in_=xco[:, :, 0:2]),
                    reads=bxco, writes=[bxcbo])
            sc.emit("sp", lambda e, xcbo=xcbo: e.dma_start(
                out=XCB[:, :, S - 2:S].rearrange("c p t -> p c t"), in_=xcbo[:, :, 0:2]),
                reads=[bxcbo], writes=[bXCB[NT]], dma=True)
          sc.barrier()

        if 2 in phases:
          with contextlib.ExitStack() as pstack:
            psb = lambda name, shape, dt: pstack.enter_context(nc.sbuf_tensor(f"{name}_L{l}P{len(uniq)}", list(shape), dt))
            uniq.append(0)
            wg = psb("wg", [128, 2, 2, KC, 128], BF16)
            bwg = Buf()
            for t, nm in ((0, "lru_w_r"), (1, "lru_w_i")):
                for d in range(2):
                    wload(wg[:, t, d, :, :], W[nm][l, d].rearrange("h c o -> c h o"), bwg)
            NB = S // 512
            xcs = Pool_([psb(f"xc{i}", [128, S], F32) for i in range(2)], nsub=NB)
            xcbs = Pool_([psb(f"xcb{i}", [128, S], BF16) for i in range(2)], nsub=NB)
            ra = [psb(f"ra{d}", [128, S], F32) for d in range(2)]
            ib = [psb(f"ib{d}", [128, S], F32) for d in range(2)]
            mh = [psb(f"mh{d}", [128, S], F32) for d in range(2)]
            bra, bib, bmh = [Buf(), Buf()], [Buf(), Buf()], [Buf(), Buf()]
            hso = psb("hso", [128, S], F32)
            bhso = Buf()

            def conv_stage(c):
                xc, bxc = xcs.next()
                xcb, bxcb = xcbs.next()
                sc.emit("sp", lambda e, c=c, xcb=xcb: e.dma_start(out=xcb[:], in_=XCB[c]), reads=bXCB, writes=bxcb, dma=True)
                sc.emit("sp", lambda e, c=c, xc=xc: e.dma_start(out=xc[:], in_=XC[c]), reads=bXC, writes=bxc, dma=True)
                return xc, bxc, xcb, bxcb

            staged = {0: conv_stage(0)}
            for c in range(KC):
                xc, bxc, xcb, bxcb = staged.pop(c)
                for d in range(2):
                    for t in range(2):
                        dst, bdst = (ra[d], bra[d]) if t == 0 else (ib[d], bib[d])
                        bcol = (0 if t == 0 else 16) + d * 8 + c
                        for tb2 in range(NB // 2):
                            pz, bpz = pf2.next()
                            sc.mm_group(bpz, [(lambda e, t=t, d=d, c=c, tb=2 * tb2 + h, h=h, pz=pz, xcb=xcb: e.matmul(
                                pz[:, h * 512:(h + 1) * 512], lhsT=wg[:, t, d, c, :], rhs=xcb[:, tb * 512:(tb + 1) * 512],
                                start=True, stop=True), [bwg, bxcb[2 * tb2 + h]]) for h in range(2)])
                            sc.emit("act", lambda e, dst=dst, tb2=tb2, pz=pz, bcol=bcol: e.activation(
                                out=dst[:, tb2 * 1024:(tb2 + 1) * 1024], in_=pz[:], func=AF.Tanh, scale=0.5,
                                bias=hbias[:, bcol:bcol + 1]), reads=[bpz, bhbias], writes=[bdst])
                    sc.emit("act", lambda e, d=d, c=c: e.activation(
                        out=mh[d][:], in_=ra[d][:], func=AF.Exp, scale=cneg[:, d * 8 + c:d * 8 + c + 1],
                        bias=cneg[:, d * 8 + c:d * 8 + c + 1]), reads=[bra[d], bcneg], writes=[bmh[d]])
                    sc.emit("act", lambda e, d=d, c=c: e.activation(
                        out=ra[d][:], in_=ra[d][:], func=AF.Exp, scale=hcneg[:, d * 8 + c:d * 8 + c + 1],
                        bias=hcneg[:, d * 8 + c:d * 8 + c + 1]), reads=[bra[d], bhcneg], writes=[bra[d]])
                    sc.emit("act", lambda e, d=d: e.activation(out=mh[d][:], in_=mh[d][:], func=AF.Sqrt,
                                                               scale=-1.0, bias=one_c[:]),
                            reads=[bmh[d], b_one], writes=[bmh[d]])
                    if d == 0 and c + 1 < KC:
                        staged[c + 1] = conv_stage(c + 1)
                for d in range(2):
                    sc.emit("dve", lambda e, d=d, xc=xc: e.scalar_tensor_tensor(
                        out=mh[d][:], in0=mh[d][:], scalar=0.5, in1=xc[:], op0=ALU.mult, op1=ALU.mult),
                        reads=[bmh[d]] + bxc, writes=[bmh[d]])
                    sc.emit("dve", lambda e, d=d: e.scalar_tensor_tensor(
                        out=ib[d][:], in0=ib[d][:], scalar=1.0, in1=mh[d][:], op0=ALU.add, op1=ALU.mult),
                        reads=[bib[d], bmh[d]], writes=[bib[d]])
                    if d == 0:
                        sc.emit("dve", lambda e: e.tensor_tensor_scan(out=hso[:], data0=ra[0][:], data1=ib[0][:],
                                                                       initial=0.0, op0=ALU.mult, op1=ALU.add),
                                reads=[bra[0], bib[0]], writes=[bhso])
                    else:
                        sc.emit("dve", lambda e: e.tensor_tensor_scan(out=mh[1][:, ::-1], data0=ra[1][:, ::-1],
                                                                       data1=ib[1][:, ::-1], initial=0.0,
                                                                       op0=ALU.mult, op1=ALU.add),
                                reads=[bra[1], bib[1], bmh[1]], writes=[bmh[1]])
                sc.emit("dve", lambda e: e.tensor_add(out=hso[:], in0=hso[:], in1=mh[1][:]),
                        reads=[bhso, bmh[1]], writes=[bhso])
                sc.emit("sp", lambda e, c=c: e.dma_start(out=HS[c], in_=hso[:]), reads=[bhso], writes=[bHS[c]], dma=True)
          sc.barrier()

        lstack.close()
        if 3 in phases:
          with contextlib.ExitStack() as pstack:
            psb = lambda name, shape, dt: pstack.enter_context(nc.sbuf_tensor(f"{name}_L{l}P{len(uniq)}", list(shape), dt))
            uniq.append(0)
            w3 = psb("w3", [128, KC, 5 * D], BF16)
            bw3 = [Buf() for _ in range(5)]
            wsrc = W["w_in"][l].rearrange("(kc p) n -> p kc n", p=128)
            for blk, c0 in ((1, 1024), (0, 0), (3, 4096), (2, 3072), (4, 5120)):
                wload(w3[:, :, blk * D:(blk + 1) * D], wsrc[:, :, c0:c0 + D], bw3[blk])
            wo = psb("wo", [128, KC, D], BF16)
            bwo = Buf()
            wload(wo[:], W["w_out"][l].rearrange("(kc p) n -> p kc n", p=128), bwo)
            wsn = psb("wsn", [128, 8, 128], F32)
            wsT = psb("wsT", [128, 8, 128], BF16)
            bwsn, bwsT = Buf(), Buf()
            sc.emit("sp", lambda e: e.dma_start(out=wsn[:], in_=W["gmlp_w_s"][l].rearrange("g p q -> p g q")),
                    writes=[bwsn], dma=True)
            for g2 in range(2):
                pz, bpz = pf.next()
                sc.mm_group(bpz, [(lambda e, g=g2 * 4 + gg, gg=gg, pz=pz: e.transpose(
                    out=pz[:, gg * 128:(gg + 1) * 128], in_=wsn[:, g, :], identity=ident_f[:]), [bwsn, b_ident_f])
                    for gg in range(4)])
                sc.emit("dve", lambda e, g2=g2, pz=pz: e.tensor_copy(
                    out=wsT[:, g2 * 4:(g2 + 1) * 4, :], in_=pz[:].rearrange("p (g q) -> p g q", g=4)),
                    reads=[bpz], writes=[bwsT])
            bsb = psb("bsb", [128, 8, 128], F32)
            lng = psb("lng", [128, D], F32)
            lnb = psb("lnb", [128, D], F32)
            bbsb, blng, blnb = Buf(), Buf(), Buf()
            load_bcast(bsb[:].rearrange("p g q -> p (g q)"), W["gmlp_b_s"][l:l + 1].rearrange("o g q -> o (g q)"), bbsb)
            load_bcast(lng[:], W["gmlp_ln_g"][l:l + 1, :], blng)
            load_bcast(lnb[:], W["gmlp_ln_b"][l:l + 1, :], blnb)

            hTs = Pool_([psb(f"hT{i}", [128, KC, TT], BF16) for i in range(2)])
            hss = Pool_([psb(f"hs{i}", [128, TT], F32) for i in range(3)])
            xss = Pool_([psb(f"xs{i}", [128, D], F32) for i in range(2)])
            vgs = Pool_([psb(f"vg{i}", [128, D], F32) for i in range(2)])
            vn = psb("vn", [128, NS, D], BF16)
            vn2 = psb("vn2", [128, NS, D], BF16)
            mTs = Pool_([psb(f"mT{i}", [128, KC, TT], BF16) for i in range(2)], nsub=KC)
            tA = Pool_([psb(f"tA{i}", [128, TT], F32) for i in range(2)])
            tB = Pool_([psb(f"tB{i}", [128, TT], F32) for i in range(2)])
            tC = Pool_([psb(f"tC{i}", [128, TT], F32) for i in range(2)])
            tD = Pool_([psb(f"tD{i}", [128, TT], F32) for i in range(2)])
            st = psb("lnst", [128, NS, 2, 6], F32)
            mv = psb("lnmv", [128, NS, 2], F32)
            lrs = psb("lnrs", [128, NS], F32)
            lnm = psb("lnnm", [128, NS], F32)
            bst = [Buf() for _ in range(NS)]
            bmv = [Buf() for _ in range(NS)]
            blrs, blnm = Buf(), Buf()

            def zmm(pz, blk, c, hT, bhT):
                return [(lambda e, k=k: e.matmul(pz[:], lhsT=w3[:, k, blk * D + c * 128: blk * D + (c + 1) * 128],
                                                  rhs=hT[:, k, :], start=(k == 0), stop=(k == KC - 1)),
                         [bhT, bw3[blk]]) for k in range(KC)]

            vns = Pool_([vn, vn2], nsub=NS)
            hloads, vstate = {}, {}

            def hload(j):
                hT, bhT = hTs.next()
                sc.emit("sp", lambda e, j=j, hT=hT: e.dma_start(out=hT[:], in_=HT[:, :, j * TT:(j + 1) * TT]),
                        reads=[bHT[j]], writes=[bhT], dma=True)
                hloads[j] = (hT, bhT)

            def vpath_sub(j, s):
                hT, bhT = hloads[j]
                if s == 0:
                    vstate[j] = (vns.next(), [])
                (vnj, bvnj), vgl = vstate[j]
                vg, bvg = vgs.next()
                for nb in range(2):
                    pz, bpz = pf.next()
                    sc.mm_group(bpz, [(lambda e, k=k, s=s, nb=nb, pz=pz, hT=hT: e.matmul(
                        pz[:], lhsT=hT[:, k, s * 128:(s + 1) * 128],
                        rhs=w3[:, k, D + nb * 512: D + (nb + 1) * 512], start=(k == 0), stop=(k == KC - 1)),
                        [bhT, bw3[1]]) for k in range(KC)])
                    sc.emit("act", lambda e, nb=nb, pz=pz, vg=vg: e.activation(
                        out=vg[:, nb * 512:(nb + 1) * 512], in_=pz[:], func=AF.Gelu_apprx_tanh),
                        reads=[bpz], writes=[bvg])
                for nb in range(2):
                    sc.emit("dve", lambda e, s=s, nb=nb, vg=vg: e.bn_stats(out=st[:, s, nb, :], in_=vg[:, nb * 512:(nb + 1) * 512]),
                            reads=[bvg], writes=[bst[s]])
                sc.emit("dve", lambda e, s=s: e.bn_aggr(out=mv[:, s, :], in_=st[:, s, :, :]), reads=[bst[s]], writes=[bmv[s]])
                vgl.append((vg, bvg))
                if s % 2 == 1:
                    s0 = s - 1
                    sc.emit("act", lambda e, s0=s0: e.activation(out=lrs[:, s0:s0 + 2], in_=mv[:, s0:s0 + 2, 1], func=AF.Sqrt,
                                                                  bias=eps_c[:]),
                            reads=[bmv[s0], bmv[s0 + 1], b_eps], writes=[blrs])
                    sc.emit("dve", lambda e, s0=s0: e.reciprocal(out=lrs[:, s0:s0 + 2], in_=lrs[:, s0:s0 + 2]),
                            reads=[blrs], writes=[blrs])
                    for s1 in (s0, s0 + 1):
                        vg1, bvg1 = vgl[s1]
                        sc.emit("dve", lambda e, s1=s1, vg1=vg1: e.tensor_scalar(
                            out=vg1[:], in0=vg1[:], scalar1=mv[:, s1, 0:1], scalar2=lrs[:, s1:s1 + 1],
                            op0=ALU.subtract, op1=ALU.mult), reads=[bvg1, bmv[s1], blrs], writes=[bvg1])
                        sc.emit("dve", lambda e, vg1=vg1: e.tensor_mul(out=vg1[:], in0=vg1[:], in1=lng[:]),
                                reads=[bvg1, blng], writes=[bvg1])
                        sc.emit("dve", lambda e, s1=s1, vg1=vg1, vnj=vnj: e.tensor_add(out=vnj[:, s1, :], in0=vg1[:], in1=lnb[:]),
                                reads=[bvg1, blnb], writes=[bvnj[s1]])

            def xload_sub(j, s):
                xs, bxs = xss.next()
                rd = [bx_src[j]] if bx_src is not None else []
                sc.emit("sp", lambda e, j=j, s=s, xs=xs: e.dma_start(
                    out=xs[:], in_=x_src[j * TT + s * 128: j * TT + (s + 1) * 128, :]),
                    reads=rd, writes=[bxs], dma=True)
                return xs, bxs

            def outproj(j, mT, bmT, xl):
                for s in range(NS):
                    xs, bxs = xl[s] if s in xl else xload_sub(j, s)
                    for nb in range(2):
                        po, bpo = pf.next()
                        sc.mm_group(bpo, [(lambda e, k=k, s=s, nb=nb, po=po, mT=mT: e.matmul(
                            po[:], lhsT=mT[:, k, s * 128:(s + 1) * 128], rhs=wo[:, k, nb * 512:(nb + 1) * 512],
                            start=(k == 0), stop=(k == KC - 1)), [bmT[k], bwo]) for k in range(KC)])
                        sc.emit("dve", lambda e, nb=nb, po=po, xs=xs: e.scalar_tensor_tensor(
                            out=xs[:, nb * 512:(nb + 1) * 512], in0=po[:], scalar=0.5, in1=xs[:, nb * 512:(nb + 1) * 512],
                            op0=ALU.mult, op1=ALU.add),
                            reads=[bpo, bxs], writes=[bxs])
                    sc.emit("pool", lambda e, j=j, s=s, xs=xs: e.dma_start(
                        out=XM[j * TT + s * 128: j * TT + (s + 1) * 128, :], in_=xs[:]),
                        reads=[bxs], writes=[bXM[j]], dma=True)

            hload(0)
            for s in range(NS):
                vpath_sub(0, s)
            pend = None
            for j in range(NT):
                if j + 1 < NT:
                    hload(j + 1)
                hT, bhT = hloads[j]
                (vnj, bvn), _ = vstate[j]
                mT, bmT = mTs.next()
                xl = {}
                if pend is not None:
                    for s in (0, 1):
                        xl[s] = xload_sub(pend[0], s)
                for c in range(KC):
                    hs, bhs = hss.next()
                    sc.emit("sp", lambda e, c=c, j=j, hs=hs: e.dma_start(out=hs[:], in_=HS[c, :, j * TT:(j + 1) * TT]),
                            reads=bHS, writes=[bhs], dma=True)
                    a_, ba = tA.next()
                    b_, bb = tB.next()
                    c_, bc = tC.next()
                    d_, bd = tD.next()
                    pu, bpu = pf.next()
                    sc.mm_group(bpu, zmm(pu, 0, c, hT, bhT))
                    sc.emit("act", lambda e, pu=pu, a_=a_: e.activation(out=a_[:], in_=pu[:], func=AF.Gelu_apprx_tanh),
                            reads=[bpu], writes=[ba])
                    pa, bpa = pf.next()
                    sc.mm_group(bpa, zmm(pa, 3, c, hT, bhT))
                    sc.emit("act", lambda e, pa=pa, b_=b_: e.activation(out=b_[:], in_=pa[:], func=AF.Tanh, scale=0.5),
                            reads=[bpa], writes=[bb])
                    pg, bpg = pf.next()
                    sc.mm_group(bpg, zmm(pg, 2, c, hT, bhT))
                    sc.emit("act", lambda e, pg=pg, c_=c_: e.activation(out=c_[:], in_=pg[:], func=AF.Gelu_apprx_tanh),
                            reads=[bpg], writes=[bc])
                    pB, bpB = pf.next()
                    sc.mm_group(bpB, zmm(pB, 4, c, hT, bhT))
                    sc.emit("act", lambda e, pB=pB, d_=d_: e.activation(out=d_[:], in_=pB[:], func=AF.Tanh, scale=0.5),
                            reads=[bpB], writes=[bd])
                    pm, bpm = pf.next()
                    sc.mm_group(bpm, [(lambda e, s=s, c=c, pm=pm, vnj=vnj: e.matmul(
                        pm[:, s * 128:(s + 1) * 128], lhsT=vnj[:, s, c * 128:(c + 1) * 128], rhs=wsT[:, c, :],
                        start=True, stop=True), [bvn[s], bwsT]) for s in range(NS)])
                    sc.emit("dve", lambda e, a_=a_, b_=b_: e.scalar_tensor_tensor(
                        out=a_[:], in0=b_[:], scalar=1.0, in1=a_[:], op0=ALU.add, op1=ALU.mult),
                        reads=[ba, bb], writes=[ba])
                    sc.emit("dve", lambda e, c=c, pm=pm, b_=b_: e.tensor_tensor(
                        out=b_[:].rearrange("p (s q) -> p s q", s=NS), in0=pm[:].rearrange("p (s q) -> p s q", s=NS),
                        in1=bsb[:, c:c + 1, :].to_broadcast([128, NS, 128]), op=ALU.add),
                        reads=[bpm, bbsb, bb], writes=[bb])
                    sc.emit("dve", lambda e, a_=a_, b_=b_: e.tensor_mul(out=a_[:], in0=a_[:], in1=b_[:]),
                            reads=[ba, bb], writes=[ba])
                    sc.emit("dve", lambda e, c_=c_, d_=d_: e.scalar_tensor_tensor(
                        out=c_[:], in0=d_[:], scalar=1.0, in1=c_[:], op0=ALU.add, op1=ALU.mult),
                        reads=[bc, bd], writes=[bc])
                    sc.emit("pool", lambda e, c_=c_, hs=hs: e.tensor_mul(out=c_[:], in0=c_[:], in1=hs[:]),
                            reads=[bc, bhs], writes=[bc])
                    sc.emit("dve", lambda e, c=c, a_=a_, c_=c_, mT=mT: e.tensor_add(out=mT[:, c, :], in0=a_[:], in1=c_[:]),
                            reads=[ba, bc], writes=[bmT[c]])
                    if c == 1 and pend is not None:
                        outproj(pend[0], pend[1], pend[2], xl)
                        pend = None
                    if j + 1 < NT and c % 2 == 1:
                        vpath_sub(j + 1, c // 2)
                pend = (j, mT, bmT)
                hloads.pop(j)
                vstate.pop(j)
            outproj(pend[0], pend[1], pend[2], {})
          sc.barrier()

        if 4 in phases:
          with contextlib.ExitStack() as pstack:
            psb = lambda name, shape, dt: pstack.enter_context(nc.sbuf_tensor(f"{name}_L{l}P{len(uniq)}", list(shape), dt))
            uniq.append(0)
            wfi = psb("wfi", [128, KC, 2 * DFF], BF16)
            bwfi = [Buf() for _ in range(2 * FC)]
            fsrc = W["w_ffn_in"][l].rearrange("(kc p) n -> p kc n", p=128)
            for f2 in range(FC // 2):
                for half in range(2):
                    c0 = half * DFF + f2 * 256
                    sc.emit("pool", lambda e, c0=c0: e.dma_start(out=wfi[:, :, c0:c0 + 256], in_=fsrc[:, :, c0:c0 + 256]),
                            writes=[bwfi[half * FC + 2 * f2], bwfi[half * FC + 2 * f2 + 1]], dma=True)
            wfo = psb("wfo", [128, FC, D], BF16)
            bwfo = Buf()
            osrc = W["w_ffn_out"][l].rearrange("(fc p) n -> p fc n", p=128)
            for f0 in range(0, FC, 8):
                f1 = min(FC, f0 + 8)
                wload(wfo[:, f0:f1, :], osrc[:, f0:f1, :], bwfo)
            g2 = psb("g2", [128, D], F32)
            bg2 = Buf()
            load_bcast(g2[:], W["norm2_g"][l:l + 1, :], bg2)
            if last and final_norm:
                gf = psb("gf", [128, D], F32)
                bgf = Buf()
                load_bcast(gf[:], W["final_g"].rearrange("(o d) -> o d", o=1), bgf)
                fss = psb("fss", [128, 4], F32)
                bfss = [Buf() for _ in range(4)]
            xss4 = Pool_([psb(f"xs4{i}", [128, D], F32) for i in range(2)])
            xrs4 = Pool_([psb(f"xr4{i}", [128, D], F32) for i in range(2)])
            hb = psb("hb4", [128, NS, D], BF16)
            bhb = [Buf() for _ in range(NS)]
            hTs4 = Pool_([psb(f"hT4{i}", [128, KC, TT], BF16) for i in range(2)], nsub=KC // 2)
            sgs = Pool_([psb(f"sg{i}", [128, TT], F32) for i in range(2)])
            ffT = psb("ffT", [128, FC, TT], BF16)
            bff = [Buf() for _ in range(FC)]
            ss = psb("ss4", [128, NS], F32)
            rstd = psb("rstd4", [128, NS], F32)
            bss = [Buf() for _ in range(NS)]
            brs = [Buf() for _ in range(NS)]
            dst, bdst = (out, None) if last else (XL, bXL)
            staged = {}

            def stage1(j):
                for s in range(NS):
                    xs, bxs = xss4.next()
                    sc.emit("sp", lambda e, j=j, s=s, xs=xs: e.dma_start(
                        out=xs[:], in_=XM[j * TT + s * 128: j * TT + (s + 1) * 128, :]),
                        reads=[bXM[j]], writes=[bxs], dma=True)
                    sc.emit("act", lambda e, s=s, xs=xs: e.activation(out=hb[:, s, :], in_=xs[:], func=AF.Square,
                                                                       accum_out=ss[:, s:s + 1]),
                            reads=[bxs], writes=[bhb[s], bss[s]])
                    sc.emit("act", lambda e, s=s: e.activation(out=rstd[:, s:s + 1], in_=ss[:, s:s + 1], func=AF.Sqrt,
                                                                scale=1.0 / D, bias=eps_c[:]),
                            reads=[bss[s], b_eps], writes=[brs[s]])
                    sc.emit("dve", lambda e, s=s: e.reciprocal(out=rstd[:, s:s + 1], in_=rstd[:, s:s + 1]),
                            reads=[brs[s]], writes=[brs[s]])
                    sc.emit("dve", lambda e, s=s, xs=xs: e.scalar_tensor_tensor(
                        out=hb[:, s, :], in0=xs[:], scalar=rstd[:, s:s + 1], in1=g2[:], op0=ALU.mult, op1=ALU.mult),
                        reads=[bxs, brs[s], bg2], writes=[bhb[s]])

            def stage2(j):
                hT, bhT = hTs4.next()
                transpose_tile(hb, bhb, hT, bhT)
                staged[j] = (hT, bhT)

            stage1(0)
            stage2(0)
            for j in range(NT):
                hT, bhT = staged.pop(j)
                for f in range(FC):
                    pg, bpg = pf.next()
                    sc.mm_group(bpg, [(lambda e, k=k, f=f, pg=pg, hT=hT: e.matmul(
                        pg[:], lhsT=wfi[:, k, f * 128:(f + 1) * 128], rhs=hT[:, k, :],
                        start=(k == 0), stop=(k == KC - 1)), [bhT[k // 2], bwfi[f]]) for k in range(KC)])
                    pu, bpu = pf.next()
                    sc.mm_group(bpu, [(lambda e, k=k, f=f, pu=pu, hT=hT: e.matmul(
                        pu[:], lhsT=wfi[:, k, DFF + f * 128: DFF + (f + 1) * 128], rhs=hT[:, k, :],
                        start=(k == 0), stop=(k == KC - 1)), [bhT[k // 2], bwfi[FC + f]]) for k in range(KC)])
                    sg, bsg = sgs.next()
                    sc.emit("act", lambda e, pg=pg, sg=sg: e.activation(out=sg[:], in_=pg[:], func=AF.Silu),
                            reads=[bpg], writes=[bsg])
                    sc.emit("dve", lambda e, f=f, pu=pu, sg=sg: e.tensor_mul(out=ffT[:, f, :], in0=pu[:], in1=sg[:]),
                            reads=[bpu, bsg], writes=[bff[f]])
                    if f == 3 and j + 1 < NT:
                        stage1(j + 1)
                    if f == FC // 2 + 2 and j + 1 < NT:
                        stage2(j + 1)
                for s in range(NS):
                    xr, bxr = xrs4.next()
                    sc.emit("sp", lambda e, j=j, s=s, xr=xr: e.dma_start(
                        out=xr[:], in_=XM[j * TT + s * 128: j * TT + (s + 1) * 128, :]),
                        reads=[bXM[j]], writes=[bxr], dma=True)
                    for nb in range(2):
                        po, bpo = pf.next()
                        sc.mm_group(bpo, [(lambda e, f=f, s=s, nb=nb, po=po: e.matmul(
                            po[:], lhsT=ffT[:, f, s * 128:(s + 1) * 128], rhs=wfo[:, f, nb * 512:(nb + 1) * 512],
                            start=(f == 0), stop=(f == FC - 1)), [bff[f], bwfo]) for f in range(FC)])
                        sc.emit("dve", lambda e, nb=nb, po=po, xr=xr: e.tensor_add(
                            out=xr[:, nb * 512:(nb + 1) * 512], in0=po[:], in1=xr[:, nb * 512:(nb + 1) * 512]),
                            reads=[bpo, bxr], writes=[bxr])
                    if last and final_norm:
                        fj, bfj = sgs.next()
                        fs, bfs = Buf(), Buf()
                        sc.emit("act", lambda e, xr=xr, fj=fj: e.activation(
                            out=fj[:].rearrange("p (a t) -> p a t", a=1)[:, 0, :], in_=xr[:, 0:TT], func=AF.Square,
                            accum_out=fss[:, 0:1]), reads=[bxr], writes=[bfj, bfss[0]])
                        sc.emit("act", lambda e, xr=xr, fj=fj: e.activation(
                            out=fj[:], in_=xr[:, TT:D], func=AF.Square, accum_out=fss[:, 1:2]),
                            reads=[bxr], writes=[bfj, bfss[1]])
                        sc.emit("dve", lambda e: e.tensor_add(out=fss[:, 2:3], in0=fss[:, 0:1], in1=fss[:, 1:2]),
                                reads=bfss[0:2], writes=[bfss[2]])
                        sc.emit("act", lambda e: e.activation(out=fss[:, 3:4], in_=fss[:, 2:3], func=AF.Sqrt,
                                                              scale=1.0 / D, bias=eps_c[:]),
                                reads=[bfss[2], b_eps], writes=[bfss[3]])
                        sc.emit("dve", lambda e: e.reciprocal(out=fss[:, 3:4], in_=fss[:, 3:4]),
                                reads=[bfss[3]], writes=[bfss[3]])
                        sc.emit("dve", lambda e, xr=xr: e.scalar_tensor_tensor(
                            out=xr[:], in0=xr[:], scalar=fss[:, 3:4], in1=gf[:], op0=ALU.mult, op1=ALU.mult),
                            reads=[bxr, bfss[3], bgf], writes=[bxr])
                    sc.emit("sp", lambda e, j=j, s=s, xr=xr: e.dma_start(
                        out=dst[j * TT + s * 128: j * TT + (s + 1) * 128, :], in_=xr[:]),
                        reads=[bxr], writes=([bdst[j]] if bdst is not None else []), dma=True)
          sc.barrier()

    for li, l in enumerate(layers):
        emit_layer(li, l)
    sc.finish()
    with nc.Block() as block:
        sc.replay(block)


LAUNCH_PLAN = [((0, 1), True)]


def kernel(**inputs):
    x = np.ascontiguousarray(inputs["x"], dtype=np.float32)
    wnames = [k for k in inputs if k != "x"]
    wts = {k: np.ascontiguousarray(inputs[k], dtype=np.float32) for k in wnames}
    cur = [x[c] for c in range(N_CORES)]
    for layers, fn in LAUNCH_PLAN:
        nc = build_program(layers=layers, final_norm=fn)
        in_maps = []
        for c in range(N_CORES):
            m = {"x": cur[c]}
            m.update(wts)
            in_maps.append(m)
        res = run_bass_kernel_spmd(nc, in_maps, core_ids=list(range(N_CORES)))
        cur = [res.results[c]["out"] for c in range(N_CORES)]
    return np.stack(cur, axis=0)
```
